# Optimizing a Trainium2 kernel written in Bass

```python
import math
import jax, jax.numpy as jnp
from jax import lax
import numpy as np

D_MODEL = 1024
BATCH = 16
SEQ = 2048
DEPTH = 4

GRID_W = 64
ROPE_THETA = 10000.0
Q_BLOCK = 128
RMS_EPS = 1e-6
LN_EPS = 1e-5
HEAD_DIM = 64
GQA_Q_HEADS = 8
GQA_KV_HEADS = 2
GQA_GROUP = GQA_Q_HEADS // GQA_KV_HEADS
GQA_WIDTH = GQA_Q_HEADS * HEAD_DIM
HY_WIDTH = 256
HY_EMB = 33
HY_FILTER_HIDDEN = 64
HY_FAST_DECAY = 0.3
HY_SLOW_DECAY = 1.5
HY_TARGET = 1e-2
MLA_HEADS = 4
MLA_Q_RANK = 256
MLA_KV_RANK = 128
MLA_NOPE = 64
MLA_ROPE = 32
MLA_V = 64
MLA_WIDTH = MLA_HEADS * MLA_V
D_MIX = GQA_WIDTH + HY_WIDTH + MLA_WIDTH
IN_SIZES = (GQA_WIDTH, GQA_KV_HEADS * HEAD_DIM, GQA_KV_HEADS * HEAD_DIM,
            3 * HY_WIDTH, MLA_Q_RANK, MLA_KV_RANK, MLA_ROPE)
D_IN = sum(IN_SIZES)
IN_SPLITS = tuple(int(s) for s in np.cumsum(IN_SIZES)[:-1])
N_MEM = 256
MEM_HEADS = 4
MEM_HEAD_DIM = D_MODEL // MEM_HEADS
N_EXPERTS = 16
EC_FACTOR = 2
EXPERT_FF = 512
DN_ALPHA = (2 * DEPTH) ** 0.25
DN_BETA = (8 * DEPTH) ** -0.25

kernel_name = "hybrid_gqa_hyena_mla_ec_moe_encoder"


def rms_norm(x, g):
    xf = x.astype(jnp.float32)
    y = xf * lax.rsqrt(jnp.mean(xf * xf, axis=-1, keepdims=True) + RMS_EPS)
    return (y * g.astype(jnp.float32)).astype(x.dtype)


def layer_norm(x, g, b):
    xf = x.astype(jnp.float32)
    mu = jnp.mean(xf, axis=-1, keepdims=True)
    xc = xf - mu
    var = jnp.mean(xc * xc, axis=-1, keepdims=True)
    return (xc * lax.rsqrt(var + LN_EPS) * g.astype(jnp.float32) + b.astype(jnp.float32)).astype(x.dtype)


def grid_coords(n_tok):
    rows = n_tok // GRID_W
    row = jnp.repeat(jnp.arange(rows, dtype=jnp.int32), GRID_W)
    col = jnp.tile(jnp.arange(GRID_W, dtype=jnp.int32), rows)
    return row, col


def rope_1d(x, pos):
    r = x.shape[-1]
    inv = ROPE_THETA ** (-(jnp.arange(r // 2, dtype=jnp.float32) * 2.0 / r))
    ang = pos.astype(jnp.float32)[:, None] * inv[None, :]
    cos = jnp.cos(ang)[None, :, None, :]
    sin = jnp.sin(ang)[None, :, None, :]
    xf = x.astype(jnp.float32)
    x1, x2 = xf[..., : r // 2], xf[..., r // 2:]
    return jnp.concatenate([x1 * cos - x2 * sin, x2 * cos + x1 * sin], axis=-1).astype(x.dtype)


def rope_2d(x, row, col):
    half = x.shape[-1] // 2
    return jnp.concatenate([rope_1d(x[..., :half], row), rope_1d(x[..., half:], col)], axis=-1)


def blocked_attention(q, k, v, scale):
    b, lq, hk, g, dk = q.shape
    nb = lq // Q_BLOCK
    qb = q.reshape(b, nb, Q_BLOCK, hk, g, dk).transpose(1, 0, 2, 3, 4, 5)

    def one_block(qi):
        s = jnp.einsum('bqhgd,bshd->bhgqs', qi, k).astype(jnp.float32) * scale
        p = jax.nn.softmax(s, axis=-1).astype(v.dtype)
        return jnp.einsum('bhgqs,bshe->bqhge', p, v)

    o = lax.map(one_block, qb)
    return o.transpose(1, 0, 2, 3, 4, 5).reshape(b, lq, hk, g, v.shape[-1])


def short_conv3(u, w, b):
    up = jnp.pad(u, ((0, 0), (1, 1), (0, 0)))
    return up[:, :-2] * w[0] + up[:, 1:-1] * w[1] + up[:, 2:] * w[2] + b


def hyena_filters(n_tok, w1, b1, w2, b2, w3, b3, wout, freq, decay):
    f32 = jnp.float32
    t = jnp.linspace(0.0, 1.0, n_tok, dtype=f32)[:, None]
    bands = (HY_EMB - 1) // 2
    fb = jnp.linspace(1e-4, bands - 1, bands, dtype=f32)[None, :]
    w = 2.0 * math.pi * jnp.arange(n_tok, dtype=f32)[:, None] / n_tok
    feats = jnp.concatenate([t, jnp.cos(fb * w), -jnp.sin(fb * w)], axis=-1)
    fr = freq.astype(f32)
    h = jnp.sin(fr * (feats @ w1.astype(f32) + b1.astype(f32)))
    h = jnp.sin(fr * (h @ w2.astype(f32) + b2.astype(f32)))
    h = jnp.sin(fr * (h @ w3.astype(f32) + b3.astype(f32)))
    h = (h @ wout.astype(f32)).reshape(n_tok, 2, HY_WIDTH)
    return h * jnp.exp(-t[:, :, None] * jnp.abs(decay.astype(f32))[None])


def bidir_long_conv(z, h, bias):
    n_tok, c = z.shape[1], z.shape[2]
    hf, hb = h[:, 0], h[:, 1]
    filt_circ = jnp.concatenate([hf, jnp.zeros((1, c), jnp.float32), hb[1:][::-1]], axis=0)
    zf = z.astype(jnp.float32)
    zs = jnp.fft.rfft(zf, n=2 * n_tok, axis=1)
    fs = jnp.fft.rfft(filt_circ, axis=0)
    y = jnp.fft.irfft(zs * fs[None], n=2 * n_tok, axis=1)[:, :n_tok]
    return (y + zf * bias.astype(jnp.float32)).astype(z.dtype)


def hybrid_mix(x, row, col, w_in, gqa_qn, gqa_kn, hy_cw, hy_cb, hy_w1, hy_b1, hy_w2, hy_b2,
               hy_w3, hy_b3, hy_wout, hy_freq, hy_decay, hy_bias, mla_qn, mla_wuq, mla_kvn,
               mla_wukv, w_o):
    b, n_tok, _ = x.shape
    proj = x @ w_in
    gq, gk, gv, hy_u, c_q, c_kv, k_r = jnp.split(proj, IN_SPLITS, axis=-1)

    q = rope_2d(rms_norm(gq.reshape(b, n_tok, GQA_Q_HEADS, HEAD_DIM), gqa_qn), row, col)
    k = rope_2d(rms_norm(gk.reshape(b, n_tok, GQA_KV_HEADS, HEAD_DIM), gqa_kn), row, col)
    v = gv.reshape(b, n_tok, GQA_KV_HEADS, HEAD_DIM)
    q = q.reshape(b, n_tok, GQA_KV_HEADS, GQA_GROUP, HEAD_DIM)
    out_a = blocked_attention(q, k, v, HEAD_DIM ** -0.5).reshape(b, n_tok, GQA_WIDTH)

    u = short_conv3(hy_u, hy_cw, hy_cb)
    x0, x1, hv = jnp.split(u, 3, axis=-1)
    filt = hyena_filters(n_tok, hy_w1, hy_b1, hy_w2, hy_b2, hy_w3, hy_b3, hy_wout, hy_freq, hy_decay)
    out_b = x0 * bidir_long_conv(hv * x1, filt, hy_bias)

    qc = (rms_norm(c_q, mla_qn) @ mla_wuq).reshape(b, n_tok, MLA_HEADS, MLA_NOPE + MLA_ROPE)
    q_nope, q_rope = qc[..., :MLA_NOPE], rope_2d(qc[..., MLA_NOPE:], row, col)
    kv = (rms_norm(c_kv, mla_kvn) @ mla_wukv).reshape(b, n_tok, MLA_HEADS, MLA_NOPE + MLA_V)
    k_nope, v_c = kv[..., :MLA_NOPE], kv[..., MLA_NOPE:]
    k_rope = rope_2d(k_r[:, :, None, :], row, col)
    q_c = jnp.concatenate([q_nope, q_rope], axis=-1)[:, :, :, None, :]
    k_c = jnp.concatenate([k_nope, jnp.broadcast_to(k_rope, (b, n_tok, MLA_HEADS, MLA_ROPE))], axis=-1)
    out_c = blocked_attention(q_c, k_c, v_c, (MLA_NOPE + MLA_ROPE) ** -0.5).reshape(b, n_tok, MLA_WIDTH)

    return jnp.concatenate([out_a, out_b, out_c], axis=-1) @ w_o


def memory_cross_attention(x, mem, wq, wkv, wo):
    b, n_tok, _ = x.shape
    q = (x @ wq).reshape(b, n_tok, MEM_HEADS, 1, MEM_HEAD_DIM)
    kv = (mem @ wkv).reshape(b, mem.shape[1], 2, MEM_HEADS, MEM_HEAD_DIM)
    o = blocked_attention(q, kv[:, :, 0], kv[:, :, 1], MEM_HEAD_DIM ** -0.5)
    return o.reshape(b, n_tok, D_MODEL) @ wo


def expert_choice_moe(x, w_router, w_gate, w_up, w_down):
    b, n_tok, d = x.shape
    cap = EC_FACTOR * n_tok // N_EXPERTS
    aff = jax.nn.softmax((x @ w_router).astype(jnp.float32), axis=-1)
    gates, idx = lax.top_k(aff.transpose(0, 2, 1), cap)
    xe = jax.vmap(lambda xb, ib: xb[ib])(x, idx)
    hid = jax.nn.silu(jnp.einsum('becd,edf->becf', xe, w_gate)) * jnp.einsum('becd,edf->becf', xe, w_up)
    ye = jnp.einsum('becf,efd->becd', hid, w_down) * gates[..., None].astype(x.dtype)
    return jax.vmap(lambda ib, yb: jnp.zeros((n_tok, d), yb.dtype).at[ib.reshape(-1)].add(yb.reshape(-1, d)))(idx, ye)


def setup_inputs(seed: int = 0) -> dict:
    key = jax.random.key(seed)
    ks = iter(jax.random.split(key, 48))

    def nrm(shape, scale):
        return jax.random.normal(next(ks), shape, jnp.float32) * scale

    def gain(shape):
        return 1.0 + nrm(shape, 0.02)

    deltas = jnp.abs(jnp.linspace(math.log(HY_TARGET) / HY_SLOW_DECAY,
                                  math.log(HY_TARGET) / HY_FAST_DECAY, HY_WIDTH, dtype=jnp.float32))
    return {
        'x': nrm((BATCH, SEQ, D_MODEL), 1.0),
        'mem': nrm((BATCH, N_MEM, D_MODEL), 1.0),
        'w_in': nrm((DEPTH, D_MODEL, D_IN), D_MODEL ** -0.5),
        'gqa_q_norm': gain((DEPTH, HEAD_DIM)),
        'gqa_k_norm': gain((DEPTH, HEAD_DIM)),
        'hy_conv_w': nrm((DEPTH, 3, 3 * HY_WIDTH), 3 ** -0.5),
        'hy_conv_b': nrm((DEPTH, 3 * HY_WIDTH), 0.02),
        'hy_w1': nrm((DEPTH, HY_EMB, HY_FILTER_HIDDEN), HY_EMB ** -0.5),
        'hy_b1': nrm((DEPTH, HY_FILTER_HIDDEN), 0.02),
        'hy_w2': nrm((DEPTH, HY_FILTER_HIDDEN, HY_FILTER_HIDDEN), HY_FILTER_HIDDEN ** -0.5),
        'hy_b2': nrm((DEPTH, HY_FILTER_HIDDEN), 0.02),
        'hy_w3': nrm((DEPTH, HY_FILTER_HIDDEN, HY_FILTER_HIDDEN), HY_FILTER_HIDDEN ** -0.5),
        'hy_b3': nrm((DEPTH, HY_FILTER_HIDDEN), 0.02),
        'hy_wout': nrm((DEPTH, HY_FILTER_HIDDEN, 2 * HY_WIDTH), 0.1 * HY_FILTER_HIDDEN ** -0.5),
        'hy_freq': gain((DEPTH, HY_FILTER_HIDDEN)),
        'hy_decay': deltas[None, None, :] * (1.0 + nrm((DEPTH, 2, HY_WIDTH), 0.05)),
        'hy_bias': nrm((DEPTH, HY_WIDTH), 0.5),
        'mla_q_norm': gain((DEPTH, MLA_Q_RANK)),
        'mla_w_uq': nrm((DEPTH, MLA_Q_RANK, MLA_HEADS * (MLA_NOPE + MLA_ROPE)), MLA_Q_RANK ** -0.5),
        'mla_kv_norm': gain((DEPTH, MLA_KV_RANK)),
        'mla_w_ukv': nrm((DEPTH, MLA_KV_RANK, MLA_HEADS * (MLA_NOPE + MLA_V)), MLA_KV_RANK ** -0.5),
        'w_o': nrm((DEPTH, D_MIX, D_MODEL), DN_BETA * D_MIX ** -0.5),
        'ln1_g': gain((DEPTH, D_MODEL)),
        'ln1_b': nrm((DEPTH, D_MODEL), 0.02),
        'xa_wq': nrm((DEPTH, D_MODEL, D_MODEL), D_MODEL ** -0.5),
        'xa_wkv': nrm((DEPTH, D_MODEL, 2 * D_MODEL), D_MODEL ** -0.5),
        'xa_wo': nrm((DEPTH, D_MODEL, D_MODEL), DN_BETA * D_MODEL ** -0.5),
        'ln2_g': gain((DEPTH, D_MODEL)),
        'ln2_b': nrm((DEPTH, D_MODEL), 0.02),
        'moe_router': nrm((DEPTH, D_MODEL, N_EXPERTS), D_MODEL ** -0.5),
        'moe_w_gate': nrm((DEPTH, N_EXPERTS, D_MODEL, EXPERT_FF), D_MODEL ** -0.5),
        'moe_w_up': nrm((DEPTH, N_EXPERTS, D_MODEL, EXPERT_FF), D_MODEL ** -0.5),
        'moe_w_down': nrm((DEPTH, N_EXPERTS, EXPERT_FF, D_MODEL), DN_BETA * EXPERT_FF ** -0.5),
        'ln3_g': gain((DEPTH, D_MODEL)),
        'ln3_b': nrm((DEPTH, D_MODEL), 0.02),
    }


def reference(x, mem, w_in, gqa_q_norm, gqa_k_norm, hy_conv_w, hy_conv_b, hy_w1, hy_b1, hy_w2,
              hy_b2, hy_w3, hy_b3, hy_wout, hy_freq, hy_decay, hy_bias, mla_q_norm, mla_w_uq,
              mla_kv_norm, mla_w_ukv, w_o, ln1_g, ln1_b, xa_wq, xa_wkv, xa_wo, ln2_g, ln2_b,
              moe_router, moe_w_gate, moe_w_up, moe_w_down, ln3_g, ln3_b):
    row, col = grid_coords(x.shape[1])
    for l in range(DEPTH):
        y = hybrid_mix(x, row, col, w_in[l], gqa_q_norm[l], gqa_k_norm[l], hy_conv_w[l], hy_conv_b[l],
                       hy_w1[l], hy_b1[l], hy_w2[l], hy_b2[l], hy_w3[l], hy_b3[l], hy_wout[l],
                       hy_freq[l], hy_decay[l], hy_bias[l], mla_q_norm[l], mla_w_uq[l],
                       mla_kv_norm[l], mla_w_ukv[l], w_o[l])
        x = layer_norm(DN_ALPHA * x + y, ln1_g[l], ln1_b[l])
        y = memory_cross_attention(x, mem, xa_wq[l], xa_wkv[l], xa_wo[l])
        x = layer_norm(DN_ALPHA * x + y, ln2_g[l], ln2_b[l])
        y = expert_choice_moe(x, moe_router[l], moe_w_gate[l], moe_w_up[l], moe_w_down[l])
        x = layer_norm(DN_ALPHA * x + y, ln3_g[l], ln3_b[l])
    return x
```

```python
import contextlib
import os
CUT = int(os.environ.get('A_CUT', '99'))
SUB = int(os.environ.get('A_SUB', '99'))
HY = int(os.environ.get('A_HY', '1'))
import math
import numpy as np
import ml_dtypes
import concourse.bass as bass
import concourse.mybir as mybir
from concourse.bass_utils import run_bass_kernel_spmd

F32 = mybir.dt.float32
BF16 = mybir.dt.bfloat16
I32 = mybir.dt.int32
U32 = mybir.dt.uint32
ALU = mybir.AluOpType
AF = mybir.ActivationFunctionType
AX = mybir.AxisListType

NCORES = 8
L = 2048
T = 4096
D = 1024
NB = 32
DEPTH = 4
ALPHA = float((2 * DEPTH) ** 0.25)
RMS_EPS = 1e-6
LN_EPS = 1e-5
TWO_PI = 2.0 * math.pi
MAGIC = 12582912.0


class Tok:
    __slots__ = ("w", "r")

    def __init__(self):
        self.w = None
        self.r = {}


class Prog:
    ENG = ["tensor", "vector", "scalar", "gpsimd", "sync"]
    NDMA = 8

    def __init__(self, nc):
        self.nc = nc
        self.ops = {e: [] for e in self.ENG}
        self.cnt = {e: 0 for e in self.ENG}
        self.seen = {e: {} for e in self.ENG}
        self.dma_n = {e: 0 for e in self.ENG}
        self.last = {}

    def op(self, eng, fn, r=(), w=(), dma=False):
        deps = {}

        def add(key, val):
            if deps.get(key, 0) < val:
                deps[key] = val

        for t in r:
            if t.w is not None:
                add(*t.w)
        for t in w:
            if t.w is not None:
                add(*t.w)
            for k, v in t.r.items():
                add(k, v)
        if dma:
            j = self.dma_n[eng]
            self.dma_n[eng] += 1
            slot = j % self.NDMA
            k = j // self.NDMA + 1
            me = ((eng, "dma", slot), 16 * k)
            if k > 1:
                add((eng, "dma", slot), 16 * (k - 1))
        else:
            self.cnt[eng] += 1
            me = ((eng, "c"), self.cnt[eng])
        seen = self.seen[eng]
        waits = []
        for key, val in deps.items():
            if seen.get(key, 0) >= val:
                continue
            seen[key] = val
            waits.append((key, val))
        self.ops[eng].append((fn, waits, me))
        self.last[me[0]] = me[1]
        for t in r:
            if t.r.get(me[0], 0) < me[1]:
                t.r[me[0]] = me[1]
        for t in w:
            t.w = me
            t.r = {}
        return me

    def barrier(self):
        snap = dict(self.last)
        for e in self.ENG:
            seen = self.seen[e]
            waits = []
            for key, val in snap.items():
                if seen.get(key, 0) >= val:
                    continue
                seen[key] = val
                waits.append((key, val))
            if waits:
                self.ops[e].append((None, waits, None))

    def emit(self, stack):
        nc = self.nc
        self.barrier()
        sems = {}
        for e in self.ENG:
            for fn, waits, me in self.ops[e]:
                for key, _ in waits:
                    if key not in sems:
                        sems[key] = None
                if me is not None and me[0] not in sems:
                    sems[me[0]] = None
        for key in sems:
            sems[key] = stack.enter_context(nc.semaphore("s_" + "_".join(str(x) for x in key)))
        block = stack.enter_context(nc.Block())

        def run(e):
            def body(eng):
                for fn, waits, me in self.ops[e]:
                    for key, val in waits:
                        eng.wait_ge(sems[key], val)
                    if fn is not None:
                        ins = fn(eng)
                        ins.then_inc(sems[me[0]], 16 if me[0][1] == "dma" else 1)
            return body

        block.tensor(run("tensor"))
        block.vector(run("vector"))
        block.scalar(run("scalar"))
        block.gpsimd(run("gpsimd"))
        block.sync(run("sync"))


def _esz(dt):
    return 2 if dt == BF16 else 4


class Arena:
    BASE = 17408
    LIMIT = 229376

    def __init__(self, nc):
        self.nc = nc
        self.off = self.BASE
        self.mark = self.BASE
        self.n = 0

    def reset(self):
        self.off = self.mark

    def tile(self, shape, dt, name="t"):
        sz = int(np.prod(shape[1:])) * _esz(dt)
        self.n += 1
        t = self.nc.alloc_sbuf_tensor_at(f"{name}_{self.n}", list(shape), dt, offset=self.off)
        self.off += (sz + 63) // 64 * 64
        assert self.off <= self.LIMIT, (name, self.off)
        return t


def _host_consts():
    c = {}
    c["ident"] = np.eye(128, dtype=np.float32)
    t = np.arange(L)
    row = (t // 64).astype(np.float64)
    col = (t % 64).astype(np.float64)

    def tab(n):
        inv = 10000.0 ** (-(np.arange(n, dtype=np.float64) * 2.0 / (2 * n)))
        ang = np.stack([row[:, None] * inv[None], col[:, None] * inv[None]], axis=1)
        cs = np.cos(ang).astype(np.float32).reshape(16, 128, 2 * n).transpose(1, 0, 2)
        sn = np.sin(ang).astype(np.float32).reshape(16, 128, 2 * n).transpose(1, 0, 2)
        return np.ascontiguousarray(cs), np.ascontiguousarray(sn)

    c["cq"], c["sq"] = tab(16)
    c["cm"], c["sm"] = tab(8)
    tt = np.linspace(0.0, 1.0, L, dtype=np.float32)[:, None]
    bands = 16
    fb = np.linspace(1e-4, bands - 1, bands, dtype=np.float32)[None, :]
    w = (2.0 * math.pi * np.arange(L, dtype=np.float32)[:, None] / L).astype(np.float32)
    feats = np.concatenate([tt, np.cos(fb * w), -np.sin(fb * w)], axis=-1).astype(np.float32)
    c["featsT"] = np.ascontiguousarray(feats.T)
    c["negt"] = np.ascontiguousarray((-tt[:, 0]).reshape(16, 128).T).astype(np.float32)
    m0 = np.ones((128, 16), np.float32)
    m0[0, 0] = 0.0
    c["m0"] = m0
    k = np.arange(L, dtype=np.int64)
    n = np.arange(L, dtype=np.int64)
    ph = ((2 * k[None, :] + 1) * n[:, None]) % 8192
    ang = ph.astype(np.float64) * (math.pi / 4096.0)
    C = np.cos(ang)
    S = np.sin(ang)
    def fwd(M):
        return np.ascontiguousarray(M.reshape(16, 128, 16, 128).transpose(2, 1, 0, 3)).astype(ml_dtypes.bfloat16)
    c["cf"] = fwd(C)
    c["sf"] = fwd(S)
    def inv(M):
        return np.ascontiguousarray((M.T / 2048.0).reshape(16, 128, L).transpose(1, 0, 2)).astype(ml_dtypes.bfloat16)
    c["ci"] = inv(C)
    c["si"] = inv(S)
    o48 = np.zeros((48, 1), np.float32)
    o48[32:] = 2048.0
    c["o48"] = o48
    return c


_CONST_SHAPES = {
    "ident": ([128, 128], F32), "cq": ([128, 16, 32], F32), "sq": ([128, 16, 32], F32),
    "cm": ([128, 16, 16], F32), "sm": ([128, 16, 16], F32), "featsT": ([33, L], F32),
    "negt": ([128, 16], F32), "m0": ([128, 16], F32),
    "cf": ([16, 128, 16, 128], BF16), "sf": ([16, 128, 16, 128], BF16),
    "ci": ([128, 16, L], BF16), "si": ([128, 16, L], BF16), "o48": ([48, 1], F32),
}

_W_SHAPES = {
    "w_in": [4, 1024, 1952], "gqa_q_norm": [4, 64], "gqa_k_norm": [4, 64], "hy_conv_w": [4, 3, 768],
    "hy_conv_b": [4, 768], "hy_w1": [4, 33, 64], "hy_b1": [4, 64], "hy_w2": [4, 64, 64], "hy_b2": [4, 64],
    "hy_w3": [4, 64, 64], "hy_b3": [4, 64], "hy_wout": [4, 64, 512], "hy_freq": [4, 64],
    "hy_decay": [4, 2, 256], "hy_bias": [4, 256], "mla_q_norm": [4, 256], "mla_w_uq": [4, 256, 384],
    "mla_kv_norm": [4, 128], "mla_w_ukv": [4, 128, 512], "w_o": [4, 1024, 1024], "ln1_g": [4, 1024],
    "ln1_b": [4, 1024], "xa_wq": [4, 1024, 1024], "xa_wkv": [4, 1024, 2048], "xa_wo": [4, 1024, 1024],
    "ln2_g": [4, 1024], "ln2_b": [4, 1024], "moe_router": [4, 1024, 16], "moe_w_gate": [4, 16, 1024, 512],
    "moe_w_up": [4, 16, 1024, 512], "moe_w_down": [4, 16, 512, 1024], "ln3_g": [4, 1024], "ln3_b": [4, 1024],
}


def build(NL=DEPTH, dbg=(), stop_after=None):
    nc = bass.Bass("TRN2", target_bir_lowering=False)

    def din(name, shape, dt=F32):
        return nc.dram_tensor(name, list(shape), dt, kind="ExternalInput").ap()

    x_in = din("x", [T, D])
    mem_in = din("mem", [512, D])
    Wt = {k: din(k, s) for k, s in _W_SHAPES.items()}
    Cn = {k: din(k, s, dt) for k, (s, dt) in _CONST_SHAPES.items()}
    out = nc.dram_tensor("out", [T, D], F32, kind="ExternalOutput").ap()

    def dscr(name, shape, dt):
        kind = "ExternalOutput" if name in dbg else "Internal"
        return nc.dram_tensor(name, list(shape), dt, kind=kind).ap()

    X32 = dscr("X32", [T, D], F32)
    Z32 = dscr("Z32", [T, D], F32)
    XT16 = dscr("XT16", [D, T], BF16)
    QTd = dscr("QTd", [6, 128, T], BF16)
    QCTd = dscr("QCTd", [4, 96, T], BF16)
    KCTd = dscr("KCTd", [4, 96, T], BF16)
    UT = dscr("UT", [768, T], F32)
    AT = dscr("AT", [D, T], BF16)
    XT16v = XT16.rearrange("(c p) t -> p c t", p=128)
    ATv = AT.rearrange("(c p) t -> p c t", p=128)
    t_X32 = [Tok() for _ in range(NB)]
    t_Z32 = [Tok() for _ in range(NB)]
    t_Zs = [Tok(), Tok()]
    t_XT = [Tok() for _ in range(8)]
    t_QT = [Tok() for _ in range(8)]
    t_QCT = [Tok() for _ in range(8)]
    t_KCT = [Tok() for _ in range(8)]
    t_UT = [Tok() for _ in range(8)]
    t_AT = {}

    def tAT(key):
        if key not in t_AT:
            t_AT[key] = Tok()
        return t_AT[key]

    P = Prog(nc)
    A = Arena(nc)
    _bc = {}

    def bcr(eng):
        if "r" not in _bc:
            _bc["r"] = eng.to_reg(T - 1)
        return _bc["r"]
    st = contextlib.ExitStack()
    ps = st.enter_context(nc.psum_tensor("ps", [128, 4096], F32))
    PT = [Tok() for _ in range(8)]

    def bank(b, lo=0, hi=512, p0=0, p1=128):
        return ps[p0:p1, b * 512 + lo:b * 512 + hi]

    def dma(q, out_, in_, r=(), w=()):
        P.op(q, lambda e: e.dma_start(out=out_, in_=in_), r=r, w=w, dma=True)

    def mm(out_, lhsT, rhs, start, stop, r=(), w=()):
        P.op("tensor", lambda e: e.matmul(out_, lhsT=lhsT, rhs=rhs, start=start, stop=stop), r=r, w=w)

    def tr(out_, in_, r=(), w=()):
        pp = in_.shape[0]
        P.op("tensor", lambda e: e.transpose(out=out_, in_=in_, identity=ident[0:pp, 0:pp]), r=list(r) + [t_const], w=w)

    def act(out_, in_, func, r=(), w=(), **kw):
        P.op("scalar", lambda e: e.activation(out=out_, in_=in_, func=func, **kw), r=r, w=w)

    def vop(name, r=(), w=(), eng="vector", **kw):
        P.op(eng, lambda e: getattr(e, name)(**kw), r=r, w=w)

    def tt(out_, in0, in1, op, r=(), w=(), eng="vector"):
        P.op(eng, lambda e: e.tensor_tensor(out=out_, in0=in0, in1=in1, op=op), r=r, w=w)

    def ts(out_, in0, s1, s2, op0, op1=None, r=(), w=(), eng="vector"):
        if op1 is None:
            P.op(eng, lambda e: e.tensor_scalar(out=out_, in0=in0, scalar1=s1, scalar2=None, op0=op0), r=r, w=w)
        else:
            P.op(eng, lambda e: e.tensor_scalar(out=out_, in0=in0, scalar1=s1, scalar2=s2, op0=op0, op1=op1), r=r, w=w)

    def stt(out_, in0, scalar, in1, op0, op1, r=(), w=(), eng="vector"):
        P.op(eng, lambda e: e.scalar_tensor_tensor(out=out_, in0=in0, scalar=scalar, in1=in1, op0=op0, op1=op1), r=r, w=w)

    def cp(out_, in_, r=(), w=(), eng="vector"):
        if eng == "scalar":
            P.op("scalar", lambda e: e.copy(out=out_, in_=in_), r=r, w=w)
        else:
            P.op(eng, lambda e: e.tensor_copy(out=out_, in_=in_), r=r, w=w)

    t_const = Tok()
    ident = A.tile([128, 128], F32, "ident")
    dma("sync", ident[:], Cn["ident"], w=[t_const])
    ones_bf = A.tile([128, 128], BF16, "ones")
    P.op("vector", lambda e: e.memset(ones_bf[:], 1.0), w=[t_const])
    ones_f = A.tile([128, 64], F32, "ones_f")
    P.op("vector", lambda e: e.memset(ones_f[:], 1.0), w=[t_const])
    CQ = A.tile([128, 16, 32], F32, "CQ"); SQ = A.tile([128, 16, 32], F32, "SQ")
    CM = A.tile([128, 16, 16], F32, "CM"); SM = A.tile([128, 16, 16], F32, "SM")
    for tl, nm in ((CQ, "cq"), (SQ, "sq"), (CM, "cm"), (SM, "sm")):
        dma("sync", tl[:], Cn[nm], w=[t_const])
    VA = A.tile([128, NB, 2, 65], BF16, "VA")
    VC = A.tile([128, NB, 4, 65], BF16, "VC")
    t_VA = [Tok() for _ in range(NB)]
    t_VC = [Tok() for _ in range(NB)]
    P.op("vector", lambda e: e.memset(VA[:], 1.0), w=t_VA)
    P.op("vector", lambda e: e.memset(VC[:], 1.0), w=t_VC)
    memT = A.tile([128, 8, 512], BF16, "memT")
    t_memT = Tok()
    A.mark = A.off

    def emit_xT(src, t_src, xst, t_xst, tbl, b0, b1):
        for c in range(8):
            b = b0 if c < 4 else b1
            tr(bank(b, (c % 4) * 128, (c % 4) * 128 + 128), src[:, c * 128:(c + 1) * 128], r=[t_src], w=[PT[b]])
        cp(xst[:, 0:4, tbl * 128:(tbl + 1) * 128], bank(b0).rearrange("p (c t) -> p c t", c=4),
           r=[PT[b0]], w=[t_xst], eng="scalar")
        cp(xst[:, 4:8, tbl * 128:(tbl + 1) * 128], bank(b1).rearrange("p (c t) -> p c t", c=4),
           r=[PT[b1]], w=[t_xst], eng="vector")

    def emit_ln(zt, t_z, G, Bt, t_gb, sm, t_sm):
        junk = sm["junk"]
        act(junk[:], zt[:], AF.Copy, r=[t_z], w=[sm["tj"], t_sm], scale=1.0 / D, accum_out=sm["s"][:, 2:3])
        act(junk[:], zt[:], AF.Square, r=[t_z], w=[sm["tj"], t_sm], scale=float(D ** -0.5), accum_out=sm["s"][:, 1:2])
        stt(sm["s"][:, 4:5], sm["s"][:, 2:3], sm["s"][:, 2:3], sm["s"][:, 1:2], ALU.mult, ALU.subtract, r=[t_sm], w=[t_sm])
        act(sm["s"][:, 5:6], sm["s"][:, 4:5], AF.Sqrt, r=[t_sm], w=[t_sm], bias=LN_EPS, scale=-1.0)
        vop("reciprocal", out=sm["s"][:, 5:6], in_=sm["s"][:, 5:6], r=[t_sm], w=[t_sm])
        stt(sm["s"][:, 6:7], sm["s"][:, 2:3], -1.0, sm["s"][:, 5:6], ALU.mult, ALU.mult, r=[t_sm], w=[t_sm])
        act(zt[:], zt[:], AF.Identity, r=[t_z, t_sm], w=[t_z], scale=sm["s"][:, 5:6], bias=sm["s"][:, 6:7])
        tt(zt[:], zt[:], G[:], ALU.mult, r=[t_z, t_gb], w=[t_z], eng="vector")
        tt(zt[:], zt[:], Bt[:], ALU.add, r=[t_z, t_gb], w=[t_z], eng="gpsimd")

    def load_gb(gname, bname, l):
        G = A.tile([128, D], F32, "G"); Bt = A.tile([128, D], F32, "B")
        t_gb = Tok()
        dma("sync", G[:], Wt[gname][l].partition_broadcast(128), w=[t_gb])
        dma("sync", Bt[:], Wt[bname][l].partition_broadcast(128), w=[t_gb])
        return G, Bt, t_gb

    def ln_scratch():
        return {"s": A.tile([128, 8], F32, "lns"), "junk": A.tile([128, D], BF16, "junk"), "tj": Tok()}, Tok()

    class Skew:
        def __init__(self, sk):
            self.sk = sk
            self.q = []

        def push(self, front, back):
            front()
            self.q.append(back)
            while len(self.q) > self.sk:
                self.q.pop(0)()

        def flush(self):
            while self.q:
                self.q.pop(0)()

    def emit_rope(xg, t_x, H, n, ct, st_, t1, t2, ro, t_t1, t_t2, t_ro):
        def v5(tl):
            return tl[:].rearrange("p (h f j i) -> p h f j i", h=H, f=2, j=2)
        for f in range(2):
            cb = ct[:, f, :].unsqueeze(1).unsqueeze(1).to_broadcast([128, H, 2, n])
            sb = st_[:, f, :].unsqueeze(1).to_broadcast([128, H, n])
            tt(v5(t1)[:, :, f], v5(xg)[:, :, f], cb, ALU.mult, r=[t_x, t_const], w=[t_t1], eng="vector")
            tt(v5(t2)[:, :, f, 0, :], v5(xg)[:, :, f, 1, :], sb, ALU.mult, r=[t_x, t_const], w=[t_t2], eng="gpsimd")
            tt(v5(t2)[:, :, f, 1, :], v5(xg)[:, :, f, 0, :], sb, ALU.mult, r=[t_x, t_const], w=[t_t2], eng="gpsimd")

        def v4(tl):
            return tl[:].rearrange("p (hf j i) -> p hf j i", j=2, i=n)
        tt(v4(ro)[:, :, 0, :], v4(t1)[:, :, 0, :], v4(t2)[:, :, 0, :], ALU.subtract, r=[t_t1, t_t2], w=[t_ro], eng="vector")
        tt(v4(ro)[:, :, 1, :], v4(t1)[:, :, 1, :], v4(t2)[:, :, 1, :], ALU.add, r=[t_t1, t_t2], w=[t_ro], eng="gpsimd")

    def stage_init():
        A.reset()
        xb = [A.tile([128, D], F32, "xb") for _ in range(2)]
        t_xb = [Tok(), Tok()]
        xst = [A.tile([128, 8, 512], BF16, "xst") for _ in range(2)]
        t_xst = [Tok(), Tok()]
        for i in range(4):
            k = i % 2
            dma("sync", xb[k][:], mem_in[i * 128:(i + 1) * 128, :], w=[t_xb[k]])
            for c in range(8):
                b = 0 if c < 4 else 1
                tr(bank(b, (c % 4) * 128, (c % 4) * 128 + 128), xb[k][:, c * 128:(c + 1) * 128], r=[t_xb[k]], w=[PT[b]])
            cp(memT[:, 0:4, i * 128:(i + 1) * 128], bank(0).rearrange("p (c t) -> p c t", c=4), r=[PT[0]], w=[t_memT], eng="scalar")
            cp(memT[:, 4:8, i * 128:(i + 1) * 128], bank(1).rearrange("p (c t) -> p c t", c=4), r=[PT[1]], w=[t_memT], eng="vector")
        for tb in range(NB):
            k = tb % 2
            ck, tbl = tb // 4, tb % 4
            dma("sync", xb[k][:], x_in[tb * 128:(tb + 1) * 128, :], w=[t_xb[k]])
            emit_xT(xb[k], t_xb[k], xst[ck % 2], t_xst[ck % 2], tbl, 2 + 2 * k, 3 + 2 * k)
            if tbl == 3:
                dma("sync", XT16v[:, :, ck * 512:(ck + 1) * 512], xst[ck % 2][:], r=[t_xst[ck % 2]], w=[t_XT[ck]])

    def stage_A(l):
        A.reset()
        win = A.tile([128, 8, 1952], BF16, "win")
        t_win = [Tok() for _ in range(8)]
        for c in range(8):
            dma("gpsimd", win[:, c, :], Wt["w_in"][l, c * 128:(c + 1) * 128, :], w=[t_win[c]])
        t_par = Tok()
        G10 = A.tile([128, 640], F32, "G10")
        for h in range(10):
            src = Wt["gqa_q_norm"][l] if h < 8 else Wt["gqa_k_norm"][l]
            dma("sync", G10[:, h * 64:(h + 1) * 64], src.partition_broadcast(128), w=[t_par])
        Gm = A.tile([128, 384], F32, "Gm")
        dma("sync", Gm[:, 0:256], Wt["mla_q_norm"][l].partition_broadcast(128), w=[t_par])
        dma("sync", Gm[:, 256:384], Wt["mla_kv_norm"][l].partition_broadcast(128), w=[t_par])
        wuq = A.tile([128, 2, 384], BF16, "wuq")
        wukv = A.tile([128, 512], BF16, "wukv")
        dma("gpsimd", wuq[:], Wt["mla_w_uq"][l].rearrange("(c p) n -> p c n", p=128), w=[t_par])
        dma("gpsimd", wukv[:], Wt["mla_w_ukv"][l], w=[t_par])

        xt = [A.tile([128, 8, 512], BF16, "xt") for _ in range(2)]
        t_xt = [Tok(), Tok()]
        uts = [A.tile([128, 512], F32, "uts") for _ in range(2)]
        t_uts = [Tok(), Tok()]
        QS = [A.tile([128, 6, 512], BF16, "QS") for _ in range(2)]
        t_QS = [Tok(), Tok()]
        QCS = [A.tile([96, 4, 512], BF16, "QCS") for _ in range(2)]
        t_QCS = [Tok(), Tok()]
        KCS = [A.tile([96, 4, 512], BF16, "KCS") for _ in range(2)]
        t_KCS = [Tok(), Tok()]
        sqt = A.tile([128, 640], F32, "sqt"); t_sqt = Tok()
        ss = A.tile([128, 16], F32, "ss"); t_ss = Tok()
        xg = A.tile([128, 640], F32, "xg"); t_xg = Tok()
        r1 = A.tile([128, 640], F32, "r1"); t_r1 = Tok()
        r2 = A.tile([128, 640], F32, "r2"); t_r2 = Tok()
        ro = A.tile([128, 640], F32, "ro"); t_ro = Tok()
        KK = A.tile([128, 256], F32, "KK"); t_KK = Tok()
        CN = A.tile([128, 384], F32, "CN"); t_CN = Tok()
        cnT = A.tile([128, 3, 128], BF16, "cnT"); t_cnT = Tok()
        RM = A.tile([128, 160], F32, "RM"); t_RM = Tok()
        m1 = A.tile([128, 160], F32, "m1"); t_m1 = Tok()
        m2 = A.tile([128, 160], F32, "m2"); t_m2 = Tok()
        mo = A.tile([128, 160], F32, "mo"); t_mo = Tok()
        QC = A.tile([128, 4, 96], F32, "QC"); t_QC = Tok()
        KC = A.tile([128, 4, 96], F32, "KC"); t_KC = Tok()

        ui = 0
        for ck in range(8):
            k2 = ck % 2
            dma("sync", xt[k2][:], XT16v[:, :, ck * 512:(ck + 1) * 512], r=[t_XT[ck]], w=[t_xt[k2]])
            for cc in range(6):
                for d in range(8):
                    mm(bank(3), win[:, d, 768 + cc * 128:768 + (cc + 1) * 128], xt[k2][:, d, :], d == 0, d == 7,
                       r=[t_win[d], t_xt[k2]], w=[PT[3]])
                u = ui % 2
                ui += 1
                cp(uts[u][:], bank(3), r=[PT[3]], w=[t_uts[u]], eng="scalar")
                dma("sync", UT[cc * 128:(cc + 1) * 128, ck * 512:(ck + 1) * 512], uts[u][:], r=[t_uts[u]], w=[t_UT[ck]])
            for tbl in range(4):
                tb = ck * 4 + tbl
                tb16 = tb % 16
                tc0, tc1 = tbl * 128, (tbl + 1) * 128
                for d in range(8):
                    mm(bank(0), xt[k2][:, d, tc0:tc1], win[:, d, 0:512], d == 0, d == 7, r=[t_win[d], t_xt[k2]], w=[PT[0]])
                for d in range(8):
                    mm(bank(1, 0, 256), xt[k2][:, d, tc0:tc1], win[:, d, 512:768], d == 0, d == 7, r=[t_win[d], t_xt[k2]], w=[PT[1]])
                for d in range(8):
                    mm(bank(2, 0, 416), xt[k2][:, d, tc0:tc1], win[:, d, 1536:1952], d == 0, d == 7, r=[t_win[d], t_xt[k2]], w=[PT[2]])
                if CUT <= 1:
                    continue
                qk = ps[:, 0:640]
                act(sqt[:], qk, AF.Square, r=[PT[0], PT[1]], w=[t_sqt])
                vop("tensor_reduce", out=ss[:, 0:10], in_=sqt[:].rearrange("p (h d) -> p h d", d=64), axis=AX.X, op=ALU.add,
                    r=[t_sqt], w=[t_ss])
                act(ss[:, 0:10], ss[:, 0:10], AF.Sqrt, r=[t_ss], w=[t_ss], scale=1.0 / 64, bias=RMS_EPS)
                vop("reciprocal", out=ss[:, 0:10], in_=ss[:, 0:10], r=[t_ss], w=[t_ss])
                tt(xg[:].rearrange("p (h d) -> p h d", d=64), qk.rearrange("p (h d) -> p h d", d=64),
                   ss[:, 0:10].unsqueeze(2).to_broadcast([128, 10, 64]), ALU.mult, r=[PT[0], PT[1], t_ss], w=[t_xg])
                tt(xg[:], xg[:], G10[:], ALU.mult, r=[t_xg, t_par], w=[t_xg], eng="gpsimd")
                if CUT <= 2:
                    continue
                emit_rope(xg, t_xg, 10, 16, CQ[:, tb16, :].rearrange("p (f i) -> p f i", f=2),
                          SQ[:, tb16, :].rearrange("p (f i) -> p f i", f=2), r1, r2, ro, t_r1, t_r2, t_ro)
                if CUT <= 3:
                    continue
                for rr in range(2):
                    cp(KK[:].rearrange("p (h r d) -> p h r d", h=2, r=2)[:, :, rr, :],
                       ro[:, 512:640].rearrange("p (h d) -> p h d", h=2), r=[t_ro], w=[t_KK], eng="gpsimd")
                for j in range(6):
                    src = ro[:, j * 128:(j + 1) * 128] if j < 4 else KK[:, (j - 4) * 128:(j - 3) * 128]
                    b = 5 if j < 3 else 6
                    tr(bank(b, (j % 3) * 128, (j % 3) * 128 + 128), src, r=[t_ro, t_KK], w=[PT[b]])
                cp(QS[k2][:, 0:3, tc0:tc1], bank(5, 0, 384).rearrange("p (c t) -> p c t", c=3), r=[PT[5]], w=[t_QS[k2]], eng="scalar")
                cp(QS[k2][:, 3:6, tc0:tc1], bank(6, 0, 384).rearrange("p (c t) -> p c t", c=3), r=[PT[6]], w=[t_QS[k2]], eng="vector")
                if CUT <= 4:
                    continue
                cp(VA[:, tb, :, 0:64], bank(1, 128, 256).rearrange("p (h d) -> p h d", h=2), r=[PT[1]], w=[t_VA[tb]], eng="scalar")
                act(sqt[:, 0:384], bank(2, 0, 384), AF.Square, r=[PT[2]], w=[t_sqt])
                vop("tensor_reduce", out=ss[:, 10:11], in_=sqt[:, 0:256], axis=AX.X, op=ALU.add, r=[t_sqt], w=[t_ss])
                vop("tensor_reduce", out=ss[:, 11:12], in_=sqt[:, 256:384], axis=AX.X, op=ALU.add, r=[t_sqt], w=[t_ss])
                act(ss[:, 10:11], ss[:, 10:11], AF.Sqrt, r=[t_ss], w=[t_ss], scale=1.0 / 256, bias=RMS_EPS)
                act(ss[:, 11:12], ss[:, 11:12], AF.Sqrt, r=[t_ss], w=[t_ss], scale=1.0 / 128, bias=RMS_EPS)
                vop("reciprocal", out=ss[:, 10:12], in_=ss[:, 10:12], r=[t_ss], w=[t_ss])
                ts(CN[:, 0:256], bank(2, 0, 256), ss[:, 10:11], None, ALU.mult, r=[PT[2], t_ss], w=[t_CN])
                ts(CN[:, 256:384], bank(2, 256, 384), ss[:, 11:12], None, ALU.mult, r=[PT[2], t_ss], w=[t_CN])
                tt(CN[:], CN[:], Gm[:], ALU.mult, r=[t_CN, t_par], w=[t_CN], eng="gpsimd")
                for j in range(3):
                    tr(bank(7, j * 128, (j + 1) * 128), CN[:, j * 128:(j + 1) * 128], r=[t_CN], w=[PT[7]])
                cp(cnT[:], bank(7, 0, 384).rearrange("p (c t) -> p c t", c=3), r=[PT[7]], w=[t_cnT], eng="scalar")
                cp(RM[:, 128:160], bank(2, 384, 416), r=[PT[2]], w=[t_RM], eng="vector")
                if CUT <= 5:
                    continue
                for kc in range(2):
                    mm(bank(0, 0, 384), cnT[:, kc, :], wuq[:, kc, :], kc == 0, kc == 1, r=[t_cnT, t_par], w=[PT[0]])
                mm(bank(4), cnT[:, 2, :], wukv[:], True, True, r=[t_cnT, t_par], w=[PT[4]])
                if SUB <= 1:
                    continue
                qcv = bank(0, 0, 384).rearrange("p (h e) -> p h e", h=4)
                kvv = bank(4).rearrange("p (h e) -> p h e", h=4)
                cp(RM[:, 0:128].rearrange("p (h e) -> p h e", h=4), qcv[:, :, 64:96], r=[PT[0]], w=[t_RM], eng="vector")
                cp(QC[:, :, 0:64], qcv[:, :, 0:64], r=[PT[0]], w=[t_QC], eng="vector")
                if SUB <= 2:
                    continue
                cp(KC[:, :, 0:64], kvv[:, :, 0:64], r=[PT[4]], w=[t_KC], eng="vector")
                cp(VC[:, tb, :, 0:64], kvv[:, :, 64:128], r=[PT[4]], w=[t_VC[tb]], eng="scalar")
                if CUT <= 6:
                    continue
                emit_rope(RM, t_RM, 5, 8, CM[:, tb16, :].rearrange("p (f i) -> p f i", f=2),
                          SM[:, tb16, :].rearrange("p (f i) -> p f i", f=2), m1, m2, mo, t_m1, t_m2, t_mo)
                cp(QC[:, :, 64:96], mo[:, 0:128].rearrange("p (h e) -> p h e", h=4), r=[t_mo], w=[t_QC], eng="gpsimd")
                cp(KC[:, :, 64:96], mo[:, 128:160].unsqueeze(1).to_broadcast([128, 4, 32]), r=[t_mo], w=[t_KC], eng="gpsimd")
                if CUT <= 7:
                    continue
                for h in range(4):
                    tr(bank(1, h * 128, (h + 1) * 128, 0, 96), QC[:, h, :], r=[t_QC], w=[PT[1]])
                for h in range(4):
                    tr(bank(2, h * 128, (h + 1) * 128, 0, 96), KC[:, h, :], r=[t_KC], w=[PT[2]])
                cp(QCS[k2][:, :, tc0:tc1], bank(1, 0, 512, 0, 96).rearrange("p (c t) -> p c t", c=4), r=[PT[1]], w=[t_QCS[k2]], eng="scalar")
                cp(KCS[k2][:, :, tc0:tc1], bank(2, 0, 512, 0, 96).rearrange("p (c t) -> p c t", c=4), r=[PT[2]], w=[t_KCS[k2]], eng="vector")
            c0, c1 = ck * 512, (ck + 1) * 512
            dma("sync", QTd[:, :, c0:c1].rearrange("j p t -> p j t"), QS[k2][:], r=[t_QS[k2]], w=[t_QT[ck]])
            dma("sync", QCTd[:, :, c0:c1].rearrange("j p t -> p j t"), QCS[k2][:], r=[t_QCS[k2]], w=[t_QCT[ck]])
            dma("sync", KCTd[:, :, c0:c1].rearrange("j p t -> p j t"), KCS[k2][:], r=[t_KCS[k2]], w=[t_KCT[ck]])

    def stage_ATT():
        A.reset()
        qt = [A.tile([128, L], BF16, "qt") for _ in range(2)]
        kt = [A.tile([128, L], BF16, "kt") for _ in range(2)]
        t_qk = [Tok(), Tok()]
        NE = 6
        et = [A.tile([128, 512], BF16, "et") for _ in range(NE)]
        t_et = [Tok() for _ in range(NE)]
        rec = [A.tile([128, 512], F32, "rec") for _ in range(2)]
        t_rec = [Tok(), Tok()]
        bcs = [A.tile([64, 512], F32, "bcs") for _ in range(2)]
        t_bcs = [Tok(), Tok()]
        ot = [A.tile([64, 512], BF16, "ot") for _ in range(2)]
        t_ot = [Tok(), Tok()]
        items = []
        cnt = {"l": 0, "g": 0}

        def add_head(seq, li, pr0, pr1, vtile, t_v, hv, scale, arow, pre):
            qtile, ktile, t_in = qt[li], kt[li], t_qk[li]
            for qb in range(4):
                g = cnt["g"]
                cnt["g"] += 1
                ob = 4 + g % 2
                gi = g % 2
                for kb in range(16):
                    idx = len(items)
                    sb = idx % 4
                    ei = idx % NE

                    def qk(sb=sb, ei=ei, kb=kb, qb=qb):
                        mm(bank(sb), ktile[pr0:pr1, kb * 128:(kb + 1) * 128], qtile[pr0:pr1, qb * 512:(qb + 1) * 512], True, True,
                           r=[t_in], w=[PT[sb]])
                        act(et[ei][:], bank(sb), AF.Exp, r=[PT[sb]], w=[t_et[ei]], scale=scale)

                    def pv(ei=ei, kb=kb, ob=ob):
                        mm(bank(ob, 0, 512, 0, 65), vtile[:, seq * 16 + kb, hv, :], et[ei][:], kb == 0, kb == 15,
                           r=[t_et[ei], t_v[seq * 16 + kb]], w=[PT[ob]])

                    n1 = n2 = None
                    if kb == 15:
                        def n1(ob=ob, gi=gi):
                            vop("reciprocal", out=rec[gi][64:65, :], in_=bank(ob, 0, 512, 64, 65), r=[PT[ob]], w=[t_rec[gi]])

                        def n2(ob=ob, gi=gi, qb=qb):
                            mm(bank(6 + gi, 0, 512, 0, 64), ones_f[64:65, 0:64], rec[gi][64:65, :], True, True,
                               r=[t_rec[gi], t_const], w=[PT[6 + gi]])
                            cp(bcs[gi][:], bank(6 + gi, 0, 512, 0, 64), r=[PT[6 + gi]], w=[t_bcs[gi]], eng="vector")
                            tt(ot[gi][:], bank(ob, 0, 512, 0, 64), bcs[gi][:], ALU.mult, r=[PT[ob], t_bcs[gi]], w=[t_ot[gi]])
                            c0 = seq * L + qb * 512
                            dma("sync", AT[arow:arow + 64, c0:c0 + 512], ot[gi][:], r=[t_ot[gi]], w=[tAT((arow // 128, c0 // 512))])
                    items.append((pre if (qb == 0 and kb == 0) else None, qk, pv, n1, n2))

        for seq in range(2):
            rq = [t_QT[seq * 4 + i] for i in range(4)]
            for j in range(4):
                li = cnt["l"] % 2
                cnt["l"] += 1

                def pre(li=li, j=j, seq=seq, rq=rq):
                    dma("sync", qt[li][:], QTd[j, :, seq * L:(seq + 1) * L], r=rq, w=[t_qk[li]])
                    dma("sync", kt[li][:], QTd[4 + j // 2, :, seq * L:(seq + 1) * L], r=rq, w=[t_qk[li]])
                for hh in range(2):
                    add_head(seq, li, hh * 64, hh * 64 + 64, VA, t_VA, j // 2, 0.125, (2 * j + hh) * 64, pre if hh == 0 else None)
            rq2 = [t_QCT[seq * 4 + i] for i in range(4)] + [t_KCT[seq * 4 + i] for i in range(4)]
            for h in range(4):
                li = cnt["l"] % 2
                cnt["l"] += 1

                def pre(li=li, h=h, seq=seq, rq2=rq2):
                    dma("sync", qt[li][0:96, :], QCTd[h, :, seq * L:(seq + 1) * L], r=rq2, w=[t_qk[li]])
                    dma("sync", kt[li][0:96, :], KCTd[h, :, seq * L:(seq + 1) * L], r=rq2, w=[t_qk[li]])
                add_head(seq, li, 0, 96, VC, t_VC, h, float(96 ** -0.5), 768 + h * 64, pre)
        D1, D2 = 2, 5
        n = len(items)
        for i in range(n + D2 + 1):
            if i < n:
                if items[i][0] is not None:
                    items[i][0]()
                items[i][1]()
            if 0 <= i - D1 < n:
                items[i - D1][2]()
                if items[i - D1][3] is not None:
                    items[i - D1][3]()
            if 0 <= i - D2 < n and items[i - D2][4] is not None:
                items[i - D2][4]()

    def stage_WO(l, xsrc, have_hyena):
        A.reset()
        wo = A.tile([128, 8, D], BF16, "wo")
        t_wo = [Tok() for _ in range(8)]
        for c in range(8):
            dma("gpsimd", wo[:, c, :], Wt["w_o"][l, c * 128:(c + 1) * 128, :], w=[t_wo[c]])
        G, Bt, t_gb = load_gb("ln1_g", "ln1_b", l)
        at = [A.tile([128, 8, 512], BF16, "at") for _ in range(2)]
        t_at = [Tok(), Tok()]
        xst = [A.tile([128, 8, 512], BF16, "xst") for _ in range(2)]
        t_xst = [Tok(), Tok()]
        if not have_hyena:
            zt_ = A.tile([128, 512], BF16, "zero")
            t_z = Tok()
            P.op("vector", lambda e: e.memset(zt_[:], 0.0), w=[t_z])
            for c in (4, 5):
                for ck in range(8):
                    dma("sync", AT[c * 128:(c + 1) * 128, ck * 512:(ck + 1) * 512], zt_[:], r=[t_z], w=[tAT((c, ck))])
        NR = 4
        sms = [ln_scratch() for _ in range(NR)]
        xbs = [A.tile([128, D], F32, "xbr") for _ in range(NR)]
        t_xbs = [Tok() for _ in range(NR)]
        sk = Skew(2)
        for ck in range(8):
            k2 = ck % 2
            dma("sync", at[k2][:], ATv[:, :, ck * 512:(ck + 1) * 512], r=[tAT((c, ck)) for c in range(8)], w=[t_at[k2]])
            for tbl in range(4):
                tb = ck * 4 + tbl

                def front(tb=tb, tbl=tbl, k2=k2):
                    kb = tb % NR
                    yb_ = 2 * (tb % 2)
                    dma("sync", xbs[kb][:], xsrc[tb * 128:(tb + 1) * 128, :], r=[t_X32[tb]], w=[t_xbs[kb]])
                    for half in range(2):
                        for c in range(8):
                            mm(bank(yb_ + half), at[k2][:, c, tbl * 128:(tbl + 1) * 128], wo[:, c, half * 512:(half + 1) * 512],
                               c == 0, c == 7, r=[t_at[k2], t_wo[c]], w=[PT[yb_ + half]])
                    stt(xbs[kb][:], xbs[kb][:], ALPHA, ps[:, yb_ * 512:(yb_ + 2) * 512], ALU.mult, ALU.add,
                        r=[t_xbs[kb], PT[yb_], PT[yb_ + 1]], w=[t_xbs[kb]])
                    emit_ln(xbs[kb], t_xbs[kb], G, Bt, t_gb, sms[kb][0], sms[kb][1])
                    dma("sync", X32[tb * 128:(tb + 1) * 128, :], xbs[kb][:], r=[t_xbs[kb]], w=[t_X32[tb]])

                def back(tb=tb, tbl=tbl, k2=k2, ck=ck):
                    kb = tb % NR
                    tbk = 4 + 2 * (tb % 2)
                    emit_xT(xbs[kb], t_xbs[kb], xst[k2], t_xst[k2], tbl, tbk, tbk + 1)
                    if tbl == 3:
                        dma("sync", XT16v[:, :, ck * 512:(ck + 1) * 512], xst[k2][:], r=[t_xst[k2]], w=[t_XT[ck]])
                sk.push(front, back)
        sk.flush()

    def stage_final():
        for tb in range(NB):
            dma("sync", out[tb * 128:(tb + 1) * 128, :], X32[tb * 128:(tb + 1) * 128, :], r=[t_X32[tb]])

    def stage_XA(l):
        A.reset()
        wkv = A.tile([128, 8, 2048], BF16, "wkv")
        wq = A.tile([128, 8, D], BF16, "wq")
        wo = A.tile([128, 8, D], BF16, "wo")
        t_wkv = [Tok() for _ in range(8)]
        t_wq = [Tok() for _ in range(8)]
        t_wo = [Tok() for _ in range(8)]
        for c in range(8):
            dma("gpsimd", wkv[:, c, :], Wt["xa_wkv"][l, c * 128:(c + 1) * 128, :], w=[t_wkv[c]])
        for c in range(8):
            dma("gpsimd", wq[:, c, :], Wt["xa_wq"][l, c * 128:(c + 1) * 128, :], w=[t_wq[c]])
        for c in range(8):
            dma("gpsimd", wo[:, c, :], Wt["xa_wo"][l, c * 128:(c + 1) * 128, :], w=[t_wo[c]])
        G, Bt, t_gb = load_gb("ln2_g", "ln2_b", l)
        KmT = A.tile([128, 8, 512], BF16, "KmT"); t_Km = Tok()
        Vm = A.tile([128, 4, D], BF16, "Vm"); t_Vm = Tok()
        xt = A.tile([128, 8, 512], BF16, "xt"); t_xt = Tok()
        QxT = A.tile([128, 8, 512], BF16, "QxT"); t_Qx = Tok()
        axT = A.tile([128, 8, 512], BF16, "axT"); t_ax = Tok()
        et = [A.tile([128, 512], BF16, "et") for _ in range(4)]
        t_et = [Tok() for _ in range(4)]
        rden = A.tile([128, 512], F32, "rden"); t_rden = Tok()
        xst = [A.tile([128, 8, 512], BF16, "xst") for _ in range(2)]
        t_xst = [Tok(), Tok()]
        for c in range(8):
            b = c % 2
            for d in range(8):
                mm(bank(b), wkv[:, d, c * 128:(c + 1) * 128], memT[:, d, :], d == 0, d == 7, r=[t_wkv[d], t_memT], w=[PT[b]])
            cp(KmT[:, c, :], bank(b), r=[PT[b]], w=[t_Km], eng="scalar" if b == 0 else "vector")
        for sb in range(4):
            for half in range(2):
                b = 2 + half
                for d in range(8):
                    mm(bank(b), memT[:, d, sb * 128:(sb + 1) * 128], wkv[:, d, 1024 + half * 512:1024 + (half + 1) * 512],
                       d == 0, d == 7, r=[t_wkv[d], t_memT], w=[PT[b]])
                cp(Vm[:, sb, half * 512:(half + 1) * 512], bank(b), r=[PT[b]], w=[t_Vm], eng="scalar" if half == 0 else "vector")
        NR = 3
        sms = [ln_scratch() for _ in range(NR)]
        xbs = [A.tile([128, D], F32, "xbr") for _ in range(NR)]
        t_xbs = [Tok() for _ in range(NR)]
        zbs = [A.tile([128, D], F32, "zbr") for _ in range(2)]
        t_zbs = [Tok(), Tok()]
        skb = Skew(2)
        ei = [0]
        for ck in range(8):
            seq = ck // 4
            k2 = ck % 2
            dma("sync", xt[:], XT16v[:, :, ck * 512:(ck + 1) * 512], r=[t_XT[ck]], w=[t_xt])
            for c in range(8):
                b = c % 2
                for d in range(8):
                    mm(bank(b), wq[:, d, c * 128:(c + 1) * 128], xt[:, d, :], d == 0, d == 7, r=[t_wq[d], t_xt], w=[PT[b]])
                cp(QxT[:, c, :], bank(b), r=[PT[b]], w=[t_Qx], eng="scalar" if b == 0 else "vector")
            skh = Skew(1)
            for h in range(4):
                es = []
                for kb in range(2):
                    es.append(ei[0] % 4)
                    ei[0] += 1

                def hfront(h=h, es=es, seq=seq):
                    for kb in range(2):
                        e_i = es[kb]
                        for cc in range(2):
                            mm(bank(2 + kb), KmT[:, 2 * h + cc, seq * 256 + kb * 128:seq * 256 + (kb + 1) * 128], QxT[:, 2 * h + cc, :],
                               cc == 0, cc == 1, r=[t_Km, t_Qx], w=[PT[2 + kb]])
                        act(et[e_i][:], bank(2 + kb), AF.Exp, r=[PT[2 + kb]], w=[t_et[e_i]], scale=1.0 / 16)

                def hback(h=h, es=es, seq=seq):
                    for kb in range(2):
                        mm(bank(4), ones_bf[:], et[es[kb]][:], kb == 0, kb == 1, r=[t_et[es[kb]], t_const], w=[PT[4]])
                    vop("reciprocal", out=rden[:], in_=bank(4), r=[PT[4]], w=[t_rden])
                    for dc in range(2):
                        for kb in range(2):
                            mm(bank(5 + dc), Vm[:, seq * 2 + kb, h * 256 + dc * 128:h * 256 + (dc + 1) * 128], et[es[kb]][:],
                               kb == 0, kb == 1, r=[t_et[es[kb]], t_Vm], w=[PT[5 + dc]])
                        tt(axT[:, 2 * h + dc, :], bank(5 + dc), rden[:], ALU.mult, r=[PT[5 + dc], t_rden], w=[t_ax])
                skh.push(hfront, hback)
            skh.flush()
            for tbl in range(4):
                tb = ck * 4 + tbl

                def front(tb=tb, tbl=tbl):
                    kb2 = tb % NR
                    z2 = tb % 2
                    dma("sync", xbs[kb2][:], X32[tb * 128:(tb + 1) * 128, :], r=[t_X32[tb]], w=[t_xbs[kb2]])
                    for half in range(2):
                        for c in range(8):
                            mm(bank(half), axT[:, c, tbl * 128:(tbl + 1) * 128], wo[:, c, half * 512:(half + 1) * 512],
                               c == 0, c == 7, r=[t_ax, t_wo[c]], w=[PT[half]])
                    stt(xbs[kb2][:], xbs[kb2][:], ALPHA, ps[:, 0:1024], ALU.mult, ALU.add, r=[t_xbs[kb2], PT[0], PT[1]], w=[t_xbs[kb2]])
                    emit_ln(xbs[kb2], t_xbs[kb2], G, Bt, t_gb, sms[kb2][0], sms[kb2][1])
                    dma("sync", X32[tb * 128:(tb + 1) * 128, :], xbs[kb2][:], r=[t_xbs[kb2]], w=[t_X32[tb]])
                    act(zbs[z2][:], xbs[kb2][:], AF.Copy, r=[t_xbs[kb2]], w=[t_zbs[z2]], scale=ALPHA)
                    dma("sync", Z32[tb * 128:(tb + 1) * 128, :], zbs[z2][:], r=[t_zbs[z2]], w=[t_Z32[tb]])

                def back(tb=tb, tbl=tbl, k2=k2, ck=ck):
                    kb2 = tb % NR
                    emit_xT(xbs[kb2], t_xbs[kb2], xst[k2], t_xst[k2], tbl, 7, 6)
                    if tbl == 3:
                        dma("sync", XT16v[:, :, ck * 512:(ck + 1) * 512], xst[k2][:], r=[t_xst[k2]], w=[t_XT[ck]])
                skb.push(front, back)
        skb.flush()

    def stage_MOE(l):
        A.reset()
        wr = A.tile([128, 8, 16], BF16, "wr"); t_wr = Tok()
        dma("gpsimd", wr[:], Wt["moe_router"][l].rearrange("(c p) e -> p c e", p=128), w=[t_wr])
        o48 = A.tile([48, 1], F32, "o48")
        dma("sync", o48[:], Cn["o48"], w=[t_wr])
        xt = A.tile([128, 8, 512], BF16, "xt"); t_xt = Tok()
        AF48 = A.tile([128, 16, 48], F32, "AF48"); t_AF = Tok()
        P.op("vector", lambda e: e.memset(AF48[:], 0.0), w=[t_AF])
        ex = A.tile([128, 16], F32, "ex"); t_ex = Tok()
        sx = A.tile([128, 2], F32, "sx"); t_sx = Tok()
        AffT = A.tile([48, L], F32, "AffT"); t_AffT = Tok()
        MX = A.tile([48, 256], F32, "MX"); t_MX = Tok()
        IX = A.tile([48, 256], U32, "IX"); t_IX = Tok()
        IXF = A.tile([48, 256], F32, "IXF"); t_IXF = Tok()
        IDXT = A.tile([128, 2, 48], I32, "IDXT"); t_IDXT = Tok()
        GT = A.tile([128, 2, 48], F32, "GT"); t_GT = Tok()
        for ck in range(8):
            dma("sync", xt[:], XT16v[:, :, ck * 512:(ck + 1) * 512], w=[t_xt])
            for tbl in range(4):
                tb = ck * 4 + tbl
                seq, tbs = tb // 16, tb % 16
                for d in range(8):
                    mm(bank(0, 0, 16), xt[:, d, tbl * 128:(tbl + 1) * 128], wr[:, d, :], d == 0, d == 7, r=[t_xt, t_wr], w=[PT[0]])
                act(ex[:], bank(0, 0, 16), AF.Exp, r=[PT[0]], w=[t_ex, t_sx], accum_out=sx[:, 0:1])
                vop("reciprocal", out=sx[:, 1:2], in_=sx[:, 0:1], r=[t_sx], w=[t_sx])
                ts(AF48[:, tbs, seq * 32:seq * 32 + 16], ex[:], sx[:, 1:2], None, ALU.mult, r=[t_ex, t_sx], w=[t_AF])
        for tbs in range(16):
            b = 1 + tbs % 2
            tr(bank(b, 0, 128, 0, 48), AF48[:, tbs, :], r=[t_AF], w=[PT[b]])
            cp(AffT[:, tbs * 128:(tbs + 1) * 128], bank(b, 0, 128, 0, 48), r=[PT[b]], w=[t_AffT], eng="vector")
        for rd in range(32):
            sl = slice(rd * 8, rd * 8 + 8)
            vop("max", out=MX[:, sl], in_=AffT[:], r=[t_AffT], w=[t_MX])
            vop("max_index", out=IX[:, sl], in_max=MX[:, sl], in_values=AffT[:], r=[t_AffT, t_MX], w=[t_IX])
            vop("match_replace", out=AffT[:], in_to_replace=MX[:, sl], in_values=AffT[:], imm_value=-1.0, r=[t_MX, t_IX], w=[t_AffT])
        cp(IXF[:], IX[:], r=[t_IX], w=[t_IXF], eng="vector")
        ts(IXF[:], IXF[:], o48[:, 0:1], None, ALU.add, r=[t_IXF, t_wr], w=[t_IXF])
        for half in range(2):
            tr(bank(3, 0, 48), IXF[:, half * 128:(half + 1) * 128], r=[t_IXF], w=[PT[3]])
            cp(IDXT[:, half, :], bank(3, 0, 48), r=[PT[3]], w=[t_IDXT], eng="vector")
            tr(bank(4, 0, 48), MX[:, half * 128:(half + 1) * 128], r=[t_MX], w=[PT[4]])
            cp(GT[:, half, :], bank(4, 0, 48), r=[PT[4]], w=[t_GT], eng="vector")
        wg = [A.tile([128, 8, 512], BF16, "wg") for _ in range(2)]
        wu = [A.tile([128, 8, 512], BF16, "wu") for _ in range(2)]
        wd = [A.tile([128, 4, D], BF16, "wd") for _ in range(2)]
        t_w = [Tok(), Tok()]
        xg = [A.tile([128, D], F32, "xg") for _ in range(2)]
        t_xg = [Tok(), Tok()]
        xeT = A.tile([128, 8, 512], BF16, "xeT"); t_xe = Tok()
        sg = [A.tile([128, 512], F32, "sg") for _ in range(2)]
        t_sg = [Tok(), Tok()]
        hidT = A.tile([128, 4, 512], BF16, "hidT"); t_hid = Tok()
        yb = [A.tile([128, D], F32, "yb") for _ in range(2)]
        t_yb = [Tok(), Tok()]

        def load_w(e):
            k = e % 2
            dma("gpsimd", wg[k][:], Wt["moe_w_gate"][l, e].rearrange("(c p) f -> p c f", p=128), w=[t_w[k]])
            dma("gpsimd", wu[k][:], Wt["moe_w_up"][l, e].rearrange("(c p) f -> p c f", p=128), w=[t_w[k]])
            dma("gpsimd", wd[k][:], Wt["moe_w_down"][l, e].rearrange("(c p) n -> p c n", p=128), w=[t_w[k]])

        load_w(0)
        gi = 0
        for e in range(16):
            k = e % 2
            for sh in range(4):
                seq, half = sh // 2, sh % 2
                g2 = gi % 2
                gi += 1
                col = seq * 32 + e
                P.op("gpsimd", lambda eng, g2=g2, half=half, col=col: eng.indirect_dma_start(
                    out=xg[g2][:], out_offset=None, in_=X32,
                    in_offset=bass.IndirectOffsetOnAxis(ap=IDXT[:, half, col:col + 1], axis=0),
                    bounds_check=bcr(eng), oob_is_err=False), r=[t_IDXT], w=[t_xg[g2]], dma=True)
                emit_xT(xg[g2], t_xg[g2], xeT, t_xe, sh, 5 + 2 * 0, 6)
            if e + 1 < 16:
                load_w(e + 1)
            for fc in range(4):
                s2 = fc % 2
                for d in range(8):
                    mm(bank(1), wg[k][:, d, fc * 128:(fc + 1) * 128], xeT[:, d, :], d == 0, d == 7, r=[t_w[k], t_xe], w=[PT[1]])
                for d in range(8):
                    mm(bank(2), wu[k][:, d, fc * 128:(fc + 1) * 128], xeT[:, d, :], d == 0, d == 7, r=[t_w[k], t_xe], w=[PT[2]])
                act(sg[s2][:], bank(1), AF.Silu, r=[PT[1]], w=[t_sg[s2]])
                tt(hidT[:, fc, :], sg[s2][:], bank(2), ALU.mult, r=[t_sg[s2], PT[2]], w=[t_hid])
            for sh in range(4):
                seq, half = sh // 2, sh % 2
                y2 = sh % 2
                col = seq * 32 + e
                for h2 in range(2):
                    for fc in range(4):
                        mm(bank(3 + h2), hidT[:, fc, sh * 128:(sh + 1) * 128], wd[k][:, fc, h2 * 512:(h2 + 1) * 512],
                           fc == 0, fc == 3, r=[t_hid, t_w[k]], w=[PT[3 + h2]])
                ts(yb[y2][:], ps[:, 3 * 512:5 * 512], GT[:, half, col:col + 1], None, ALU.mult, r=[PT[3], PT[4], t_GT], w=[t_yb[y2]])
                P.op("gpsimd", lambda eng, y2=y2, half=half, col=col: eng.indirect_dma_start(
                    out=Z32, out_offset=bass.IndirectOffsetOnAxis(ap=IDXT[:, half, col:col + 1], axis=0),
                    in_=yb[y2][:], in_offset=None, compute_op=ALU.add,
                    bounds_check=bcr(eng), oob_is_err=False), r=[t_IDXT, t_yb[y2]], w=[t_Zs[seq]], dma=True)

    def stage_LN3(l, dst):
        A.reset()
        G, Bt, t_gb = load_gb("ln3_g", "ln3_b", l)
        NR = 6
        sms = [ln_scratch() for _ in range(NR)]
        xbs = [A.tile([128, D], F32, "xbr") for _ in range(NR)]
        t_xbs = [Tok() for _ in range(NR)]
        xst = [A.tile([128, 8, 512], BF16, "xst") for _ in range(2)]
        t_xst = [Tok(), Tok()]
        sk = Skew(3)
        for tb in range(NB):
            def front(tb=tb):
                kb = tb % NR
                dma("sync", xbs[kb][:], Z32[tb * 128:(tb + 1) * 128, :], w=[t_xbs[kb]])
                emit_ln(xbs[kb], t_xbs[kb], G, Bt, t_gb, sms[kb][0], sms[kb][1])
                dma("sync", dst[tb * 128:(tb + 1) * 128, :], xbs[kb][:], r=[t_xbs[kb]], w=[t_X32[tb]])

            def back(tb=tb):
                kb = tb % NR
                ck, tbl = tb // 4, tb % 4
                k2 = ck % 2
                bb = 2 * (tb % 4)
                emit_xT(xbs[kb], t_xbs[kb], xst[k2], t_xst[k2], tbl, bb, bb + 1)
                if tbl == 3:
                    dma("sync", XT16v[:, :, ck * 512:(ck + 1) * 512], xst[k2][:], r=[t_xst[k2]], w=[t_XT[ck]])
            sk.push(front, back)
        sk.flush()

    def stage_HY(l):
        A.reset()
        HS = A.tile([128, 16, 256], BF16, "HS"); t_HS = Tok()
        HD = A.tile([128, 16, 256], BF16, "HD"); t_HD = Tok()
        x0T = A.tile([128, 2, T], BF16, "x0T"); t_x0 = Tok()
        zT = A.tile([128, 2, T], BF16, "zT"); t_zT = Tok()
        z_tm = A.tile([128, 16, 512], BF16, "z_tm"); t_ztm = Tok()
        hbias = A.tile([128, 2], F32, "hbias"); t_hb = Tok()
        for j in range(2):
            dma("sync", hbias[:, j:j + 1], Wt["hy_bias"][l, j * 128:(j + 1) * 128].rearrange("(p o) -> p o", o=1), w=[t_hb])
        sub_mark = A.off
        featsT = A.tile([33, L], F32, "featsT"); t_f = Tok()
        dma("sync", featsT[:], Cn["featsT"], w=[t_f])
        negt = A.tile([128, 16], F32, "negt"); m0 = A.tile([128, 16], F32, "m0")
        dma("sync", negt[:], Cn["negt"], w=[t_f])
        dma("sync", m0[:], Cn["m0"], w=[t_f])
        w1 = A.tile([33, 64], F32, "w1"); w2 = A.tile([64, 64], F32, "w2"); w3 = A.tile([64, 64], F32, "w3")
        wout = A.tile([64, 512], F32, "wout")
        dma("sync", w1[:], Wt["hy_w1"][l], w=[t_f])
        dma("sync", w2[:], Wt["hy_w2"][l], w=[t_f])
        dma("sync", w3[:], Wt["hy_w3"][l], w=[t_f])
        dma("sync", wout[:], Wt["hy_wout"][l], w=[t_f])
        prm = A.tile([64, 8], F32, "prm"); t_prm = Tok()
        for i, nm in enumerate(("hy_b1", "hy_b2", "hy_b3", "hy_freq")):
            dma("sync", prm[:, i:i + 1], Wt[nm][l].rearrange("(p o) -> p o", o=1), w=[t_prm])
        ts(prm[:, 4:7], prm[:, 0:3], prm[:, 3:4], None, ALU.mult, r=[t_prm], w=[t_prm])
        AD = A.tile([128, 512], F32, "AD"); t_AD = Tok()
        dma("sync", AD[:], Wt["hy_decay"][l].rearrange("a c -> (a c)").partition_broadcast(128), w=[t_AD])
        act(AD[:], AD[:], AF.Abs, r=[t_AD], w=[t_AD])
        hA = A.tile([64, L], F32, "hA"); hB = A.tile([64, L], F32, "hB")
        t_hA, t_hB = Tok(), Tok()
        a1 = [A.tile([64, 512], F32, "a1") for _ in range(2)]
        a2 = [A.tile([64, 512], F32, "a2") for _ in range(2)]
        t_a = [Tok(), Tok()]
        chain = [(featsT, t_f, 33, w1, hA, t_hA), (hA, t_hA, 64, w2, hB, t_hB), (hB, t_hB, 64, w3, hA, t_hA)]
        for i, (src, t_src, kk, wi, dst, t_dst) in enumerate(chain):
            for nq in range(4):
                b = nq % 2
                mm(bank(b, 0, 512, 0, 64), wi[0:kk, :], src[0:kk, nq * 512:(nq + 1) * 512], True, True, r=[t_f, t_src], w=[PT[b]])
                ts(a1[b][:], bank(b, 0, 512, 0, 64), prm[:, 3:4], prm[:, 4 + i:5 + i], ALU.mult, ALU.add, r=[PT[b], t_prm], w=[t_a[b]])
                ts(a2[b][:], a1[b][:], 1.0 / TWO_PI, MAGIC, ALU.mult, ALU.add, r=[t_a[b]], w=[t_a[b]])
                ts(a2[b][:], a2[b][:], MAGIC, -TWO_PI, ALU.subtract, ALU.mult, r=[t_a[b]], w=[t_a[b]])
                tt(a1[b][:], a1[b][:], a2[b][:], ALU.add, r=[t_a[b]], w=[t_a[b]])
                ts(a1[b][:], a1[b][:], 3.1415925, -3.1415925, ALU.min, ALU.max, r=[t_a[b]], w=[t_a[b]])
                act(dst[:, nq * 512:(nq + 1) * 512], a1[b][:], AF.Sin, r=[t_a[b]], w=[t_dst])
        h3, t_h3 = hA, t_hA
        E = [A.tile([128, 512], F32, "E") for _ in range(2)]
        fl = [A.tile([128, 512], F32, "fl") for _ in range(2)]
        t_E = [Tok(), Tok()]
        t_fl = [Tok(), Tok()]
        for lb in range(16):
            b = 2 + lb % 2
            k = lb % 2
            mm(bank(b), h3[:, lb * 128:(lb + 1) * 128], wout[:], True, True, r=[t_h3, t_f], w=[PT[b]])
            act(E[k][:], AD[:], AF.Exp, r=[t_AD, t_f], w=[t_E[k]], scale=negt[:, lb:lb + 1])
            tt(fl[k][:], bank(b), E[k][:], ALU.mult, r=[PT[b], t_E[k]], w=[t_fl[k]])
            ts(fl[k][:, 256:512], fl[k][:, 256:512], m0[:, lb:lb + 1], None, ALU.mult, r=[t_fl[k], t_f], w=[t_fl[k]], eng="vector")
            tt(HS[:, lb, :], fl[k][:, 0:256], fl[k][:, 256:512], ALU.add, r=[t_fl[k]], w=[t_HS], eng="gpsimd")
            tt(HD[:, lb, :], fl[k][:, 256:512], fl[k][:, 0:256], ALU.subtract, r=[t_fl[k]], w=[t_HD], eng="vector")
        P.barrier()
        A.off = sub_mark
        cw = A.tile([128, 6, 4], F32, "cw"); t_cw = Tok()
        for cc in range(6):
            for k in range(3):
                dma("sync", cw[:, cc, k:k + 1], Wt["hy_conv_w"][l, k, cc * 128:(cc + 1) * 128].rearrange("(p o) -> p o", o=1), w=[t_cw])
            dma("sync", cw[:, cc, 3:4], Wt["hy_conv_b"][l, cc * 128:(cc + 1) * 128].rearrange("(p o) -> p o", o=1), w=[t_cw])
        ut = [A.tile([128, L], F32, "ut") for _ in range(2)]
        t_ut = [Tok(), Tok()]
        uc = [A.tile([128, L], F32, "uc") for _ in range(2)]
        t_uc = [Tok(), Tok()]
        zf = A.tile([128, L], F32, "zf"); t_zf = Tok()
        ui = [0]

        def conv(seq, cc, dstk):
            u = ui[0] % 2
            ui[0] += 1
            dma("sync", ut[u][:], UT[cc * 128:(cc + 1) * 128, seq * L:(seq + 1) * L], w=[t_ut[u]])
            o, t_o = uc[dstk], t_uc[dstk]
            ts(o[:], ut[u][:], cw[:, cc, 1:2], cw[:, cc, 3:4], ALU.mult, ALU.add, r=[t_ut[u], t_cw], w=[t_o])
            stt(o[:, 1:L], ut[u][:, 0:L - 1], cw[:, cc, 0:1], o[:, 1:L], ALU.mult, ALU.add, r=[t_ut[u], t_cw, t_o], w=[t_o])
            stt(o[:, 0:L - 1], ut[u][:, 1:L], cw[:, cc, 2:3], o[:, 0:L - 1], ALU.mult, ALU.add, r=[t_ut[u], t_cw, t_o], w=[t_o])

        for seq in range(2):
            for j in range(2):
                conv(seq, 2 + j, 0)
                conv(seq, 4 + j, 1)
                tt(zf[:], uc[0][:], uc[1][:], ALU.mult, r=[t_uc[0], t_uc[1]], w=[t_zf])
                cp(zT[:, j, seq * L:(seq + 1) * L], zf[:], r=[t_zf], w=[t_zT], eng="scalar")
                for g in range(4):
                    b = 4 + g % 2
                    for i in range(4):
                        tb = g * 4 + i
                        tr(bank(b, i * 128, (i + 1) * 128), zf[:, tb * 128:(tb + 1) * 128], r=[t_zf], w=[PT[b]])
                    c0 = seq * 256 + j * 128
                    cp(z_tm[:, g * 4:(g + 1) * 4, c0:c0 + 128], bank(b).rearrange("p (a t) -> p a t", a=4), r=[PT[b]], w=[t_ztm],
                       eng="vector" if g % 2 else "scalar")
                conv(seq, j, 0)
                cp(x0T[:, j, seq * L:(seq + 1) * L], uc[0][:], r=[t_uc[0]], w=[t_x0], eng="scalar")
        P.barrier()
        A.off = sub_mark
        Pre = A.tile([128, 16, 512], BF16, "Pre"); t_Pre = Tok()
        Pim = A.tile([128, 16, 512], BF16, "Pim"); t_Pim = Tok()
        cfb = [A.tile([128, 16, 128], BF16, "cfb") for _ in range(2)]
        sfb = [A.tile([128, 16, 128], BF16, "sfb") for _ in range(2)]
        t_cs = [Tok(), Tok()]
        hh = [A.tile([128, 512], F32, "hh") for _ in range(2)]
        t_hh = [Tok(), Tok()]
        _tq1 = A.tile([128, 4, 512], F32, "tq")
        tq_ = [_tq1, _tq1]
        _ttq = Tok()
        t_tq = [_ttq, _ttq]
        for fb in range(16):
            k = fb % 2
            b0 = 4 * k
            dma("sync", cfb[k][:], Cn["cf"][fb], w=[t_cs[k]])
            dma("sync", sfb[k][:], Cn["sf"][fb], w=[t_cs[k]])
            for lb in range(16):
                mm(bank(b0, 0, 256), cfb[k][:, lb, :], HS[:, lb, :], lb == 0, lb == 15, r=[t_cs[k]], w=[PT[b0]])
            for lb in range(16):
                mm(bank(b0 + 1, 0, 256), sfb[k][:, lb, :], HD[:, lb, :], lb == 0, lb == 15, r=[t_cs[k]], w=[PT[b0 + 1]])
            for tb in range(16):
                mm(bank(b0 + 2), cfb[k][:, tb, :], z_tm[:, tb, :], tb == 0, tb == 15, r=[t_cs[k]], w=[PT[b0 + 2]])
            for tb in range(16):
                mm(bank(b0 + 3), sfb[k][:, tb, :], z_tm[:, tb, :], tb == 0, tb == 15, r=[t_cs[k]], w=[PT[b0 + 3]])
            cp(hh[k][:, 0:256], bank(b0, 0, 256), r=[PT[b0]], w=[t_hh[k]], eng="vector")
            cp(hh[k][:, 256:512], bank(b0 + 1, 0, 256), r=[PT[b0 + 1]], w=[t_hh[k]], eng="vector")
            hre = hh[k][:, 0:256].unsqueeze(1).to_broadcast([128, 2, 256])
            him = hh[k][:, 256:512].unsqueeze(1).to_broadcast([128, 2, 256])
            Av = bank(b0 + 2).rearrange("p (s c) -> p s c", s=2)
            Bv = bank(b0 + 3).rearrange("p (s c) -> p s c", s=2)
            tv = [tq_[k][:, i, :].rearrange("p (s c) -> p s c", s=2) for i in range(4)]
            tt(tv[0], Av, hre, ALU.mult, r=[PT[b0 + 2], t_hh[k]], w=[t_tq[k]])
            tt(tv[1], Bv, him, ALU.mult, r=[PT[b0 + 3], t_hh[k]], w=[t_tq[k]])
            tt(tv[2], Bv, hre, ALU.mult, r=[PT[b0 + 3], t_hh[k]], w=[t_tq[k]])
            tt(tv[3], Av, him, ALU.mult, r=[PT[b0 + 2], t_hh[k]], w=[t_tq[k]])
            tt(Pre[:, fb, :], tq_[k][:, 0, :], tq_[k][:, 1, :], ALU.add, r=[t_tq[k]], w=[t_Pre], eng="gpsimd")
            tt(Pim[:, fb, :], tq_[k][:, 2, :], tq_[k][:, 3, :], ALU.subtract, r=[t_tq[k]], w=[t_Pim], eng="gpsimd")
        cit = A.tile([128, 16, 512], BF16, "cit")
        sit = A.tile([128, 16, 512], BF16, "sit")
        t_ci = Tok()
        tmp = [A.tile([128, 512], F32, "tmp") for _ in range(2)]
        t_tmp = [Tok(), Tok()]
        obt = [A.tile([128, 512], BF16, "obt") for _ in range(2)]
        t_obt = [Tok(), Tok()]
        n = 0
        for tq in range(4):
            dma("sync", cit[:], Cn["ci"][:, :, tq * 512:(tq + 1) * 512], w=[t_ci])
            dma("sync", sit[:], Cn["si"][:, :, tq * 512:(tq + 1) * 512], w=[t_ci])
            for sq in range(2):
                for j in range(2):
                    b = n % 2
                    n += 1
                    c0 = sq * 256 + j * 128
                    for kb in range(16):
                        mm(bank(b), Pre[:, kb, c0:c0 + 128], cit[:, kb, :], kb == 0, False, r=[t_Pre, t_ci], w=[PT[b]])
                        mm(bank(b), Pim[:, kb, c0:c0 + 128], sit[:, kb, :], False, kb == 15, r=[t_Pim, t_ci], w=[PT[b]])
                    t0 = sq * L + tq * 512
                    stt(tmp[b][:], zT[:, j, t0:t0 + 512], hbias[:, j:j + 1], bank(b), ALU.mult, ALU.add, r=[t_zT, t_hb, PT[b]], w=[t_tmp[b]])
                    tt(obt[b][:], tmp[b][:], x0T[:, j, t0:t0 + 512], ALU.mult, r=[t_tmp[b], t_x0], w=[t_obt[b]], eng="gpsimd")
                    dma("sync", AT[512 + j * 128:512 + (j + 1) * 128, t0:t0 + 512], obt[b][:], r=[t_obt[b]], w=[tAT((4 + j, t0 // 512))])

    stage_init()
    P.barrier()
    for l in range(NL if stop_after != "init" else 0):
        stage_A(l)
        P.barrier()
        if stop_after == "A":
            break
        if HY:
            stage_HY(l)
            P.barrier()
            if stop_after == "HY":
                break
        stage_ATT()
        P.barrier()
        if stop_after == "ATT":
            break
        stage_WO(l, x_in if l == 0 else X32, HY)
        P.barrier()
        if stop_after == "WO":
            break
        stage_XA(l)
        P.barrier()
        if stop_after == "XA":
            break
        stage_MOE(l)
        P.barrier()
        last = (l == NL - 1)
        stage_LN3(l, out if last else X32)
        P.barrier()
    if stop_after in ("WO", "XA"):
        stage_final()
    P.emit(st)
    st.close()
    return nc


_CONSTS = None


def _run(inputs, NL=DEPTH, dbg=(), stop_after=None, cores=NCORES):
    global _CONSTS
    if _CONSTS is None:
        _CONSTS = _host_consts()
    nc = build(NL, dbg, stop_after)
    x = np.ascontiguousarray(np.asarray(inputs["x"], dtype=np.float32)).reshape(16, L, D)
    mem = np.ascontiguousarray(np.asarray(inputs["mem"], dtype=np.float32)).reshape(16, 256, D)
    in_maps = []
    for c in range(cores):
        m = {"x": x[2 * c:2 * c + 2].reshape(T, D), "mem": mem[2 * c:2 * c + 2].reshape(512, D)}
        for k in _W_SHAPES:
            m[k] = np.ascontiguousarray(np.asarray(inputs[k], dtype=np.float32))
        for k in _CONST_SHAPES:
            m[k] = _CONSTS[k]
        in_maps.append(m)
    res = run_bass_kernel_spmd(nc, in_maps, core_ids=list(range(cores)))
    return res.results


def kernel(**inputs):
    res = _run(inputs)
    out = np.stack([r["out"].reshape(2, L, D) for r in res], axis=0).reshape(16, L, D)
    return out.astype(np.float32)
```

```python
import contextlib
import os
CUT = int(os.environ.get('A_CUT', '99'))
SUB = int(os.environ.get('A_SUB', '99'))
HY = int(os.environ.get('A_HY', '1'))
import math
import numpy as np
import ml_dtypes
import concourse.bass as bass
import concourse.mybir as mybir
from concourse.bass_utils import run_bass_kernel_spmd

F32 = mybir.dt.float32
BF16 = mybir.dt.bfloat16
I32 = mybir.dt.int32
U32 = mybir.dt.uint32
ALU = mybir.AluOpType
AF = mybir.ActivationFunctionType
AX = mybir.AxisListType

NCORES = 8
L = 2048
T = 4096
D = 1024
NB = 32
DEPTH = 4
ALPHA = float((2 * DEPTH) ** 0.25)
RMS_EPS = 1e-6
LN_EPS = 1e-5
TWO_PI = 2.0 * math.pi
MAGIC = 12582912.0


class Tok:
    __slots__ = ("w", "r")

    def __init__(self):
        self.w = None
        self.r = {}


class Prog:
    ENG = ["tensor", "vector", "scalar", "gpsimd", "sync"]
    NDMA = 8

    def __init__(self, nc):
        self.nc = nc
        self.ops = {e: [] for e in self.ENG}
        self.cnt = {e: 0 for e in self.ENG}
        self.seen = {e: {} for e in self.ENG}
        self.dma_n = {e: 0 for e in self.ENG}
        self.last = {}

    def op(self, eng, fn, r=(), w=(), dma=False):
        deps = {}

        def add(key, val):
            if deps.get(key, 0) < val:
                deps[key] = val

        for t in r:
            if t.w is not None:
                add(*t.w)
        for t in w:
            if t.w is not None:
                add(*t.w)
            for k, v in t.r.items():
                add(k, v)
        if dma:
            j = self.dma_n[eng]
            self.dma_n[eng] += 1
            slot = j % self.NDMA
            k = j // self.NDMA + 1
            me = ((eng, "dma", slot), 16 * k)
            if k > 1:
                add((eng, "dma", slot), 16 * (k - 1))
        else:
            self.cnt[eng] += 1
            me = ((eng, "c"), self.cnt[eng])
        seen = self.seen[eng]
        waits = []
        for key, val in deps.items():
            if seen.get(key, 0) >= val:
                continue
            seen[key] = val
            waits.append((key, val))
        self.ops[eng].append((fn, waits, me))
        self.last[me[0]] = me[1]
        for t in r:
            if t.r.get(me[0], 0) < me[1]:
                t.r[me[0]] = me[1]
        for t in w:
            t.w = me
            t.r = {}
        return me

    def barrier(self):
        snap = dict(self.last)
        for e in self.ENG:
            seen = self.seen[e]
            waits = []
            for key, val in snap.items():
                if seen.get(key, 0) >= val:
                    continue
                seen[key] = val
                waits.append((key, val))
            if waits:
                self.ops[e].append((None, waits, None))

    def emit(self, stack):
        nc = self.nc
        self.barrier()
        sems = {}
        for e in self.ENG:
            for fn, waits, me in self.ops[e]:
                for key, _ in waits:
                    if key not in sems:
                        sems[key] = None
                if me is not None and me[0] not in sems:
                    sems[me[0]] = None
        for key in sems:
            sems[key] = stack.enter_context(nc.semaphore("s_" + "_".join(str(x) for x in key)))
        block = stack.enter_context(nc.Block())

        def run(e):
            def body(eng):
                for fn, waits, me in self.ops[e]:
                    for key, val in waits:
                        eng.wait_ge(sems[key], val)
                    if fn is not None:
                        ins = fn(eng)
                        ins.then_inc(sems[me[0]], 16 if me[0][1] == "dma" else 1)
            return body

        block.tensor(run("tensor"))
        block.vector(run("vector"))
        block.scalar(run("scalar"))
        block.gpsimd(run("gpsimd"))
        block.sync(run("sync"))


def _esz(dt):
    return 2 if dt == BF16 else 4


class Arena:
    BASE = 17408
    LIMIT = 229376

    def __init__(self, nc):
        self.nc = nc
        self.off = self.BASE
        self.mark = self.BASE
        self.n = 0

    def reset(self):
        self.off = self.mark

    def tile(self, shape, dt, name="t"):
        sz = int(np.prod(shape[1:])) * _esz(dt)
        self.n += 1
        t = self.nc.alloc_sbuf_tensor_at(f"{name}_{self.n}", list(shape), dt, offset=self.off)
        self.off += (sz + 63) // 64 * 64
        assert self.off <= self.LIMIT, (name, self.off)
        return t


def _host_consts():
    c = {}
    c["ident"] = np.eye(128, dtype=np.float32)
    t = np.arange(L)
    row = (t // 64).astype(np.float64)
    col = (t % 64).astype(np.float64)

    def tab(n):
        inv = 10000.0 ** (-(np.arange(n, dtype=np.float64) * 2.0 / (2 * n)))
        ang = np.stack([row[:, None] * inv[None], col[:, None] * inv[None]], axis=1)
        cs = np.cos(ang).astype(np.float32).reshape(16, 128, 2 * n).transpose(1, 0, 2)
        sn = np.sin(ang).astype(np.float32).reshape(16, 128, 2 * n).transpose(1, 0, 2)
        return np.ascontiguousarray(cs), np.ascontiguousarray(sn)

    c["cq"], c["sq"] = tab(16)
    c["cm"], c["sm"] = tab(8)
    tt = np.linspace(0.0, 1.0, L, dtype=np.float32)[:, None]
    bands = 16
    fb = np.linspace(1e-4, bands - 1, bands, dtype=np.float32)[None, :]
    w = (2.0 * math.pi * np.arange(L, dtype=np.float32)[:, None] / L).astype(np.float32)
    feats = np.concatenate([tt, np.cos(fb * w), -np.sin(fb * w)], axis=-1).astype(np.float32)
    c["featsT"] = np.ascontiguousarray(feats.T)
    c["negt"] = np.ascontiguousarray((-tt[:, 0]).reshape(16, 128).T).astype(np.float32)
    m0 = np.ones((128, 16), np.float32)
    m0[0, 0] = 0.0
    c["m0"] = m0
    k = np.arange(L, dtype=np.int64)
    n = np.arange(L, dtype=np.int64)
    ph = ((2 * k[None, :] + 1) * n[:, None]) % 8192
    ang = ph.astype(np.float64) * (math.pi / 4096.0)
    C = np.cos(ang)
    S = np.sin(ang)
    def fwd(M):
        return np.ascontiguousarray(M.reshape(16, 128, 16, 128).transpose(2, 1, 0, 3)).astype(ml_dtypes.bfloat16)
    c["cf"] = fwd(C)
    c["sf"] = fwd(S)
    def inv(M):
        return np.ascontiguousarray((M.T / 2048.0).reshape(16, 128, L).transpose(1, 0, 2)).astype(ml_dtypes.bfloat16)
    c["ci"] = inv(C)
    c["si"] = inv(S)
    o48 = np.zeros((48, 1), np.float32)
    o48[32:] = 2048.0
    c["o48"] = o48
    return c


_CONST_SHAPES = {
    "ident": ([128, 128], F32), "cq": ([128, 16, 32], F32), "sq": ([128, 16, 32], F32),
    "cm": ([128, 16, 16], F32), "sm": ([128, 16, 16], F32), "featsT": ([33, L], F32),
    "negt": ([128, 16], F32), "m0": ([128, 16], F32),
    "cf": ([16, 128, 16, 128], BF16), "sf": ([16, 128, 16, 128], BF16),
    "ci": ([128, 16, L], BF16), "si": ([128, 16, L], BF16), "o48": ([48, 1], F32),
}

_W_SHAPES = {
    "w_in": [4, 1024, 1952], "gqa_q_norm": [4, 64], "gqa_k_norm": [4, 64], "hy_conv_w": [4, 3, 768],
    "hy_conv_b": [4, 768], "hy_w1": [4, 33, 64], "hy_b1": [4, 64], "hy_w2": [4, 64, 64], "hy_b2": [4, 64],
    "hy_w3": [4, 64, 64], "hy_b3": [4, 64], "hy_wout": [4, 64, 512], "hy_freq": [4, 64],
    "hy_decay": [4, 2, 256], "hy_bias": [4, 256], "mla_q_norm": [4, 256], "mla_w_uq": [4, 256, 384],
    "mla_kv_norm": [4, 128], "mla_w_ukv": [4, 128, 512], "w_o": [4, 1024, 1024], "ln1_g": [4, 1024],
    "ln1_b": [4, 1024], "xa_wq": [4, 1024, 1024], "xa_wkv": [4, 1024, 2048], "xa_wo": [4, 1024, 1024],
    "ln2_g": [4, 1024], "ln2_b": [4, 1024], "moe_router": [4, 1024, 16], "moe_w_gate": [4, 16, 1024, 512],
    "moe_w_up": [4, 16, 1024, 512], "moe_w_down": [4, 16, 512, 1024], "ln3_g": [4, 1024], "ln3_b": [4, 1024],
}


def build(NL=DEPTH, dbg=(), stop_after=None):
    nc = bass.Bass("TRN2", target_bir_lowering=False)

    def din(name, shape, dt=F32):
        return nc.dram_tensor(name, list(shape), dt, kind="ExternalInput").ap()

    x_in = din("x", [T, D])
    mem_in = din("mem", [512, D])
    Wt = {k: din(k, s) for k, s in _W_SHAPES.items()}
    Cn = {k: din(k, s, dt) for k, (s, dt) in _CONST_SHAPES.items()}
    out = nc.dram_tensor("out", [T, D], F32, kind="ExternalOutput").ap()

    def dscr(name, shape, dt):
        kind = "ExternalOutput" if name in dbg else "Internal"
        return nc.dram_tensor(name, list(shape), dt, kind=kind).ap()

    X32 = dscr("X32", [T, D], F32)
    Z32 = dscr("Z32", [T, D], F32)
    XT16 = dscr("XT16", [D, T], BF16)
    QTd = dscr("QTd", [6, 128, T], BF16)
    QCTd = dscr("QCTd", [4, 96, T], BF16)
    KCTd = dscr("KCTd", [4, 96, T], BF16)
    UT = dscr("UT", [768, T], F32)
    AT = dscr("AT", [D, T], BF16)
    XT16v = XT16.rearrange("(c p) t -> p c t", p=128)
    ATv = AT.rearrange("(c p) t -> p c t", p=128)
    t_X32 = [Tok() for _ in range(NB)]
    t_Z32 = [Tok() for _ in range(NB)]
    t_Zs = [Tok(), Tok()]
    t_XT = [Tok() for _ in range(8)]
    t_QT = [Tok() for _ in range(8)]
    t_QCT = [Tok() for _ in range(8)]
    t_KCT = [Tok() for _ in range(8)]
    t_UT = [Tok() for _ in range(8)]
    t_AT = {}

    def tAT(key):
        if key not in t_AT:
            t_AT[key] = Tok()
        return t_AT[key]

    P = Prog(nc)
    A = Arena(nc)
    _bc = {}

    def bcr(eng):
        if "r" not in _bc:
            _bc["r"] = eng.to_reg(T - 1)
        return _bc["r"]
    st = contextlib.ExitStack()
    ps = st.enter_context(nc.psum_tensor("ps", [128, 4096], F32))
    PT = [Tok() for _ in range(8)]

    def bank(b, lo=0, hi=512, p0=0, p1=128):
        return ps[p0:p1, b * 512 + lo:b * 512 + hi]

    def dma(q, out_, in_, r=(), w=()):
        P.op(q, lambda e: e.dma_start(out=out_, in_=in_), r=r, w=w, dma=True)

    def mm(out_, lhsT, rhs, start, stop, r=(), w=()):
        P.op("tensor", lambda e: e.matmul(out_, lhsT=lhsT, rhs=rhs, start=start, stop=stop), r=r, w=w)

    def tr(out_, in_, r=(), w=()):
        pp = in_.shape[0]
        P.op("tensor", lambda e: e.transpose(out=out_, in_=in_, identity=ident[0:pp, 0:pp]), r=list(r) + [t_const], w=w)

    def act(out_, in_, func, r=(), w=(), **kw):
        P.op("scalar", lambda e: e.activation(out=out_, in_=in_, func=func, **kw), r=r, w=w)

    def vop(name, r=(), w=(), eng="vector", **kw):
        P.op(eng, lambda e: getattr(e, name)(**kw), r=r, w=w)

    def tt(out_, in0, in1, op, r=(), w=(), eng="vector"):
        P.op(eng, lambda e: e.tensor_tensor(out=out_, in0=in0, in1=in1, op=op), r=r, w=w)

    def ts(out_, in0, s1, s2, op0, op1=None, r=(), w=(), eng="vector"):
        if op1 is None:
            P.op(eng, lambda e: e.tensor_scalar(out=out_, in0=in0, scalar1=s1, scalar2=None, op0=op0), r=r, w=w)
        else:
            P.op(eng, lambda e: e.tensor_scalar(out=out_, in0=in0, scalar1=s1, scalar2=s2, op0=op0, op1=op1), r=r, w=w)

    def stt(out_, in0, scalar, in1, op0, op1, r=(), w=(), eng="vector"):
        P.op(eng, lambda e: e.scalar_tensor_tensor(out=out_, in0=in0, scalar=scalar, in1=in1, op0=op0, op1=op1), r=r, w=w)

    def cp(out_, in_, r=(), w=(), eng="vector"):
        if eng == "scalar":
            P.op("scalar", lambda e: e.copy(out=out_, in_=in_), r=r, w=w)
        else:
            P.op(eng, lambda e: e.tensor_copy(out=out_, in_=in_), r=r, w=w)

    t_const = Tok()
    ident = A.tile([128, 128], F32, "ident")
    dma("sync", ident[:], Cn["ident"], w=[t_const])
    ones_bf = A.tile([128, 128], BF16, "ones")
    P.op("vector", lambda e: e.memset(ones_bf[:], 1.0), w=[t_const])
    ones_f = A.tile([128, 64], F32, "ones_f")
    P.op("vector", lambda e: e.memset(ones_f[:], 1.0), w=[t_const])
    CQ = A.tile([128, 16, 32], F32, "CQ"); SQ = A.tile([128, 16, 32], F32, "SQ")
    CM = A.tile([128, 16, 16], F32, "CM"); SM = A.tile([128, 16, 16], F32, "SM")
    for tl, nm in ((CQ, "cq"), (SQ, "sq"), (CM, "cm"), (SM, "sm")):
        dma("sync", tl[:], Cn[nm], w=[t_const])
    VA = A.tile([128, NB, 2, 65], BF16, "VA")
    VC = A.tile([128, NB, 4, 65], BF16, "VC")
    t_VA = [Tok() for _ in range(NB)]
    t_VC = [Tok() for _ in range(NB)]
    P.op("vector", lambda e: e.memset(VA[:], 1.0), w=t_VA)
    P.op("vector", lambda e: e.memset(VC[:], 1.0), w=t_VC)
    memT = A.tile([128, 8, 512], BF16, "memT")
    t_memT = Tok()
    A.mark = A.off

    def emit_xT(src, t_src, xst, t_xst, tbl, b0, b1):
        for c in range(8):
            b = b0 if c < 4 else b1
            tr(bank(b, (c % 4) * 128, (c % 4) * 128 + 128), src[:, c * 128:(c + 1) * 128], r=[t_src], w=[PT[b]])
        cp(xst[:, 0:4, tbl * 128:(tbl + 1) * 128], bank(b0).rearrange("p (c t) -> p c t", c=4),
           r=[PT[b0]], w=[t_xst], eng="scalar")
        cp(xst[:, 4:8, tbl * 128:(tbl + 1) * 128], bank(b1).rearrange("p (c t) -> p c t", c=4),
           r=[PT[b1]], w=[t_xst], eng="vector")

    def emit_ln(zt, t_z, G, Bt, t_gb, sm, t_sm):
        junk = sm["junk"]
        act(junk[:], zt[:], AF.Copy, r=[t_z], w=[sm["tj"], t_sm], scale=1.0 / D, accum_out=sm["s"][:, 2:3])
        act(junk[:], zt[:], AF.Square, r=[t_z], w=[sm["tj"], t_sm], scale=float(D ** -0.5), accum_out=sm["s"][:, 1:2])
        stt(sm["s"][:, 4:5], sm["s"][:, 2:3], sm["s"][:, 2:3], sm["s"][:, 1:2], ALU.mult, ALU.subtract, r=[t_sm], w=[t_sm])
        act(sm["s"][:, 5:6], sm["s"][:, 4:5], AF.Sqrt, r=[t_sm], w=[t_sm], bias=LN_EPS, scale=-1.0)
        vop("reciprocal", out=sm["s"][:, 5:6], in_=sm["s"][:, 5:6], r=[t_sm], w=[t_sm])
        stt(sm["s"][:, 6:7], sm["s"][:, 2:3], -1.0, sm["s"][:, 5:6], ALU.mult, ALU.mult, r=[t_sm], w=[t_sm])
        act(zt[:], zt[:], AF.Identity, r=[t_z, t_sm], w=[t_z], scale=sm["s"][:, 5:6], bias=sm["s"][:, 6:7])
        tt(zt[:], zt[:], G[:], ALU.mult, r=[t_z, t_gb], w=[t_z], eng="vector")
        tt(zt[:], zt[:], Bt[:], ALU.add, r=[t_z, t_gb], w=[t_z], eng="gpsimd")

    def load_gb(gname, bname, l):
        G = A.tile([128, D], F32, "G"); Bt = A.tile([128, D], F32, "B")
        t_gb = Tok()
        dma("sync", G[:], Wt[gname][l].partition_broadcast(128), w=[t_gb])
        dma("sync", Bt[:], Wt[bname][l].partition_broadcast(128), w=[t_gb])
        return G, Bt, t_gb

    def ln_scratch():
        return {"s": A.tile([128, 8], F32, "lns"), "junk": A.tile([128, D], BF16, "junk"), "tj": Tok()}, Tok()


    def run_pipe(n, stages):
        K = len(stages)
        for i in range(n + K - 1):
            for k, f in enumerate(stages):
                j = i - k
                if 0 <= j < n:
                    f(j)

    def ln_hops(xbs, t_xbs, sms, junks, G, Bt, t_gb, NR):
        def h1(tb):
            kb = tb % NR
            sm, t_sm = sms[kb]
            jk, t_jk = junks[tb % 2]
            act(jk[:], xbs[kb][:], AF.Copy, r=[t_xbs[kb]], w=[t_jk, t_sm], scale=1.0 / D, accum_out=sm["s"][:, 2:3])
            act(jk[:], xbs[kb][:], AF.Square, r=[t_xbs[kb]], w=[t_jk, t_sm], scale=float(D ** -0.5), accum_out=sm["s"][:, 1:2])

        def h2(tb):
            sm, t_sm = sms[tb % NR]
            stt(sm["s"][:, 4:5], sm["s"][:, 2:3], sm["s"][:, 2:3], sm["s"][:, 1:2], ALU.mult, ALU.subtract, r=[t_sm], w=[t_sm])

        def h3(tb):
            sm, t_sm = sms[tb % NR]
            act(sm["s"][:, 5:6], sm["s"][:, 4:5], AF.Sqrt, r=[t_sm], w=[t_sm], bias=LN_EPS, scale=-1.0)

        def h4(tb):
            sm, t_sm = sms[tb % NR]
            vop("reciprocal", out=sm["s"][:, 5:6], in_=sm["s"][:, 5:6], r=[t_sm], w=[t_sm])
            stt(sm["s"][:, 6:7], sm["s"][:, 2:3], -1.0, sm["s"][:, 5:6], ALU.mult, ALU.mult, r=[t_sm], w=[t_sm])

        def h5(tb):
            kb = tb % NR
            sm, t_sm = sms[kb]
            act(xbs[kb][:], xbs[kb][:], AF.Identity, r=[t_xbs[kb], t_sm], w=[t_xbs[kb]], scale=sm["s"][:, 5:6], bias=sm["s"][:, 6:7])

        def h6(tb):
            kb = tb % NR
            tt(xbs[kb][:], xbs[kb][:], G[:], ALU.mult, r=[t_xbs[kb], t_gb], w=[t_xbs[kb]], eng="vector")

        def h7(tb):
            kb = tb % NR
            tt(xbs[kb][:], xbs[kb][:], Bt[:], ALU.add, r=[t_xbs[kb], t_gb], w=[t_xbs[kb]], eng="gpsimd")
        return [h1, h2, h3, h4, h5, h6, h7]

    def small_sm():
        return {"s": A.tile([128, 8], F32, "lns")}, Tok()

    class Skew:
        def __init__(self, sk):
            self.sk = sk
            self.q = []

        def push(self, front, back):
            front()
            self.q.append(back)
            while len(self.q) > self.sk:
                self.q.pop(0)()

        def flush(self):
            while self.q:
                self.q.pop(0)()

    def emit_rope(xg, t_x, H, n, ct, st_, t1, t2, ro, t_t1, t_t2, t_ro):
        def v5(tl):
            return tl[:].rearrange("p (h f j i) -> p h f j i", h=H, f=2, j=2)
        for f in range(2):
            cb = ct[:, f, :].unsqueeze(1).unsqueeze(1).to_broadcast([128, H, 2, n])
            sb = st_[:, f, :].unsqueeze(1).to_broadcast([128, H, n])
            tt(v5(t1)[:, :, f], v5(xg)[:, :, f], cb, ALU.mult, r=[t_x, t_const], w=[t_t1], eng="vector")
            tt(v5(t2)[:, :, f, 0, :], v5(xg)[:, :, f, 1, :], sb, ALU.mult, r=[t_x, t_const], w=[t_t2], eng="gpsimd")
            tt(v5(t2)[:, :, f, 1, :], v5(xg)[:, :, f, 0, :], sb, ALU.mult, r=[t_x, t_const], w=[t_t2], eng="gpsimd")

        def v4(tl):
            return tl[:].rearrange("p (hf j i) -> p hf j i", j=2, i=n)
        tt(v4(ro)[:, :, 0, :], v4(t1)[:, :, 0, :], v4(t2)[:, :, 0, :], ALU.subtract, r=[t_t1, t_t2], w=[t_ro], eng="vector")
        tt(v4(ro)[:, :, 1, :], v4(t1)[:, :, 1, :], v4(t2)[:, :, 1, :], ALU.add, r=[t_t1, t_t2], w=[t_ro], eng="gpsimd")

    def stage_init():
        A.reset()
        xb = [A.tile([128, D], F32, "xb") for _ in range(2)]
        t_xb = [Tok(), Tok()]
        xst = [A.tile([128, 8, 512], BF16, "xst") for _ in range(2)]
        t_xst = [Tok(), Tok()]
        for i in range(4):
            k = i % 2
            dma("sync", xb[k][:], mem_in[i * 128:(i + 1) * 128, :], w=[t_xb[k]])
            for c in range(8):
                b = 0 if c < 4 else 1
                tr(bank(b, (c % 4) * 128, (c % 4) * 128 + 128), xb[k][:, c * 128:(c + 1) * 128], r=[t_xb[k]], w=[PT[b]])
            cp(memT[:, 0:4, i * 128:(i + 1) * 128], bank(0).rearrange("p (c t) -> p c t", c=4), r=[PT[0]], w=[t_memT], eng="scalar")
            cp(memT[:, 4:8, i * 128:(i + 1) * 128], bank(1).rearrange("p (c t) -> p c t", c=4), r=[PT[1]], w=[t_memT], eng="vector")
        for tb in range(NB):
            k = tb % 2
            ck, tbl = tb // 4, tb % 4
            dma("sync", xb[k][:], x_in[tb * 128:(tb + 1) * 128, :], w=[t_xb[k]])
            emit_xT(xb[k], t_xb[k], xst[ck % 2], t_xst[ck % 2], tbl, 2 + 2 * k, 3 + 2 * k)
            if tbl == 3:
                dma("sync", XT16v[:, :, ck * 512:(ck + 1) * 512], xst[ck % 2][:], r=[t_xst[ck % 2]], w=[t_XT[ck]])

    def stage_A(l):
        A.reset()
        win = A.tile([128, 8, 1952], BF16, "win")
        t_win = [Tok() for _ in range(8)]
        for c in range(8):
            dma("gpsimd", win[:, c, :], Wt["w_in"][l, c * 128:(c + 1) * 128, :], w=[t_win[c]])
        t_par = Tok()
        G10 = A.tile([128, 640], F32, "G10")
        for h in range(10):
            src = Wt["gqa_q_norm"][l] if h < 8 else Wt["gqa_k_norm"][l]
            dma("sync", G10[:, h * 64:(h + 1) * 64], src.partition_broadcast(128), w=[t_par])
        Gm = A.tile([128, 384], F32, "Gm")
        dma("sync", Gm[:, 0:256], Wt["mla_q_norm"][l].partition_broadcast(128), w=[t_par])
        dma("sync", Gm[:, 256:384], Wt["mla_kv_norm"][l].partition_broadcast(128), w=[t_par])
        wuq = A.tile([128, 2, 384], BF16, "wuq")
        wukv = A.tile([128, 512], BF16, "wukv")
        dma("gpsimd", wuq[:], Wt["mla_w_uq"][l].rearrange("(c p) n -> p c n", p=128), w=[t_par])
        dma("gpsimd", wukv[:], Wt["mla_w_ukv"][l], w=[t_par])

        xt = [A.tile([128, 8, 512], BF16, "xt") for _ in range(2)]
        t_xt = [Tok(), Tok()]
        uts = [A.tile([128, 512], F32, "uts") for _ in range(2)]
        t_uts = [Tok(), Tok()]
        QS = [A.tile([128, 6, 512], BF16, "QS") for _ in range(2)]
        t_QS = [Tok(), Tok()]
        QCS = [A.tile([96, 4, 512], BF16, "QCS") for _ in range(2)]
        t_QCS = [Tok(), Tok()]
        KCS = [A.tile([96, 4, 512], BF16, "KCS") for _ in range(2)]
        t_KCS = [Tok(), Tok()]
        sqt = A.tile([128, 640], F32, "sqt"); t_sqt = Tok()
        ss = A.tile([128, 16], F32, "ss"); t_ss = Tok()
        xg = A.tile([128, 640], F32, "xg"); t_xg = Tok()
        r1 = A.tile([128, 640], F32, "r1"); t_r1 = Tok()
        r2 = A.tile([128, 640], F32, "r2"); t_r2 = Tok()
        ro = A.tile([128, 640], F32, "ro"); t_ro = Tok()
        KK = A.tile([128, 256], F32, "KK"); t_KK = Tok()
        CN = A.tile([128, 384], F32, "CN"); t_CN = Tok()
        cnT = A.tile([128, 3, 128], BF16, "cnT"); t_cnT = Tok()
        RM = A.tile([128, 160], F32, "RM"); t_RM = Tok()
        m1 = A.tile([128, 160], F32, "m1"); t_m1 = Tok()
        m2 = A.tile([128, 160], F32, "m2"); t_m2 = Tok()
        mo = A.tile([128, 160], F32, "mo"); t_mo = Tok()
        QC = A.tile([128, 4, 96], F32, "QC"); t_QC = Tok()
        KC = A.tile([128, 4, 96], F32, "KC"); t_KC = Tok()

        ui = 0
        for ck in range(8):
            k2 = ck % 2
            dma("sync", xt[k2][:], XT16v[:, :, ck * 512:(ck + 1) * 512], r=[t_XT[ck]], w=[t_xt[k2]])
            for cc in range(6):
                for d in range(8):
                    mm(bank(3), win[:, d, 768 + cc * 128:768 + (cc + 1) * 128], xt[k2][:, d, :], d == 0, d == 7,
                       r=[t_win[d], t_xt[k2]], w=[PT[3]])
                u = ui % 2
                ui += 1
                cp(uts[u][:], bank(3), r=[PT[3]], w=[t_uts[u]], eng="scalar")
                dma("sync", UT[cc * 128:(cc + 1) * 128, ck * 512:(ck + 1) * 512], uts[u][:], r=[t_uts[u]], w=[t_UT[ck]])
            for tbl in range(4):
                tb = ck * 4 + tbl
                tb16 = tb % 16
                tc0, tc1 = tbl * 128, (tbl + 1) * 128
                for d in range(8):
                    mm(bank(0), xt[k2][:, d, tc0:tc1], win[:, d, 0:512], d == 0, d == 7, r=[t_win[d], t_xt[k2]], w=[PT[0]])
                for d in range(8):
                    mm(bank(1, 0, 256), xt[k2][:, d, tc0:tc1], win[:, d, 512:768], d == 0, d == 7, r=[t_win[d], t_xt[k2]], w=[PT[1]])
                for d in range(8):
                    mm(bank(2, 0, 416), xt[k2][:, d, tc0:tc1], win[:, d, 1536:1952], d == 0, d == 7, r=[t_win[d], t_xt[k2]], w=[PT[2]])
                if CUT <= 1:
                    continue
                qk = ps[:, 0:640]
                act(sqt[:], qk, AF.Square, r=[PT[0], PT[1]], w=[t_sqt])
                vop("tensor_reduce", out=ss[:, 0:10], in_=sqt[:].rearrange("p (h d) -> p h d", d=64), axis=AX.X, op=ALU.add,
                    r=[t_sqt], w=[t_ss])
                act(ss[:, 0:10], ss[:, 0:10], AF.Sqrt, r=[t_ss], w=[t_ss], scale=1.0 / 64, bias=RMS_EPS)
                vop("reciprocal", out=ss[:, 0:10], in_=ss[:, 0:10], r=[t_ss], w=[t_ss])
                tt(xg[:].rearrange("p (h d) -> p h d", d=64), qk.rearrange("p (h d) -> p h d", d=64),
                   ss[:, 0:10].unsqueeze(2).to_broadcast([128, 10, 64]), ALU.mult, r=[PT[0], PT[1], t_ss], w=[t_xg])
                tt(xg[:], xg[:], G10[:], ALU.mult, r=[t_xg, t_par], w=[t_xg], eng="gpsimd")
                if CUT <= 2:
                    continue
                emit_rope(xg, t_xg, 10, 16, CQ[:, tb16, :].rearrange("p (f i) -> p f i", f=2),
                          SQ[:, tb16, :].rearrange("p (f i) -> p f i", f=2), r1, r2, ro, t_r1, t_r2, t_ro)
                if CUT <= 3:
                    continue
                for rr in range(2):
                    cp(KK[:].rearrange("p (h r d) -> p h r d", h=2, r=2)[:, :, rr, :],
                       ro[:, 512:640].rearrange("p (h d) -> p h d", h=2), r=[t_ro], w=[t_KK], eng="gpsimd")
                for j in range(6):
                    src = ro[:, j * 128:(j + 1) * 128] if j < 4 else KK[:, (j - 4) * 128:(j - 3) * 128]
                    b = 5 if j < 3 else 6
                    tr(bank(b, (j % 3) * 128, (j % 3) * 128 + 128), src, r=[t_ro, t_KK], w=[PT[b]])
                cp(QS[k2][:, 0:3, tc0:tc1], bank(5, 0, 384).rearrange("p (c t) -> p c t", c=3), r=[PT[5]], w=[t_QS[k2]], eng="scalar")
                cp(QS[k2][:, 3:6, tc0:tc1], bank(6, 0, 384).rearrange("p (c t) -> p c t", c=3), r=[PT[6]], w=[t_QS[k2]], eng="vector")
                if CUT <= 4:
                    continue
                cp(VA[:, tb, :, 0:64], bank(1, 128, 256).rearrange("p (h d) -> p h d", h=2), r=[PT[1]], w=[t_VA[tb]], eng="scalar")
                act(sqt[:, 0:384], bank(2, 0, 384), AF.Square, r=[PT[2]], w=[t_sqt])
                vop("tensor_reduce", out=ss[:, 10:11], in_=sqt[:, 0:256], axis=AX.X, op=ALU.add, r=[t_sqt], w=[t_ss])
                vop("tensor_reduce", out=ss[:, 11:12], in_=sqt[:, 256:384], axis=AX.X, op=ALU.add, r=[t_sqt], w=[t_ss])
                act(ss[:, 10:11], ss[:, 10:11], AF.Sqrt, r=[t_ss], w=[t_ss], scale=1.0 / 256, bias=RMS_EPS)
                act(ss[:, 11:12], ss[:, 11:12], AF.Sqrt, r=[t_ss], w=[t_ss], scale=1.0 / 128, bias=RMS_EPS)
                vop("reciprocal", out=ss[:, 10:12], in_=ss[:, 10:12], r=[t_ss], w=[t_ss])
                ts(CN[:, 0:256], bank(2, 0, 256), ss[:, 10:11], None, ALU.mult, r=[PT[2], t_ss], w=[t_CN])
                ts(CN[:, 256:384], bank(2, 256, 384), ss[:, 11:12], None, ALU.mult, r=[PT[2], t_ss], w=[t_CN])
                tt(CN[:], CN[:], Gm[:], ALU.mult, r=[t_CN, t_par], w=[t_CN], eng="gpsimd")
                for j in range(3):
                    tr(bank(7, j * 128, (j + 1) * 128), CN[:, j * 128:(j + 1) * 128], r=[t_CN], w=[PT[7]])
                cp(cnT[:], bank(7, 0, 384).rearrange("p (c t) -> p c t", c=3), r=[PT[7]], w=[t_cnT], eng="scalar")
                cp(RM[:, 128:160], bank(2, 384, 416), r=[PT[2]], w=[t_RM], eng="vector")
                if CUT <= 5:
                    continue
                for kc in range(2):
                    mm(bank(0, 0, 384), cnT[:, kc, :], wuq[:, kc, :], kc == 0, kc == 1, r=[t_cnT, t_par], w=[PT[0]])
                mm(bank(4), cnT[:, 2, :], wukv[:], True, True, r=[t_cnT, t_par], w=[PT[4]])
                if SUB <= 1:
                    continue
                qcv = bank(0, 0, 384).rearrange("p (h e) -> p h e", h=4)
                kvv = bank(4).rearrange("p (h e) -> p h e", h=4)
                cp(RM[:, 0:128].rearrange("p (h e) -> p h e", h=4), qcv[:, :, 64:96], r=[PT[0]], w=[t_RM], eng="vector")
                cp(QC[:, :, 0:64], qcv[:, :, 0:64], r=[PT[0]], w=[t_QC], eng="vector")
                if SUB <= 2:
                    continue
                cp(KC[:, :, 0:64], kvv[:, :, 0:64], r=[PT[4]], w=[t_KC], eng="vector")
                cp(VC[:, tb, :, 0:64], kvv[:, :, 64:128], r=[PT[4]], w=[t_VC[tb]], eng="scalar")
                if CUT <= 6:
                    continue
                emit_rope(RM, t_RM, 5, 8, CM[:, tb16, :].rearrange("p (f i) -> p f i", f=2),
                          SM[:, tb16, :].rearrange("p (f i) -> p f i", f=2), m1, m2, mo, t_m1, t_m2, t_mo)
                cp(QC[:, :, 64:96], mo[:, 0:128].rearrange("p (h e) -> p h e", h=4), r=[t_mo], w=[t_QC], eng="gpsimd")
                cp(KC[:, :, 64:96], mo[:, 128:160].unsqueeze(1).to_broadcast([128, 4, 32]), r=[t_mo], w=[t_KC], eng="gpsimd")
                if CUT <= 7:
                    continue
                for h in range(4):
                    tr(bank(1, h * 128, (h + 1) * 128, 0, 96), QC[:, h, :], r=[t_QC], w=[PT[1]])
                for h in range(4):
                    tr(bank(2, h * 128, (h + 1) * 128, 0, 96), KC[:, h, :], r=[t_KC], w=[PT[2]])
                cp(QCS[k2][:, :, tc0:tc1], bank(1, 0, 512, 0, 96).rearrange("p (c t) -> p c t", c=4), r=[PT[1]], w=[t_QCS[k2]], eng="scalar")
                cp(KCS[k2][:, :, tc0:tc1], bank(2, 0, 512, 0, 96).rearrange("p (c t) -> p c t", c=4), r=[PT[2]], w=[t_KCS[k2]], eng="vector")
            c0, c1 = ck * 512, (ck + 1) * 512
            dma("sync", QTd[:, :, c0:c1].rearrange("j p t -> p j t"), QS[k2][:], r=[t_QS[k2]], w=[t_QT[ck]])
            dma("sync", QCTd[:, :, c0:c1].rearrange("j p t -> p j t"), QCS[k2][:], r=[t_QCS[k2]], w=[t_QCT[ck]])
            dma("sync", KCTd[:, :, c0:c1].rearrange("j p t -> p j t"), KCS[k2][:], r=[t_KCS[k2]], w=[t_KCT[ck]])

    def stage_ATT():
        A.reset()
        qt = [A.tile([128, L], BF16, "qt") for _ in range(2)]
        kt = [A.tile([128, L], BF16, "kt") for _ in range(2)]
        t_qk = [Tok(), Tok()]
        NE = 6
        et = [A.tile([128, 512], BF16, "et") for _ in range(NE)]
        t_et = [Tok() for _ in range(NE)]
        rec = [A.tile([128, 512], F32, "rec") for _ in range(2)]
        t_rec = [Tok(), Tok()]
        bcs = [A.tile([64, 512], F32, "bcs") for _ in range(2)]
        t_bcs = [Tok(), Tok()]
        ot = [A.tile([64, 512], BF16, "ot") for _ in range(2)]
        t_ot = [Tok(), Tok()]
        items = []
        cnt = {"l": 0, "g": 0}

        def add_head(seq, li, pr0, pr1, vtile, t_v, hv, scale, arow, pre):
            qtile, ktile, t_in = qt[li], kt[li], t_qk[li]
            for qb in range(4):
                g = cnt["g"]
                cnt["g"] += 1
                ob = 4 + g % 2
                gi = g % 2
                for kb in range(16):
                    idx = len(items)
                    sb = idx % 4
                    ei = idx % NE

                    def qk(sb=sb, ei=ei, kb=kb, qb=qb):
                        mm(bank(sb), ktile[pr0:pr1, kb * 128:(kb + 1) * 128], qtile[pr0:pr1, qb * 512:(qb + 1) * 512], True, True,
                           r=[t_in], w=[PT[sb]])
                        act(et[ei][:], bank(sb), AF.Exp, r=[PT[sb]], w=[t_et[ei]], scale=scale)

                    def pv(ei=ei, kb=kb, ob=ob):
                        mm(bank(ob, 0, 512, 0, 65), vtile[:, seq * 16 + kb, hv, :], et[ei][:], kb == 0, kb == 15,
                           r=[t_et[ei], t_v[seq * 16 + kb]], w=[PT[ob]])

                    n1 = n2 = None
                    if kb == 15:
                        def n1(ob=ob, gi=gi):
                            vop("reciprocal", out=rec[gi][64:65, :], in_=bank(ob, 0, 512, 64, 65), r=[PT[ob]], w=[t_rec[gi]])

                        def n2(ob=ob, gi=gi, qb=qb):
                            mm(bank(6 + gi, 0, 512, 0, 64), ones_f[64:65, 0:64], rec[gi][64:65, :], True, True,
                               r=[t_rec[gi], t_const], w=[PT[6 + gi]])
                            cp(bcs[gi][:], bank(6 + gi, 0, 512, 0, 64), r=[PT[6 + gi]], w=[t_bcs[gi]], eng="vector")
                            tt(ot[gi][:], bank(ob, 0, 512, 0, 64), bcs[gi][:], ALU.mult, r=[PT[ob], t_bcs[gi]], w=[t_ot[gi]])
                            c0 = seq * L + qb * 512
                            dma("sync", AT[arow:arow + 64, c0:c0 + 512], ot[gi][:], r=[t_ot[gi]], w=[tAT((arow // 128, c0 // 512))])
                    items.append((pre if (qb == 0 and kb == 0) else None, qk, pv, n1, n2))

        for seq in range(2):
            rq = [t_QT[seq * 4 + i] for i in range(4)]
            for j in range(4):
                li = cnt["l"] % 2
                cnt["l"] += 1

                def pre(li=li, j=j, seq=seq, rq=rq):
                    dma("sync", qt[li][:], QTd[j, :, seq * L:(seq + 1) * L], r=rq, w=[t_qk[li]])
                    dma("sync", kt[li][:], QTd[4 + j // 2, :, seq * L:(seq + 1) * L], r=rq, w=[t_qk[li]])
                for hh in range(2):
                    add_head(seq, li, hh * 64, hh * 64 + 64, VA, t_VA, j // 2, 0.125, (2 * j + hh) * 64, pre if hh == 0 else None)
            rq2 = [t_QCT[seq * 4 + i] for i in range(4)] + [t_KCT[seq * 4 + i] for i in range(4)]
            for h in range(4):
                li = cnt["l"] % 2
                cnt["l"] += 1

                def pre(li=li, h=h, seq=seq, rq2=rq2):
                    dma("sync", qt[li][0:96, :], QCTd[h, :, seq * L:(seq + 1) * L], r=rq2, w=[t_qk[li]])
                    dma("sync", kt[li][0:96, :], KCTd[h, :, seq * L:(seq + 1) * L], r=rq2, w=[t_qk[li]])
                add_head(seq, li, 0, 96, VC, t_VC, h, float(96 ** -0.5), 768 + h * 64, pre)
        D1, D2 = 2, 5
        n = len(items)
        for i in range(n + D2 + 1):
            if i < n:
                if items[i][0] is not None:
                    items[i][0]()
                items[i][1]()
            if 0 <= i - D1 < n:
                items[i - D1][2]()
                if items[i - D1][3] is not None:
                    items[i - D1][3]()
            if 0 <= i - D2 < n and items[i - D2][4] is not None:
                items[i - D2][4]()

    def stage_WO(l, xsrc, have_hyena):
        A.reset()
        wo = A.tile([128, 8, D], BF16, "wo")
        t_wo = [Tok() for _ in range(8)]
        for c in range(8):
            dma("gpsimd", wo[:, c, :], Wt["w_o"][l, c * 128:(c + 1) * 128, :], w=[t_wo[c]])
        G, Bt, t_gb = load_gb("ln1_g", "ln1_b", l)
        at = [A.tile([128, 8, 512], BF16, "at") for _ in range(2)]
        t_at = [Tok(), Tok()]
        xst = [A.tile([128, 8, 512], BF16, "xst") for _ in range(2)]
        t_xst = [Tok(), Tok()]
        if not have_hyena:
            zt_ = A.tile([128, 512], BF16, "zero")
            t_z = Tok()
            P.op("vector", lambda e: e.memset(zt_[:], 0.0), w=[t_z])
            for c in (4, 5):
                for ck in range(8):
                    dma("sync", AT[c * 128:(c + 1) * 128, ck * 512:(ck + 1) * 512], zt_[:], r=[t_z], w=[tAT((c, ck))])
        NR = 12
        sms = [small_sm() for _ in range(NR)]
        junks = [(A.tile([128, D], BF16, "junk"), Tok()) for _ in range(2)]
        xbs = [A.tile([128, D], F32, "xbr") for _ in range(NR)]
        t_xbs = [Tok() for _ in range(NR)]

        def s0(tb):
            ck, tbl = tb // 4, tb % 4
            k2 = ck % 2
            kb = tb % NR
            yb_ = 2 * (tb % 2)
            if tbl == 0:
                dma("sync", at[k2][:], ATv[:, :, ck * 512:(ck + 1) * 512], r=[tAT((c, ck)) for c in range(8)], w=[t_at[k2]])
            dma("sync", xbs[kb][:], xsrc[tb * 128:(tb + 1) * 128, :], r=[t_X32[tb]], w=[t_xbs[kb]])
            for half in range(2):
                for c in range(8):
                    mm(bank(yb_ + half), at[k2][:, c, tbl * 128:(tbl + 1) * 128], wo[:, c, half * 512:(half + 1) * 512],
                       c == 0, c == 7, r=[t_at[k2], t_wo[c]], w=[PT[yb_ + half]])
            stt(xbs[kb][:], xbs[kb][:], ALPHA, ps[:, yb_ * 512:(yb_ + 2) * 512], ALU.mult, ALU.add,
                r=[t_xbs[kb], PT[yb_], PT[yb_ + 1]], w=[t_xbs[kb]])

        def s8(tb):
            ck, tbl = tb // 4, tb % 4
            k2 = ck % 2
            kb = tb % NR
            dma("sync", X32[tb * 128:(tb + 1) * 128, :], xbs[kb][:], r=[t_xbs[kb]], w=[t_X32[tb]])
            tbk = 4 + 2 * (tb % 2)
            emit_xT(xbs[kb], t_xbs[kb], xst[k2], t_xst[k2], tbl, tbk, tbk + 1)

        def s9(tb):
            ck, tbl = tb // 4, tb % 4
            k2 = ck % 2
            if tbl == 3:
                dma("sync", XT16v[:, :, ck * 512:(ck + 1) * 512], xst[k2][:], r=[t_xst[k2]], w=[t_XT[ck]])
        run_pipe(NB, [s0] + ln_hops(xbs, t_xbs, sms, junks, G, Bt, t_gb, NR) + [s8, s9])

    def stage_final():
        for tb in range(NB):
            dma("sync", out[tb * 128:(tb + 1) * 128, :], X32[tb * 128:(tb + 1) * 128, :], r=[t_X32[tb]])

    def stage_XA(l):
        A.reset()
        wkv = A.tile([128, 8, 2048], BF16, "wkv")
        wq = A.tile([128, 8, D], BF16, "wq")
        wo = A.tile([128, 8, D], BF16, "wo")
        t_wkv = [Tok() for _ in range(8)]
        t_wq = [Tok() for _ in range(8)]
        t_wo = [Tok() for _ in range(8)]
        for c in range(8):
            dma("gpsimd", wkv[:, c, :], Wt["xa_wkv"][l, c * 128:(c + 1) * 128, :], w=[t_wkv[c]])
        for c in range(8):
            dma("gpsimd", wq[:, c, :], Wt["xa_wq"][l, c * 128:(c + 1) * 128, :], w=[t_wq[c]])
        for c in range(8):
            dma("gpsimd", wo[:, c, :], Wt["xa_wo"][l, c * 128:(c + 1) * 128, :], w=[t_wo[c]])
        G, Bt, t_gb = load_gb("ln2_g", "ln2_b", l)
        KmT = A.tile([128, 8, 512], BF16, "KmT"); t_Km = Tok()
        Vm = A.tile([128, 4, D], BF16, "Vm"); t_Vm = Tok()
        xt = A.tile([128, 8, 512], BF16, "xt"); t_xt = Tok()
        QxT = A.tile([128, 8, 512], BF16, "QxT"); t_Qx = Tok()
        axT = A.tile([128, 8, 512], BF16, "axT"); t_ax = Tok()
        et = [A.tile([128, 512], BF16, "et") for _ in range(4)]
        t_et = [Tok() for _ in range(4)]
        rden = A.tile([128, 512], F32, "rden"); t_rden = Tok()
        xst = [A.tile([128, 8, 512], BF16, "xst") for _ in range(2)]
        t_xst = [Tok(), Tok()]
        for c in range(8):
            b = c % 2
            for d in range(8):
                mm(bank(b), wkv[:, d, c * 128:(c + 1) * 128], memT[:, d, :], d == 0, d == 7, r=[t_wkv[d], t_memT], w=[PT[b]])
            cp(KmT[:, c, :], bank(b), r=[PT[b]], w=[t_Km], eng="scalar" if b == 0 else "vector")
        for sb in range(4):
            for half in range(2):
                b = 2 + half
                for d in range(8):
                    mm(bank(b), memT[:, d, sb * 128:(sb + 1) * 128], wkv[:, d, 1024 + half * 512:1024 + (half + 1) * 512],
                       d == 0, d == 7, r=[t_wkv[d], t_memT], w=[PT[b]])
                cp(Vm[:, sb, half * 512:(half + 1) * 512], bank(b), r=[PT[b]], w=[t_Vm], eng="scalar" if half == 0 else "vector")
        NR = 3
        sms = [ln_scratch() for _ in range(NR)]
        xbs = [A.tile([128, D], F32, "xbr") for _ in range(NR)]
        t_xbs = [Tok() for _ in range(NR)]
        zbs = [A.tile([128, D], F32, "zbr") for _ in range(2)]
        t_zbs = [Tok(), Tok()]
        skb = Skew(2)
        ei = [0]
        for ck in range(8):
            seq = ck // 4
            k2 = ck % 2
            dma("sync", xt[:], XT16v[:, :, ck * 512:(ck + 1) * 512], r=[t_XT[ck]], w=[t_xt])
            for c in range(8):
                b = c % 2
                for d in range(8):
                    mm(bank(b), wq[:, d, c * 128:(c + 1) * 128], xt[:, d, :], d == 0, d == 7, r=[t_wq[d], t_xt], w=[PT[b]])
                cp(QxT[:, c, :], bank(b), r=[PT[b]], w=[t_Qx], eng="scalar" if b == 0 else "vector")
            skh = Skew(1)
            for h in range(4):
                es = []
                for kb in range(2):
                    es.append(ei[0] % 4)
                    ei[0] += 1

                def hfront(h=h, es=es, seq=seq):
                    for kb in range(2):
                        e_i = es[kb]
                        for cc in range(2):
                            mm(bank(2 + kb), KmT[:, 2 * h + cc, seq * 256 + kb * 128:seq * 256 + (kb + 1) * 128], QxT[:, 2 * h + cc, :],
                               cc == 0, cc == 1, r=[t_Km, t_Qx], w=[PT[2 + kb]])
                        act(et[e_i][:], bank(2 + kb), AF.Exp, r=[PT[2 + kb]], w=[t_et[e_i]], scale=1.0 / 16)

                def hback(h=h, es=es, seq=seq):
                    for kb in range(2):
                        mm(bank(4), ones_bf[:], et[es[kb]][:], kb == 0, kb == 1, r=[t_et[es[kb]], t_const], w=[PT[4]])
                    vop("reciprocal", out=rden[:], in_=bank(4), r=[PT[4]], w=[t_rden])
                    for dc in range(2):
                        for kb in range(2):
                            mm(bank(5 + dc), Vm[:, seq * 2 + kb, h * 256 + dc * 128:h * 256 + (dc + 1) * 128], et[es[kb]][:],
                               kb == 0, kb == 1, r=[t_et[es[kb]], t_Vm], w=[PT[5 + dc]])
                        tt(axT[:, 2 * h + dc, :], bank(5 + dc), rden[:], ALU.mult, r=[PT[5 + dc], t_rden], w=[t_ax])
                skh.push(hfront, hback)
            skh.flush()
            for tbl in range(4):
                tb = ck * 4 + tbl

                def front(tb=tb, tbl=tbl):
                    kb2 = tb % NR
                    z2 = tb % 2
                    dma("sync", xbs[kb2][:], X32[tb * 128:(tb + 1) * 128, :], r=[t_X32[tb]], w=[t_xbs[kb2]])
                    for half in range(2):
                        for c in range(8):
                            mm(bank(half), axT[:, c, tbl * 128:(tbl + 1) * 128], wo[:, c, half * 512:(half + 1) * 512],
                               c == 0, c == 7, r=[t_ax, t_wo[c]], w=[PT[half]])
                    stt(xbs[kb2][:], xbs[kb2][:], ALPHA, ps[:, 0:1024], ALU.mult, ALU.add, r=[t_xbs[kb2], PT[0], PT[1]], w=[t_xbs[kb2]])
                    emit_ln(xbs[kb2], t_xbs[kb2], G, Bt, t_gb, sms[kb2][0], sms[kb2][1])
                    dma("sync", X32[tb * 128:(tb + 1) * 128, :], xbs[kb2][:], r=[t_xbs[kb2]], w=[t_X32[tb]])
                    act(zbs[z2][:], xbs[kb2][:], AF.Copy, r=[t_xbs[kb2]], w=[t_zbs[z2]], scale=ALPHA)
                    dma("sync", Z32[tb * 128:(tb + 1) * 128, :], zbs[z2][:], r=[t_zbs[z2]], w=[t_Z32[tb]])

                def back(tb=tb, tbl=tbl, k2=k2, ck=ck):
                    kb2 = tb % NR
                    emit_xT(xbs[kb2], t_xbs[kb2], xst[k2], t_xst[k2], tbl, 7, 6)
                    if tbl == 3:
                        dma("sync", XT16v[:, :, ck * 512:(ck + 1) * 512], xst[k2][:], r=[t_xst[k2]], w=[t_XT[ck]])
                skb.push(front, back)
        skb.flush()

    def stage_MOE(l):
        A.reset()
        wr = A.tile([128, 8, 16], BF16, "wr"); t_wr = Tok()
        dma("gpsimd", wr[:], Wt["moe_router"][l].rearrange("(c p) e -> p c e", p=128), w=[t_wr])
        o48 = A.tile([48, 1], F32, "o48")
        dma("sync", o48[:], Cn["o48"], w=[t_wr])
        xt = A.tile([128, 8, 512], BF16, "xt"); t_xt = Tok()
        AF48 = A.tile([128, 16, 48], F32, "AF48"); t_AF = Tok()
        P.op("vector", lambda e: e.memset(AF48[:], 0.0), w=[t_AF])
        ex = A.tile([128, 16], F32, "ex"); t_ex = Tok()
        sx = A.tile([128, 2], F32, "sx"); t_sx = Tok()
        AffT = A.tile([48, L], F32, "AffT"); t_AffT = Tok()
        MX = A.tile([48, 256], F32, "MX"); t_MX = Tok()
        IX = A.tile([48, 256], U32, "IX"); t_IX = Tok()
        IXF = A.tile([48, 256], F32, "IXF"); t_IXF = Tok()
        IDXT = A.tile([128, 2, 48], I32, "IDXT"); t_IDXT = Tok()
        GT = A.tile([128, 2, 48], F32, "GT"); t_GT = Tok()
        for ck in range(8):
            dma("sync", xt[:], XT16v[:, :, ck * 512:(ck + 1) * 512], w=[t_xt])
            for tbl in range(4):
                tb = ck * 4 + tbl
                seq, tbs = tb // 16, tb % 16
                for d in range(8):
                    mm(bank(0, 0, 16), xt[:, d, tbl * 128:(tbl + 1) * 128], wr[:, d, :], d == 0, d == 7, r=[t_xt, t_wr], w=[PT[0]])
                act(ex[:], bank(0, 0, 16), AF.Exp, r=[PT[0]], w=[t_ex, t_sx], accum_out=sx[:, 0:1])
                vop("reciprocal", out=sx[:, 1:2], in_=sx[:, 0:1], r=[t_sx], w=[t_sx])
                ts(AF48[:, tbs, seq * 32:seq * 32 + 16], ex[:], sx[:, 1:2], None, ALU.mult, r=[t_ex, t_sx], w=[t_AF])
        for tbs in range(16):
            b = 1 + tbs % 2
            tr(bank(b, 0, 128, 0, 48), AF48[:, tbs, :], r=[t_AF], w=[PT[b]])
            cp(AffT[:, tbs * 128:(tbs + 1) * 128], bank(b, 0, 128, 0, 48), r=[PT[b]], w=[t_AffT], eng="vector")
        for rd in range(32):
            sl = slice(rd * 8, rd * 8 + 8)
            vop("max", out=MX[:, sl], in_=AffT[:], r=[t_AffT], w=[t_MX])
            vop("max_index", out=IX[:, sl], in_max=MX[:, sl], in_values=AffT[:], r=[t_AffT, t_MX], w=[t_IX])
            vop("match_replace", out=AffT[:], in_to_replace=MX[:, sl], in_values=AffT[:], imm_value=-1.0, r=[t_MX, t_IX], w=[t_AffT])
        cp(IXF[:], IX[:], r=[t_IX], w=[t_IXF], eng="vector")
        ts(IXF[:], IXF[:], o48[:, 0:1], None, ALU.add, r=[t_IXF, t_wr], w=[t_IXF])
        for half in range(2):
            tr(bank(3, 0, 48), IXF[:, half * 128:(half + 1) * 128], r=[t_IXF], w=[PT[3]])
            cp(IDXT[:, half, :], bank(3, 0, 48), r=[PT[3]], w=[t_IDXT], eng="vector")
            tr(bank(4, 0, 48), MX[:, half * 128:(half + 1) * 128], r=[t_MX], w=[PT[4]])
            cp(GT[:, half, :], bank(4, 0, 48), r=[PT[4]], w=[t_GT], eng="vector")
        wg = [A.tile([128, 8, 512], BF16, "wg") for _ in range(2)]
        wu = [A.tile([128, 8, 512], BF16, "wu") for _ in range(2)]
        wd = [A.tile([128, 4, D], BF16, "wd") for _ in range(2)]
        t_w = [Tok(), Tok()]
        xg = [A.tile([128, D], F32, "xg") for _ in range(2)]
        t_xg = [Tok(), Tok()]
        xeT = A.tile([128, 8, 512], BF16, "xeT"); t_xe = Tok()
        sg = [A.tile([128, 512], F32, "sg") for _ in range(2)]
        t_sg = [Tok(), Tok()]
        hidT = A.tile([128, 4, 512], BF16, "hidT"); t_hid = Tok()
        yb = [A.tile([128, D], F32, "yb") for _ in range(2)]
        t_yb = [Tok(), Tok()]

        def load_w(e):
            k = e % 2
            dma("gpsimd", wg[k][:], Wt["moe_w_gate"][l, e].rearrange("(c p) f -> p c f", p=128), w=[t_w[k]])
            dma("gpsimd", wu[k][:], Wt["moe_w_up"][l, e].rearrange("(c p) f -> p c f", p=128), w=[t_w[k]])
            dma("gpsimd", wd[k][:], Wt["moe_w_down"][l, e].rearrange("(c p) n -> p c n", p=128), w=[t_w[k]])

        load_w(0)
        gi = 0
        for e in range(16):
            k = e % 2
            for sh in range(4):
                seq, half = sh // 2, sh % 2
                g2 = gi % 2
                gi += 1
                col = seq * 32 + e
                P.op("gpsimd", lambda eng, g2=g2, half=half, col=col: eng.indirect_dma_start(
                    out=xg[g2][:], out_offset=None, in_=X32,
                    in_offset=bass.IndirectOffsetOnAxis(ap=IDXT[:, half, col:col + 1], axis=0),
                    bounds_check=bcr(eng), oob_is_err=False), r=[t_IDXT], w=[t_xg[g2]], dma=True)
                emit_xT(xg[g2], t_xg[g2], xeT, t_xe, sh, 5 + 2 * 0, 6)
            if e + 1 < 16:
                load_w(e + 1)
            for fc in range(4):
                s2 = fc % 2
                for d in range(8):
                    mm(bank(1), wg[k][:, d, fc * 128:(fc + 1) * 128], xeT[:, d, :], d == 0, d == 7, r=[t_w[k], t_xe], w=[PT[1]])
                for d in range(8):
                    mm(bank(2), wu[k][:, d, fc * 128:(fc + 1) * 128], xeT[:, d, :], d == 0, d == 7, r=[t_w[k], t_xe], w=[PT[2]])
                act(sg[s2][:], bank(1), AF.Silu, r=[PT[1]], w=[t_sg[s2]])
                tt(hidT[:, fc, :], sg[s2][:], bank(2), ALU.mult, r=[t_sg[s2], PT[2]], w=[t_hid])
            for sh in range(4):
                seq, half = sh // 2, sh % 2
                y2 = sh % 2
                col = seq * 32 + e
                for h2 in range(2):
                    for fc in range(4):
                        mm(bank(3 + h2), hidT[:, fc, sh * 128:(sh + 1) * 128], wd[k][:, fc, h2 * 512:(h2 + 1) * 512],
                           fc == 0, fc == 3, r=[t_hid, t_w[k]], w=[PT[3 + h2]])
                ts(yb[y2][:], ps[:, 3 * 512:5 * 512], GT[:, half, col:col + 1], None, ALU.mult, r=[PT[3], PT[4], t_GT], w=[t_yb[y2]])
                P.op("gpsimd", lambda eng, y2=y2, half=half, col=col: eng.indirect_dma_start(
                    out=Z32, out_offset=bass.IndirectOffsetOnAxis(ap=IDXT[:, half, col:col + 1], axis=0),
                    in_=yb[y2][:], in_offset=None, compute_op=ALU.add,
                    bounds_check=bcr(eng), oob_is_err=False), r=[t_IDXT, t_yb[y2]], w=[t_Zs[seq]], dma=True)

    def stage_LN3(l, dst):
        A.reset()
        G, Bt, t_gb = load_gb("ln3_g", "ln3_b", l)
        NR = 12
        sms = [small_sm() for _ in range(NR)]
        junks = [(A.tile([128, D], BF16, "junk"), Tok()) for _ in range(2)]
        xbs = [A.tile([128, D], F32, "xbr") for _ in range(NR)]
        t_xbs = [Tok() for _ in range(NR)]
        xst = [A.tile([128, 8, 512], BF16, "xst") for _ in range(2)]
        t_xst = [Tok(), Tok()]

        def s0(tb):
            kb = tb % NR
            dma("sync", xbs[kb][:], Z32[tb * 128:(tb + 1) * 128, :], w=[t_xbs[kb]])

        def s8(tb):
            ck, tbl = tb // 4, tb % 4
            k2 = ck % 2
            kb = tb % NR
            dma("sync", dst[tb * 128:(tb + 1) * 128, :], xbs[kb][:], r=[t_xbs[kb]], w=[t_X32[tb]])
            bb = 2 * (tb % 4)
            emit_xT(xbs[kb], t_xbs[kb], xst[k2], t_xst[k2], tbl, bb, bb + 1)

        def s9(tb):
            ck, tbl = tb // 4, tb % 4
            k2 = ck % 2
            if tbl == 3:
                dma("sync", XT16v[:, :, ck * 512:(ck + 1) * 512], xst[k2][:], r=[t_xst[k2]], w=[t_XT[ck]])
        run_pipe(NB, [s0] + ln_hops(xbs, t_xbs, sms, junks, G, Bt, t_gb, NR) + [s8, s9])

    def stage_HY(l):
        A.reset()
        HS = A.tile([128, 16, 256], BF16, "HS"); t_HS = Tok()
        HD = A.tile([128, 16, 256], BF16, "HD"); t_HD = Tok()
        x0T = A.tile([128, 2, T], BF16, "x0T"); t_x0 = Tok()
        zT = A.tile([128, 2, T], BF16, "zT"); t_zT = Tok()
        z_tm = A.tile([128, 16, 512], BF16, "z_tm"); t_ztm = Tok()
        hbias = A.tile([128, 2], F32, "hbias"); t_hb = Tok()
        for j in range(2):
            dma("sync", hbias[:, j:j + 1], Wt["hy_bias"][l, j * 128:(j + 1) * 128].rearrange("(p o) -> p o", o=1), w=[t_hb])
        sub_mark = A.off
        featsT = A.tile([33, L], F32, "featsT"); t_f = Tok()
        dma("sync", featsT[:], Cn["featsT"], w=[t_f])
        negt = A.tile([128, 16], F32, "negt"); m0 = A.tile([128, 16], F32, "m0")
        dma("sync", negt[:], Cn["negt"], w=[t_f])
        dma("sync", m0[:], Cn["m0"], w=[t_f])
        w1 = A.tile([33, 64], F32, "w1"); w2 = A.tile([64, 64], F32, "w2"); w3 = A.tile([64, 64], F32, "w3")
        wout = A.tile([64, 512], F32, "wout")
        dma("sync", w1[:], Wt["hy_w1"][l], w=[t_f])
        dma("sync", w2[:], Wt["hy_w2"][l], w=[t_f])
        dma("sync", w3[:], Wt["hy_w3"][l], w=[t_f])
        dma("sync", wout[:], Wt["hy_wout"][l], w=[t_f])
        prm = A.tile([64, 8], F32, "prm"); t_prm = Tok()
        for i, nm in enumerate(("hy_b1", "hy_b2", "hy_b3", "hy_freq")):
            dma("sync", prm[:, i:i + 1], Wt[nm][l].rearrange("(p o) -> p o", o=1), w=[t_prm])
        ts(prm[:, 4:7], prm[:, 0:3], prm[:, 3:4], None, ALU.mult, r=[t_prm], w=[t_prm])
        AD = A.tile([128, 512], F32, "AD"); t_AD = Tok()
        dma("sync", AD[:], Wt["hy_decay"][l].rearrange("a c -> (a c)").partition_broadcast(128), w=[t_AD])
        act(AD[:], AD[:], AF.Abs, r=[t_AD], w=[t_AD])
        hA = A.tile([64, L], F32, "hA"); hB = A.tile([64, L], F32, "hB")
        t_hA, t_hB = Tok(), Tok()
        a1 = [A.tile([64, 512], F32, "a1") for _ in range(2)]
        a2 = [A.tile([64, 512], F32, "a2") for _ in range(2)]
        t_a = [Tok(), Tok()]
        chain = [(featsT, t_f, 33, w1, hA, t_hA), (hA, t_hA, 64, w2, hB, t_hB), (hB, t_hB, 64, w3, hA, t_hA)]
        for i, (src, t_src, kk, wi, dst, t_dst) in enumerate(chain):
            for nq in range(4):
                b = nq % 2
                mm(bank(b, 0, 512, 0, 64), wi[0:kk, :], src[0:kk, nq * 512:(nq + 1) * 512], True, True, r=[t_f, t_src], w=[PT[b]])
                ts(a1[b][:], bank(b, 0, 512, 0, 64), prm[:, 3:4], prm[:, 4 + i:5 + i], ALU.mult, ALU.add, r=[PT[b], t_prm], w=[t_a[b]])
                ts(a2[b][:], a1[b][:], 1.0 / TWO_PI, MAGIC, ALU.mult, ALU.add, r=[t_a[b]], w=[t_a[b]])
                ts(a2[b][:], a2[b][:], MAGIC, -TWO_PI, ALU.subtract, ALU.mult, r=[t_a[b]], w=[t_a[b]])
                tt(a1[b][:], a1[b][:], a2[b][:], ALU.add, r=[t_a[b]], w=[t_a[b]])
                ts(a1[b][:], a1[b][:], 3.1415925, -3.1415925, ALU.min, ALU.max, r=[t_a[b]], w=[t_a[b]])
                act(dst[:, nq * 512:(nq + 1) * 512], a1[b][:], AF.Sin, r=[t_a[b]], w=[t_dst])
        h3, t_h3 = hA, t_hA
        E = [A.tile([128, 512], F32, "E") for _ in range(2)]
        fl = [A.tile([128, 512], F32, "fl") for _ in range(2)]
        t_E = [Tok(), Tok()]
        t_fl = [Tok(), Tok()]
        for lb in range(16):
            b = 2 + lb % 2
            k = lb % 2
            mm(bank(b), h3[:, lb * 128:(lb + 1) * 128], wout[:], True, True, r=[t_h3, t_f], w=[PT[b]])
            act(E[k][:], AD[:], AF.Exp, r=[t_AD, t_f], w=[t_E[k]], scale=negt[:, lb:lb + 1])
            tt(fl[k][:], bank(b), E[k][:], ALU.mult, r=[PT[b], t_E[k]], w=[t_fl[k]])
            ts(fl[k][:, 256:512], fl[k][:, 256:512], m0[:, lb:lb + 1], None, ALU.mult, r=[t_fl[k], t_f], w=[t_fl[k]], eng="vector")
            tt(HS[:, lb, :], fl[k][:, 0:256], fl[k][:, 256:512], ALU.add, r=[t_fl[k]], w=[t_HS], eng="gpsimd")
            tt(HD[:, lb, :], fl[k][:, 256:512], fl[k][:, 0:256], ALU.subtract, r=[t_fl[k]], w=[t_HD], eng="vector")
        P.barrier()
        A.off = sub_mark
        cw = A.tile([128, 6, 4], F32, "cw"); t_cw = Tok()
        for cc in range(6):
            for k in range(3):
                dma("sync", cw[:, cc, k:k + 1], Wt["hy_conv_w"][l, k, cc * 128:(cc + 1) * 128].rearrange("(p o) -> p o", o=1), w=[t_cw])
            dma("sync", cw[:, cc, 3:4], Wt["hy_conv_b"][l, cc * 128:(cc + 1) * 128].rearrange("(p o) -> p o", o=1), w=[t_cw])
        ut = [A.tile([128, L], F32, "ut") for _ in range(2)]
        t_ut = [Tok(), Tok()]
        uc = [A.tile([128, L], F32, "uc") for _ in range(2)]
        t_uc = [Tok(), Tok()]
        zf = A.tile([128, L], F32, "zf"); t_zf = Tok()
        ui = [0]

        def conv(seq, cc, dstk):
            u = ui[0] % 2
            ui[0] += 1
            dma("sync", ut[u][:], UT[cc * 128:(cc + 1) * 128, seq * L:(seq + 1) * L], w=[t_ut[u]])
            o, t_o = uc[dstk], t_uc[dstk]
            ts(o[:], ut[u][:], cw[:, cc, 1:2], cw[:, cc, 3:4], ALU.mult, ALU.add, r=[t_ut[u], t_cw], w=[t_o])
            stt(o[:, 1:L], ut[u][:, 0:L - 1], cw[:, cc, 0:1], o[:, 1:L], ALU.mult, ALU.add, r=[t_ut[u], t_cw, t_o], w=[t_o])
            stt(o[:, 0:L - 1], ut[u][:, 1:L], cw[:, cc, 2:3], o[:, 0:L - 1], ALU.mult, ALU.add, r=[t_ut[u], t_cw, t_o], w=[t_o])

        for seq in range(2):
            for j in range(2):
                conv(seq, 2 + j, 0)
                conv(seq, 4 + j, 1)
                tt(zf[:], uc[0][:], uc[1][:], ALU.mult, r=[t_uc[0], t_uc[1]], w=[t_zf])
                cp(zT[:, j, seq * L:(seq + 1) * L], zf[:], r=[t_zf], w=[t_zT], eng="scalar")
                for g in range(4):
                    b = 4 + g % 2
                    for i in range(4):
                        tb = g * 4 + i
                        tr(bank(b, i * 128, (i + 1) * 128), zf[:, tb * 128:(tb + 1) * 128], r=[t_zf], w=[PT[b]])
                    c0 = seq * 256 + j * 128
                    cp(z_tm[:, g * 4:(g + 1) * 4, c0:c0 + 128], bank(b).rearrange("p (a t) -> p a t", a=4), r=[PT[b]], w=[t_ztm],
                       eng="vector" if g % 2 else "scalar")
                conv(seq, j, 0)
                cp(x0T[:, j, seq * L:(seq + 1) * L], uc[0][:], r=[t_uc[0]], w=[t_x0], eng="scalar")
        P.barrier()
        A.off = sub_mark
        Pre = A.tile([128, 16, 512], BF16, "Pre"); t_Pre = Tok()
        Pim = A.tile([128, 16, 512], BF16, "Pim"); t_Pim = Tok()
        cfb = [A.tile([128, 16, 128], BF16, "cfb") for _ in range(2)]
        sfb = [A.tile([128, 16, 128], BF16, "sfb") for _ in range(2)]
        t_cs = [Tok(), Tok()]
        hh = [A.tile([128, 512], F32, "hh") for _ in range(2)]
        t_hh = [Tok(), Tok()]
        _tq1 = A.tile([128, 4, 512], F32, "tq")
        tq_ = [_tq1, _tq1]
        _ttq = Tok()
        t_tq = [_ttq, _ttq]
        for fb in range(16):
            k = fb % 2
            b0 = 4 * k
            dma("sync", cfb[k][:], Cn["cf"][fb], w=[t_cs[k]])
            dma("sync", sfb[k][:], Cn["sf"][fb], w=[t_cs[k]])
            for lb in range(16):
                mm(bank(b0, 0, 256), cfb[k][:, lb, :], HS[:, lb, :], lb == 0, lb == 15, r=[t_cs[k]], w=[PT[b0]])
            for lb in range(16):
                mm(bank(b0 + 1, 0, 256), sfb[k][:, lb, :], HD[:, lb, :], lb == 0, lb == 15, r=[t_cs[k]], w=[PT[b0 + 1]])
            for tb in range(16):
                mm(bank(b0 + 2), cfb[k][:, tb, :], z_tm[:, tb, :], tb == 0, tb == 15, r=[t_cs[k]], w=[PT[b0 + 2]])
            for tb in range(16):
                mm(bank(b0 + 3), sfb[k][:, tb, :], z_tm[:, tb, :], tb == 0, tb == 15, r=[t_cs[k]], w=[PT[b0 + 3]])
            cp(hh[k][:, 0:256], bank(b0, 0, 256), r=[PT[b0]], w=[t_hh[k]], eng="vector")
            cp(hh[k][:, 256:512], bank(b0 + 1, 0, 256), r=[PT[b0 + 1]], w=[t_hh[k]], eng="vector")
            hre = hh[k][:, 0:256].unsqueeze(1).to_broadcast([128, 2, 256])
            him = hh[k][:, 256:512].unsqueeze(1).to_broadcast([128, 2, 256])
            Av = bank(b0 + 2).rearrange("p (s c) -> p s c", s=2)
            Bv = bank(b0 + 3).rearrange("p (s c) -> p s c", s=2)
            tv = [tq_[k][:, i, :].rearrange("p (s c) -> p s c", s=2) for i in range(4)]
            tt(tv[0], Av, hre, ALU.mult, r=[PT[b0 + 2], t_hh[k]], w=[t_tq[k]])
            tt(tv[1], Bv, him, ALU.mult, r=[PT[b0 + 3], t_hh[k]], w=[t_tq[k]])
            tt(tv[2], Bv, hre, ALU.mult, r=[PT[b0 + 3], t_hh[k]], w=[t_tq[k]])
            tt(tv[3], Av, him, ALU.mult, r=[PT[b0 + 2], t_hh[k]], w=[t_tq[k]])
            tt(Pre[:, fb, :], tq_[k][:, 0, :], tq_[k][:, 1, :], ALU.add, r=[t_tq[k]], w=[t_Pre], eng="gpsimd")
            tt(Pim[:, fb, :], tq_[k][:, 2, :], tq_[k][:, 3, :], ALU.subtract, r=[t_tq[k]], w=[t_Pim], eng="gpsimd")
        cit = A.tile([128, 16, 512], BF16, "cit")
        sit = A.tile([128, 16, 512], BF16, "sit")
        t_ci = Tok()
        tmp = [A.tile([128, 512], F32, "tmp") for _ in range(2)]
        t_tmp = [Tok(), Tok()]
        obt = [A.tile([128, 512], BF16, "obt") for _ in range(2)]
        t_obt = [Tok(), Tok()]
        n = 0
        for tq in range(4):
            dma("sync", cit[:], Cn["ci"][:, :, tq * 512:(tq + 1) * 512], w=[t_ci])
            dma("sync", sit[:], Cn["si"][:, :, tq * 512:(tq + 1) * 512], w=[t_ci])
            for sq in range(2):
                for j in range(2):
                    b = n % 2
                    n += 1
                    c0 = sq * 256 + j * 128
                    for kb in range(16):
                        mm(bank(b), Pre[:, kb, c0:c0 + 128], cit[:, kb, :], kb == 0, False, r=[t_Pre, t_ci], w=[PT[b]])
                        mm(bank(b), Pim[:, kb, c0:c0 + 128], sit[:, kb, :], False, kb == 15, r=[t_Pim, t_ci], w=[PT[b]])
                    t0 = sq * L + tq * 512
                    stt(tmp[b][:], zT[:, j, t0:t0 + 512], hbias[:, j:j + 1], bank(b), ALU.mult, ALU.add, r=[t_zT, t_hb, PT[b]], w=[t_tmp[b]])
                    tt(obt[b][:], tmp[b][:], x0T[:, j, t0:t0 + 512], ALU.mult, r=[t_tmp[b], t_x0], w=[t_obt[b]], eng="gpsimd")
                    dma("sync", AT[512 + j * 128:512 + (j + 1) * 128, t0:t0 + 512], obt[b][:], r=[t_obt[b]], w=[tAT((4 + j, t0 // 512))])

    stage_init()
    P.barrier()
    for l in range(NL if stop_after != "init" else 0):
        stage_A(l)
        P.barrier()
        if stop_after == "A":
            break
        if HY:
            stage_HY(l)
            P.barrier()
            if stop_after == "HY":
                break
        stage_ATT()
        P.barrier()
        if stop_after == "ATT":
            break
        stage_WO(l, x_in if l == 0 else X32, HY)
        P.barrier()
        if stop_after == "WO":
            break
        stage_XA(l)
        P.barrier()
        if stop_after == "XA":
            break
        stage_MOE(l)
        P.barrier()
        last = (l == NL - 1)
        stage_LN3(l, out if last else X32)
        P.barrier()
    if stop_after in ("WO", "XA"):
        stage_final()
    P.emit(st)
    st.close()
    return nc


_CONSTS = None


def _run(inputs, NL=DEPTH, dbg=(), stop_after=None, cores=NCORES):
    global _CONSTS
    if _CONSTS is None:
        _CONSTS = _host_consts()
    nc = build(NL, dbg, stop_after)
    x = np.ascontiguousarray(np.asarray(inputs["x"], dtype=np.float32)).reshape(16, L, D)
    mem = np.ascontiguousarray(np.asarray(inputs["mem"], dtype=np.float32)).reshape(16, 256, D)
    in_maps = []
    for c in range(cores):
        m = {"x": x[2 * c:2 * c + 2].reshape(T, D), "mem": mem[2 * c:2 * c + 2].reshape(512, D)}
        for k in _W_SHAPES:
            m[k] = np.ascontiguousarray(np.asarray(inputs[k], dtype=np.float32))
        for k in _CONST_SHAPES:
            m[k] = _CONSTS[k]
        in_maps.append(m)
    res = run_bass_kernel_spmd(nc, in_maps, core_ids=list(range(cores)))
    return res.results


def kernel(**inputs):
    res = _run(inputs)
    out = np.stack([r["out"].reshape(2, L, D) for r in res], axis=0).reshape(16, L, D)
    return out.astype(np.float32)
```

```python
import contextlib
import os
CUT = int(os.environ.get('A_CUT', '99'))
SUB = int(os.environ.get('A_SUB', '99'))
HY = int(os.environ.get('A_HY', '1'))
import math
import numpy as np
import ml_dtypes
import concourse.bass as bass
import concourse.mybir as mybir
from concourse.bass_utils import run_bass_kernel_spmd

F32 = mybir.dt.float32
BF16 = mybir.dt.bfloat16
I32 = mybir.dt.int32
U32 = mybir.dt.uint32
ALU = mybir.AluOpType
AF = mybir.ActivationFunctionType
AX = mybir.AxisListType

NCORES = 8
L = 2048
T = 4096
D = 1024
NB = 32
DEPTH = 4
ALPHA = float((2 * DEPTH) ** 0.25)
RMS_EPS = 1e-6
LN_EPS = 1e-5
TWO_PI = 2.0 * math.pi
MAGIC = 12582912.0


class Tok:
    __slots__ = ("w", "r")

    def __init__(self):
        self.w = None
        self.r = {}


class Prog:
    ENG = ["tensor", "vector", "scalar", "gpsimd", "sync"]
    NDMA = 8

    def __init__(self, nc):
        self.nc = nc
        self.ops = {e: [] for e in self.ENG}
        self.cnt = {e: 0 for e in self.ENG}
        self.seen = {e: {} for e in self.ENG}
        self.dma_n = {e: 0 for e in self.ENG}
        self.last = {}

    def op(self, eng, fn, r=(), w=(), dma=False):
        deps = {}

        def add(key, val):
            if deps.get(key, 0) < val:
                deps[key] = val

        for t in r:
            if t.w is not None:
                add(*t.w)
        for t in w:
            if t.w is not None:
                add(*t.w)
            for k, v in t.r.items():
                add(k, v)
        if dma:
            j = self.dma_n[eng]
            self.dma_n[eng] += 1
            slot = j % self.NDMA
            k = j // self.NDMA + 1
            me = ((eng, "dma", slot), 16 * k)
            if k > 1:
                add((eng, "dma", slot), 16 * (k - 1))
        else:
            self.cnt[eng] += 1
            me = ((eng, "c"), self.cnt[eng])
        seen = self.seen[eng]
        waits = []
        for key, val in deps.items():
            if seen.get(key, 0) >= val:
                continue
            seen[key] = val
            waits.append((key, val))
        self.ops[eng].append((fn, waits, me))
        self.last[me[0]] = me[1]
        for t in r:
            if t.r.get(me[0], 0) < me[1]:
                t.r[me[0]] = me[1]
        for t in w:
            t.w = me
            t.r = {}
        return me

    def barrier(self):
        snap = dict(self.last)
        for e in self.ENG:
            seen = self.seen[e]
            waits = []
            for key, val in snap.items():
                if seen.get(key, 0) >= val:
                    continue
                seen[key] = val
                waits.append((key, val))
            if waits:
                self.ops[e].append((None, waits, None))

    def emit(self, stack):
        nc = self.nc
        self.barrier()
        sems = {}
        for e in self.ENG:
            for fn, waits, me in self.ops[e]:
                for key, _ in waits:
                    if key not in sems:
                        sems[key] = None
                if me is not None and me[0] not in sems:
                    sems[me[0]] = None
        for key in sems:
            sems[key] = stack.enter_context(nc.semaphore("s_" + "_".join(str(x) for x in key)))
        block = stack.enter_context(nc.Block())

        def run(e):
            def body(eng):
                for fn, waits, me in self.ops[e]:
                    for key, val in waits:
                        eng.wait_ge(sems[key], val)
                    if fn is not None:
                        ins = fn(eng)
                        ins.then_inc(sems[me[0]], 16 if me[0][1] == "dma" else 1)
            return body

        block.tensor(run("tensor"))
        block.vector(run("vector"))
        block.scalar(run("scalar"))
        block.gpsimd(run("gpsimd"))
        block.sync(run("sync"))


def _esz(dt):
    return 2 if dt == BF16 else 4


class Arena:
    BASE = 17408
    LIMIT = 229376

    def __init__(self, nc):
        self.nc = nc
        self.off = self.BASE
        self.mark = self.BASE
        self.n = 0

    def reset(self):
        self.off = self.mark

    def tile(self, shape, dt, name="t"):
        sz = int(np.prod(shape[1:])) * _esz(dt)
        self.n += 1
        t = self.nc.alloc_sbuf_tensor_at(f"{name}_{self.n}", list(shape), dt, offset=self.off)
        self.off += (sz + 63) // 64 * 64
        assert self.off <= self.LIMIT, (name, self.off)
        return t


def _host_consts():
    c = {}
    c["ident"] = np.eye(128, dtype=np.float32)
    t = np.arange(L)
    row = (t // 64).astype(np.float64)
    col = (t % 64).astype(np.float64)

    def tab(n):
        inv = 10000.0 ** (-(np.arange(n, dtype=np.float64) * 2.0 / (2 * n)))
        ang = np.stack([row[:, None] * inv[None], col[:, None] * inv[None]], axis=1)
        cs = np.cos(ang).astype(np.float32).reshape(16, 128, 2 * n).transpose(1, 0, 2)
        sn = np.sin(ang).astype(np.float32).reshape(16, 128, 2 * n).transpose(1, 0, 2)
        return np.ascontiguousarray(cs), np.ascontiguousarray(sn)

    c["cq"], c["sq"] = tab(16)
    c["cm"], c["sm"] = tab(8)
    tt = np.linspace(0.0, 1.0, L, dtype=np.float32)[:, None]
    bands = 16
    fb = np.linspace(1e-4, bands - 1, bands, dtype=np.float32)[None, :]
    w = (2.0 * math.pi * np.arange(L, dtype=np.float32)[:, None] / L).astype(np.float32)
    feats = np.concatenate([tt, np.cos(fb * w), -np.sin(fb * w)], axis=-1).astype(np.float32)
    c["featsT"] = np.ascontiguousarray(feats.T)
    c["negt"] = np.ascontiguousarray((-tt[:, 0]).reshape(16, 128).T).astype(np.float32)
    m0 = np.ones((128, 16), np.float32)
    m0[0, 0] = 0.0
    c["m0"] = m0
    k = np.arange(L, dtype=np.int64)
    n = np.arange(L, dtype=np.int64)
    ph = ((2 * k[None, :] + 1) * n[:, None]) % 8192
    ang = ph.astype(np.float64) * (math.pi / 4096.0)
    C = np.cos(ang)
    S = np.sin(ang)
    def fwd(M):
        return np.ascontiguousarray(M.reshape(16, 128, 16, 128).transpose(2, 1, 0, 3)).astype(ml_dtypes.bfloat16)
    c["cf"] = fwd(C)
    c["sf"] = fwd(S)
    def inv(M):
        return np.ascontiguousarray((M.T / 2048.0).reshape(16, 128, L).transpose(1, 0, 2)).astype(ml_dtypes.bfloat16)
    c["ci"] = inv(C)
    c["si"] = inv(S)
    o48 = np.zeros((48, 1), np.float32)
    o48[32:] = 2048.0
    c["o48"] = o48
    return c


_CONST_SHAPES = {
    "ident": ([128, 128], F32), "cq": ([128, 16, 32], F32), "sq": ([128, 16, 32], F32),
    "cm": ([128, 16, 16], F32), "sm": ([128, 16, 16], F32), "featsT": ([33, L], F32),
    "negt": ([128, 16], F32), "m0": ([128, 16], F32),
    "cf": ([16, 128, 16, 128], BF16), "sf": ([16, 128, 16, 128], BF16),
    "ci": ([128, 16, L], BF16), "si": ([128, 16, L], BF16), "o48": ([48, 1], F32),
}

_W_SHAPES = {
    "w_in": [4, 1024, 1952], "gqa_q_norm": [4, 64], "gqa_k_norm": [4, 64], "hy_conv_w": [4, 3, 768],
    "hy_conv_b": [4, 768], "hy_w1": [4, 33, 64], "hy_b1": [4, 64], "hy_w2": [4, 64, 64], "hy_b2": [4, 64],
    "hy_w3": [4, 64, 64], "hy_b3": [4, 64], "hy_wout": [4, 64, 512], "hy_freq": [4, 64],
    "hy_decay": [4, 2, 256], "hy_bias": [4, 256], "mla_q_norm": [4, 256], "mla_w_uq": [4, 256, 384],
    "mla_kv_norm": [4, 128], "mla_w_ukv": [4, 128, 512], "w_o": [4, 1024, 1024], "ln1_g": [4, 1024],
    "ln1_b": [4, 1024], "xa_wq": [4, 1024, 1024], "xa_wkv": [4, 1024, 2048], "xa_wo": [4, 1024, 1024],
    "ln2_g": [4, 1024], "ln2_b": [4, 1024], "moe_router": [4, 1024, 16], "moe_w_gate": [4, 16, 1024, 512],
    "moe_w_up": [4, 16, 1024, 512], "moe_w_down": [4, 16, 512, 1024], "ln3_g": [4, 1024], "ln3_b": [4, 1024],
}


def build(NL=DEPTH, dbg=(), stop_after=None):
    nc = bass.Bass("TRN2", target_bir_lowering=False)

    def din(name, shape, dt=F32):
        return nc.dram_tensor(name, list(shape), dt, kind="ExternalInput").ap()

    x_in = din("x", [T, D])
    mem_in = din("mem", [512, D])
    Wt = {k: din(k, s) for k, s in _W_SHAPES.items()}
    Cn = {k: din(k, s, dt) for k, (s, dt) in _CONST_SHAPES.items()}
    out = nc.dram_tensor("out", [T, D], F32, kind="ExternalOutput").ap()

    def dscr(name, shape, dt):
        kind = "ExternalOutput" if name in dbg else "Internal"
        return nc.dram_tensor(name, list(shape), dt, kind=kind).ap()

    X32 = dscr("X32", [T, D], F32)
    Z32 = dscr("Z32", [T, D], F32)
    XT16 = dscr("XT16", [D, T], BF16)
    QTd = dscr("QTd", [6, 128, T], BF16)
    QCTd = dscr("QCTd", [4, 96, T], BF16)
    KCTd = dscr("KCTd", [4, 96, T], BF16)
    UT = dscr("UT", [768, T], F32)
    AT = dscr("AT", [D, T], BF16)
    XT16v = XT16.rearrange("(c p) t -> p c t", p=128)
    ATv = AT.rearrange("(c p) t -> p c t", p=128)
    t_X32 = [Tok() for _ in range(NB)]
    t_Z32 = [Tok() for _ in range(NB)]
    t_Zs = [Tok(), Tok()]
    t_XT = [Tok() for _ in range(8)]
    t_QT = [Tok() for _ in range(8)]
    t_QCT = [Tok() for _ in range(8)]
    t_KCT = [Tok() for _ in range(8)]
    t_UT = [Tok() for _ in range(8)]
    t_AT = {}

    def tAT(key):
        if key not in t_AT:
            t_AT[key] = Tok()
        return t_AT[key]

    P = Prog(nc)
    A = Arena(nc)
    _bc = {}

    def bcr(eng):
        if "r" not in _bc:
            _bc["r"] = eng.to_reg(T - 1)
        return _bc["r"]
    st = contextlib.ExitStack()
    ps = st.enter_context(nc.psum_tensor("ps", [128, 4096], F32))
    PT = [Tok() for _ in range(8)]

    def bank(b, lo=0, hi=512, p0=0, p1=128):
        return ps[p0:p1, b * 512 + lo:b * 512 + hi]

    def dma(q, out_, in_, r=(), w=()):
        P.op(q, lambda e: e.dma_start(out=out_, in_=in_), r=r, w=w, dma=True)

    def mm(out_, lhsT, rhs, start, stop, r=(), w=()):
        P.op("tensor", lambda e: e.matmul(out_, lhsT=lhsT, rhs=rhs, start=start, stop=stop), r=r, w=w)

    def tr(out_, in_, r=(), w=()):
        pp = in_.shape[0]
        P.op("tensor", lambda e: e.transpose(out=out_, in_=in_, identity=ident[0:pp, 0:pp]), r=list(r) + [t_const], w=w)

    def act(out_, in_, func, r=(), w=(), **kw):
        P.op("scalar", lambda e: e.activation(out=out_, in_=in_, func=func, **kw), r=r, w=w)

    def vop(name, r=(), w=(), eng="vector", **kw):
        P.op(eng, lambda e: getattr(e, name)(**kw), r=r, w=w)

    def tt(out_, in0, in1, op, r=(), w=(), eng="vector"):
        P.op(eng, lambda e: e.tensor_tensor(out=out_, in0=in0, in1=in1, op=op), r=r, w=w)

    def ts(out_, in0, s1, s2, op0, op1=None, r=(), w=(), eng="vector"):
        if op1 is None:
            P.op(eng, lambda e: e.tensor_scalar(out=out_, in0=in0, scalar1=s1, scalar2=None, op0=op0), r=r, w=w)
        else:
            P.op(eng, lambda e: e.tensor_scalar(out=out_, in0=in0, scalar1=s1, scalar2=s2, op0=op0, op1=op1), r=r, w=w)

    def stt(out_, in0, scalar, in1, op0, op1, r=(), w=(), eng="vector"):
        P.op(eng, lambda e: e.scalar_tensor_tensor(out=out_, in0=in0, scalar=scalar, in1=in1, op0=op0, op1=op1), r=r, w=w)

    def cp(out_, in_, r=(), w=(), eng="vector"):
        if eng == "scalar":
            P.op("scalar", lambda e: e.copy(out=out_, in_=in_), r=r, w=w)
        else:
            P.op(eng, lambda e: e.tensor_copy(out=out_, in_=in_), r=r, w=w)

    t_const = Tok()
    ident = A.tile([128, 128], F32, "ident")
    dma("sync", ident[:], Cn["ident"], w=[t_const])
    ones_bf = A.tile([128, 128], BF16, "ones")
    P.op("vector", lambda e: e.memset(ones_bf[:], 1.0), w=[t_const])
    ones_f = A.tile([128, 64], F32, "ones_f")
    P.op("vector", lambda e: e.memset(ones_f[:], 1.0), w=[t_const])
    CQ = A.tile([128, 16, 32], F32, "CQ"); SQ = A.tile([128, 16, 32], F32, "SQ")
    CM = A.tile([128, 16, 16], F32, "CM"); SM = A.tile([128, 16, 16], F32, "SM")
    for tl, nm in ((CQ, "cq"), (SQ, "sq"), (CM, "cm"), (SM, "sm")):
        dma("sync", tl[:], Cn[nm], w=[t_const])
    VA = A.tile([128, NB, 2, 65], BF16, "VA")
    VC = A.tile([128, NB, 4, 65], BF16, "VC")
    t_VA = [Tok() for _ in range(NB)]
    t_VC = [Tok() for _ in range(NB)]
    P.op("vector", lambda e: e.memset(VA[:], 1.0), w=t_VA)
    P.op("vector", lambda e: e.memset(VC[:], 1.0), w=t_VC)
    memT = A.tile([128, 8, 512], BF16, "memT")
    t_memT = Tok()
    A.mark = A.off

    def emit_xT(src, t_src, xst, t_xst, tbl, b0, b1):
        for c in range(8):
            b = b0 if c < 4 else b1
            tr(bank(b, (c % 4) * 128, (c % 4) * 128 + 128), src[:, c * 128:(c + 1) * 128], r=[t_src], w=[PT[b]])
        cp(xst[:, 0:4, tbl * 128:(tbl + 1) * 128], bank(b0).rearrange("p (c t) -> p c t", c=4),
           r=[PT[b0]], w=[t_xst], eng="scalar")
        cp(xst[:, 4:8, tbl * 128:(tbl + 1) * 128], bank(b1).rearrange("p (c t) -> p c t", c=4),
           r=[PT[b1]], w=[t_xst], eng="vector")

    def emit_ln(zt, t_z, G, Bt, t_gb, sm, t_sm):
        junk = sm["junk"]
        act(junk[:], zt[:], AF.Copy, r=[t_z], w=[sm["tj"], t_sm], scale=1.0 / D, accum_out=sm["s"][:, 2:3])
        act(junk[:], zt[:], AF.Square, r=[t_z], w=[sm["tj"], t_sm], scale=float(D ** -0.5), accum_out=sm["s"][:, 1:2])
        stt(sm["s"][:, 4:5], sm["s"][:, 2:3], sm["s"][:, 2:3], sm["s"][:, 1:2], ALU.mult, ALU.subtract, r=[t_sm], w=[t_sm])
        act(sm["s"][:, 5:6], sm["s"][:, 4:5], AF.Sqrt, r=[t_sm], w=[t_sm], bias=LN_EPS, scale=-1.0)
        vop("reciprocal", out=sm["s"][:, 5:6], in_=sm["s"][:, 5:6], r=[t_sm], w=[t_sm])
        stt(sm["s"][:, 6:7], sm["s"][:, 2:3], -1.0, sm["s"][:, 5:6], ALU.mult, ALU.mult, r=[t_sm], w=[t_sm])
        act(zt[:], zt[:], AF.Identity, r=[t_z, t_sm], w=[t_z], scale=sm["s"][:, 5:6], bias=sm["s"][:, 6:7])
        tt(zt[:], zt[:], G[:], ALU.mult, r=[t_z, t_gb], w=[t_z], eng="vector")
        tt(zt[:], zt[:], Bt[:], ALU.add, r=[t_z, t_gb], w=[t_z], eng="gpsimd")

    def load_gb(gname, bname, l):
        G = A.tile([128, D], F32, "G"); Bt = A.tile([128, D], F32, "B")
        t_gb = Tok()
        dma("sync", G[:], Wt[gname][l].partition_broadcast(128), w=[t_gb])
        dma("sync", Bt[:], Wt[bname][l].partition_broadcast(128), w=[t_gb])
        return G, Bt, t_gb

    def ln_scratch():
        return {"s": A.tile([128, 8], F32, "lns"), "junk": A.tile([128, D], BF16, "junk"), "tj": Tok()}, Tok()


    def run_pipe(n, stages):
        K = len(stages)
        for i in range(n + K - 1):
            for k, f in enumerate(stages):
                j = i - k
                if 0 <= j < n:
                    f(j)

    def ln_hops(xbs, t_xbs, sms, junks, G, Bt, t_gb, NR):
        def h1(tb):
            kb = tb % NR
            sm, t_sm = sms[kb]
            jk, t_jk = junks[tb % 2]
            act(jk[:], xbs[kb][:], AF.Copy, r=[t_xbs[kb]], w=[t_jk, t_sm], scale=1.0 / D, accum_out=sm["s"][:, 2:3])
            act(jk[:], xbs[kb][:], AF.Square, r=[t_xbs[kb]], w=[t_jk, t_sm], scale=float(D ** -0.5), accum_out=sm["s"][:, 1:2])

        def h2(tb):
            sm, t_sm = sms[tb % NR]
            stt(sm["s"][:, 4:5], sm["s"][:, 2:3], sm["s"][:, 2:3], sm["s"][:, 1:2], ALU.mult, ALU.subtract, r=[t_sm], w=[t_sm])

        def h3(tb):
            sm, t_sm = sms[tb % NR]
            act(sm["s"][:, 5:6], sm["s"][:, 4:5], AF.Sqrt, r=[t_sm], w=[t_sm], bias=LN_EPS, scale=-1.0)

        def h4(tb):
            sm, t_sm = sms[tb % NR]
            vop("reciprocal", out=sm["s"][:, 5:6], in_=sm["s"][:, 5:6], r=[t_sm], w=[t_sm])
            stt(sm["s"][:, 6:7], sm["s"][:, 2:3], -1.0, sm["s"][:, 5:6], ALU.mult, ALU.mult, r=[t_sm], w=[t_sm])

        def h5(tb):
            kb = tb % NR
            sm, t_sm = sms[kb]
            act(xbs[kb][:], xbs[kb][:], AF.Identity, r=[t_xbs[kb], t_sm], w=[t_xbs[kb]], scale=sm["s"][:, 5:6], bias=sm["s"][:, 6:7])

        def h6(tb):
            kb = tb % NR
            tt(xbs[kb][:], xbs[kb][:], G[:], ALU.mult, r=[t_xbs[kb], t_gb], w=[t_xbs[kb]], eng="vector")

        def h7(tb):
            kb = tb % NR
            tt(xbs[kb][:], xbs[kb][:], Bt[:], ALU.add, r=[t_xbs[kb], t_gb], w=[t_xbs[kb]], eng="gpsimd")
        return [h1, h2, h3, h4, h5, h6, h7]

    def small_sm():
        return {"s": A.tile([128, 8], F32, "lns")}, Tok()

    class Skew:
        def __init__(self, sk):
            self.sk = sk
            self.q = []

        def push(self, front, back):
            front()
            self.q.append(back)
            while len(self.q) > self.sk:
                self.q.pop(0)()

        def flush(self):
            while self.q:
                self.q.pop(0)()

    def emit_rope(xg, t_x, H, n, ct, st_, t1, t2, ro, t_t1, t_t2, t_ro):
        def v5(tl):
            return tl[:].rearrange("p (h f j i) -> p h f j i", h=H, f=2, j=2)
        for f in range(2):
            cb = ct[:, f, :].unsqueeze(1).unsqueeze(1).to_broadcast([128, H, 2, n])
            sb = st_[:, f, :].unsqueeze(1).to_broadcast([128, H, n])
            tt(v5(t1)[:, :, f], v5(xg)[:, :, f], cb, ALU.mult, r=[t_x, t_const], w=[t_t1], eng="vector")
            tt(v5(t2)[:, :, f, 0, :], v5(xg)[:, :, f, 1, :], sb, ALU.mult, r=[t_x, t_const], w=[t_t2], eng="gpsimd")
            tt(v5(t2)[:, :, f, 1, :], v5(xg)[:, :, f, 0, :], sb, ALU.mult, r=[t_x, t_const], w=[t_t2], eng="gpsimd")

        def v4(tl):
            return tl[:].rearrange("p (hf j i) -> p hf j i", j=2, i=n)
        tt(v4(ro)[:, :, 0, :], v4(t1)[:, :, 0, :], v4(t2)[:, :, 0, :], ALU.subtract, r=[t_t1, t_t2], w=[t_ro], eng="vector")
        tt(v4(ro)[:, :, 1, :], v4(t1)[:, :, 1, :], v4(t2)[:, :, 1, :], ALU.add, r=[t_t1, t_t2], w=[t_ro], eng="gpsimd")

    def stage_init():
        A.reset()
        xb = [A.tile([128, D], F32, "xb") for _ in range(2)]
        t_xb = [Tok(), Tok()]
        xst = [A.tile([128, 8, 512], BF16, "xst") for _ in range(2)]
        t_xst = [Tok(), Tok()]
        for i in range(4):
            k = i % 2
            dma("sync", xb[k][:], mem_in[i * 128:(i + 1) * 128, :], w=[t_xb[k]])
            for c in range(8):
                b = 0 if c < 4 else 1
                tr(bank(b, (c % 4) * 128, (c % 4) * 128 + 128), xb[k][:, c * 128:(c + 1) * 128], r=[t_xb[k]], w=[PT[b]])
            cp(memT[:, 0:4, i * 128:(i + 1) * 128], bank(0).rearrange("p (c t) -> p c t", c=4), r=[PT[0]], w=[t_memT], eng="scalar")
            cp(memT[:, 4:8, i * 128:(i + 1) * 128], bank(1).rearrange("p (c t) -> p c t", c=4), r=[PT[1]], w=[t_memT], eng="vector")
        for tb in range(NB):
            k = tb % 2
            ck, tbl = tb // 4, tb % 4
            dma("sync", xb[k][:], x_in[tb * 128:(tb + 1) * 128, :], w=[t_xb[k]])
            emit_xT(xb[k], t_xb[k], xst[ck % 2], t_xst[ck % 2], tbl, 2 + 2 * k, 3 + 2 * k)
            if tbl == 3:
                dma("sync", XT16v[:, :, ck * 512:(ck + 1) * 512], xst[ck % 2][:], r=[t_xst[ck % 2]], w=[t_XT[ck]])

    def stage_A(l):
        A.reset()
        win = A.tile([128, 8, 1952], BF16, "win")
        t_win = [Tok() for _ in range(8)]
        for c in range(8):
            dma("gpsimd", win[:, c, :], Wt["w_in"][l, c * 128:(c + 1) * 128, :], w=[t_win[c]])
        t_par = Tok()
        G10 = A.tile([128, 640], F32, "G10")
        for h in range(10):
            src = Wt["gqa_q_norm"][l] if h < 8 else Wt["gqa_k_norm"][l]
            dma("sync", G10[:, h * 64:(h + 1) * 64], src.partition_broadcast(128), w=[t_par])
        Gm = A.tile([128, 384], F32, "Gm")
        dma("sync", Gm[:, 0:256], Wt["mla_q_norm"][l].partition_broadcast(128), w=[t_par])
        dma("sync", Gm[:, 256:384], Wt["mla_kv_norm"][l].partition_broadcast(128), w=[t_par])
        wuq = A.tile([128, 2, 384], BF16, "wuq")
        wukv = A.tile([128, 512], BF16, "wukv")
        dma("gpsimd", wuq[:], Wt["mla_w_uq"][l].rearrange("(c p) n -> p c n", p=128), w=[t_par])
        dma("gpsimd", wukv[:], Wt["mla_w_ukv"][l], w=[t_par])

        xt = [A.tile([128, 8, 512], BF16, "xt") for _ in range(2)]
        t_xt = [Tok(), Tok()]
        uts = [A.tile([128, 512], F32, "uts") for _ in range(2)]
        t_uts = [Tok(), Tok()]
        QS = [A.tile([128, 6, 512], BF16, "QS") for _ in range(2)]
        t_QS = [Tok(), Tok()]
        QCS = [A.tile([96, 4, 512], BF16, "QCS") for _ in range(2)]
        t_QCS = [Tok(), Tok()]
        KCS = [A.tile([96, 4, 512], BF16, "KCS") for _ in range(2)]
        t_KCS = [Tok(), Tok()]
        sqt = A.tile([128, 640], F32, "sqt"); t_sqt = Tok()
        ss = A.tile([128, 16], F32, "ss"); t_ss = Tok()
        xg = A.tile([128, 640], F32, "xg"); t_xg = Tok()
        r1 = A.tile([128, 640], F32, "r1"); t_r1 = Tok()
        r2 = A.tile([128, 640], F32, "r2"); t_r2 = Tok()
        ro = A.tile([128, 640], F32, "ro"); t_ro = Tok()
        KK = A.tile([128, 256], F32, "KK"); t_KK = Tok()
        CN = A.tile([128, 384], F32, "CN"); t_CN = Tok()
        cnT = A.tile([128, 3, 128], BF16, "cnT"); t_cnT = Tok()
        RM = A.tile([128, 160], F32, "RM"); t_RM = Tok()
        m1 = A.tile([128, 160], F32, "m1"); t_m1 = Tok()
        m2 = A.tile([128, 160], F32, "m2"); t_m2 = Tok()
        mo = A.tile([128, 160], F32, "mo"); t_mo = Tok()
        QC = A.tile([128, 4, 96], F32, "QC"); t_QC = Tok()
        KC = A.tile([128, 4, 96], F32, "KC"); t_KC = Tok()

        ui = 0
        for ck in range(8):
            k2 = ck % 2
            dma("sync", xt[k2][:], XT16v[:, :, ck * 512:(ck + 1) * 512], r=[t_XT[ck]], w=[t_xt[k2]])
            for cc in range(6):
                for d in range(8):
                    mm(bank(3), win[:, d, 768 + cc * 128:768 + (cc + 1) * 128], xt[k2][:, d, :], d == 0, d == 7,
                       r=[t_win[d], t_xt[k2]], w=[PT[3]])
                u = ui % 2
                ui += 1
                cp(uts[u][:], bank(3), r=[PT[3]], w=[t_uts[u]], eng="scalar")
                dma("sync", UT[cc * 128:(cc + 1) * 128, ck * 512:(ck + 1) * 512], uts[u][:], r=[t_uts[u]], w=[t_UT[ck]])
            for tbl in range(4):
                tb = ck * 4 + tbl
                tb16 = tb % 16
                tc0, tc1 = tbl * 128, (tbl + 1) * 128
                for d in range(8):
                    mm(bank(0), xt[k2][:, d, tc0:tc1], win[:, d, 0:512], d == 0, d == 7, r=[t_win[d], t_xt[k2]], w=[PT[0]])
                for d in range(8):
                    mm(bank(1, 0, 256), xt[k2][:, d, tc0:tc1], win[:, d, 512:768], d == 0, d == 7, r=[t_win[d], t_xt[k2]], w=[PT[1]])
                for d in range(8):
                    mm(bank(2, 0, 416), xt[k2][:, d, tc0:tc1], win[:, d, 1536:1952], d == 0, d == 7, r=[t_win[d], t_xt[k2]], w=[PT[2]])
                if CUT <= 1:
                    continue
                qk = ps[:, 0:640]
                act(sqt[:], qk, AF.Square, r=[PT[0], PT[1]], w=[t_sqt])
                vop("tensor_reduce", out=ss[:, 0:10], in_=sqt[:].rearrange("p (h d) -> p h d", d=64), axis=AX.X, op=ALU.add,
                    r=[t_sqt], w=[t_ss])
                act(ss[:, 0:10], ss[:, 0:10], AF.Sqrt, r=[t_ss], w=[t_ss], scale=1.0 / 64, bias=RMS_EPS)
                vop("reciprocal", out=ss[:, 0:10], in_=ss[:, 0:10], r=[t_ss], w=[t_ss])
                tt(xg[:].rearrange("p (h d) -> p h d", d=64), qk.rearrange("p (h d) -> p h d", d=64),
                   ss[:, 0:10].unsqueeze(2).to_broadcast([128, 10, 64]), ALU.mult, r=[PT[0], PT[1], t_ss], w=[t_xg])
                tt(xg[:], xg[:], G10[:], ALU.mult, r=[t_xg, t_par], w=[t_xg], eng="gpsimd")
                if CUT <= 2:
                    continue
                emit_rope(xg, t_xg, 10, 16, CQ[:, tb16, :].rearrange("p (f i) -> p f i", f=2),
                          SQ[:, tb16, :].rearrange("p (f i) -> p f i", f=2), r1, r2, ro, t_r1, t_r2, t_ro)
                if CUT <= 3:
                    continue
                for rr in range(2):
                    cp(KK[:].rearrange("p (h r d) -> p h r d", h=2, r=2)[:, :, rr, :],
                       ro[:, 512:640].rearrange("p (h d) -> p h d", h=2), r=[t_ro], w=[t_KK], eng="gpsimd")
                for j in range(6):
                    src = ro[:, j * 128:(j + 1) * 128] if j < 4 else KK[:, (j - 4) * 128:(j - 3) * 128]
                    b = 5 if j < 3 else 6
                    tr(bank(b, (j % 3) * 128, (j % 3) * 128 + 128), src, r=[t_ro, t_KK], w=[PT[b]])
                cp(QS[k2][:, 0:3, tc0:tc1], bank(5, 0, 384).rearrange("p (c t) -> p c t", c=3), r=[PT[5]], w=[t_QS[k2]], eng="scalar")
                cp(QS[k2][:, 3:6, tc0:tc1], bank(6, 0, 384).rearrange("p (c t) -> p c t", c=3), r=[PT[6]], w=[t_QS[k2]], eng="vector")
                if CUT <= 4:
                    continue
                cp(VA[:, tb, :, 0:64], bank(1, 128, 256).rearrange("p (h d) -> p h d", h=2), r=[PT[1]], w=[t_VA[tb]], eng="scalar")
                act(sqt[:, 0:384], bank(2, 0, 384), AF.Square, r=[PT[2]], w=[t_sqt])
                vop("tensor_reduce", out=ss[:, 10:11], in_=sqt[:, 0:256], axis=AX.X, op=ALU.add, r=[t_sqt], w=[t_ss])
                vop("tensor_reduce", out=ss[:, 11:12], in_=sqt[:, 256:384], axis=AX.X, op=ALU.add, r=[t_sqt], w=[t_ss])
                act(ss[:, 10:11], ss[:, 10:11], AF.Sqrt, r=[t_ss], w=[t_ss], scale=1.0 / 256, bias=RMS_EPS)
                act(ss[:, 11:12], ss[:, 11:12], AF.Sqrt, r=[t_ss], w=[t_ss], scale=1.0 / 128, bias=RMS_EPS)
                vop("reciprocal", out=ss[:, 10:12], in_=ss[:, 10:12], r=[t_ss], w=[t_ss])
                ts(CN[:, 0:256], bank(2, 0, 256), ss[:, 10:11], None, ALU.mult, r=[PT[2], t_ss], w=[t_CN])
                ts(CN[:, 256:384], bank(2, 256, 384), ss[:, 11:12], None, ALU.mult, r=[PT[2], t_ss], w=[t_CN])
                tt(CN[:], CN[:], Gm[:], ALU.mult, r=[t_CN, t_par], w=[t_CN], eng="gpsimd")
                for j in range(3):
                    tr(bank(7, j * 128, (j + 1) * 128), CN[:, j * 128:(j + 1) * 128], r=[t_CN], w=[PT[7]])
                cp(cnT[:], bank(7, 0, 384).rearrange("p (c t) -> p c t", c=3), r=[PT[7]], w=[t_cnT], eng="scalar")
                cp(RM[:, 128:160], bank(2, 384, 416), r=[PT[2]], w=[t_RM], eng="vector")
                if CUT <= 5:
                    continue
                for kc in range(2):
                    mm(bank(0, 0, 384), cnT[:, kc, :], wuq[:, kc, :], kc == 0, kc == 1, r=[t_cnT, t_par], w=[PT[0]])
                mm(bank(4), cnT[:, 2, :], wukv[:], True, True, r=[t_cnT, t_par], w=[PT[4]])
                if SUB <= 1:
                    continue
                qcv = bank(0, 0, 384).rearrange("p (h e) -> p h e", h=4)
                kvv = bank(4).rearrange("p (h e) -> p h e", h=4)
                cp(RM[:, 0:128].rearrange("p (h e) -> p h e", h=4), qcv[:, :, 64:96], r=[PT[0]], w=[t_RM], eng="vector")
                cp(QC[:, :, 0:64], qcv[:, :, 0:64], r=[PT[0]], w=[t_QC], eng="vector")
                if SUB <= 2:
                    continue
                cp(KC[:, :, 0:64], kvv[:, :, 0:64], r=[PT[4]], w=[t_KC], eng="vector")
                cp(VC[:, tb, :, 0:64], kvv[:, :, 64:128], r=[PT[4]], w=[t_VC[tb]], eng="scalar")
                if CUT <= 6:
                    continue
                emit_rope(RM, t_RM, 5, 8, CM[:, tb16, :].rearrange("p (f i) -> p f i", f=2),
                          SM[:, tb16, :].rearrange("p (f i) -> p f i", f=2), m1, m2, mo, t_m1, t_m2, t_mo)
                cp(QC[:, :, 64:96], mo[:, 0:128].rearrange("p (h e) -> p h e", h=4), r=[t_mo], w=[t_QC], eng="gpsimd")
                cp(KC[:, :, 64:96], mo[:, 128:160].unsqueeze(1).to_broadcast([128, 4, 32]), r=[t_mo], w=[t_KC], eng="gpsimd")
                if CUT <= 7:
                    continue
                for h in range(4):
                    tr(bank(1, h * 128, (h + 1) * 128, 0, 96), QC[:, h, :], r=[t_QC], w=[PT[1]])
                for h in range(4):
                    tr(bank(2, h * 128, (h + 1) * 128, 0, 96), KC[:, h, :], r=[t_KC], w=[PT[2]])
                cp(QCS[k2][:, :, tc0:tc1], bank(1, 0, 512, 0, 96).rearrange("p (c t) -> p c t", c=4), r=[PT[1]], w=[t_QCS[k2]], eng="scalar")
                cp(KCS[k2][:, :, tc0:tc1], bank(2, 0, 512, 0, 96).rearrange("p (c t) -> p c t", c=4), r=[PT[2]], w=[t_KCS[k2]], eng="vector")
            c0, c1 = ck * 512, (ck + 1) * 512
            dma("sync", QTd[:, :, c0:c1].rearrange("j p t -> p j t"), QS[k2][:], r=[t_QS[k2]], w=[t_QT[ck]])
            dma("sync", QCTd[:, :, c0:c1].rearrange("j p t -> p j t"), QCS[k2][:], r=[t_QCS[k2]], w=[t_QCT[ck]])
            dma("sync", KCTd[:, :, c0:c1].rearrange("j p t -> p j t"), KCS[k2][:], r=[t_KCS[k2]], w=[t_KCT[ck]])

    def stage_ATT():
        A.reset()
        qt = [A.tile([128, L], BF16, "qt") for _ in range(2)]
        kt = [A.tile([128, L], BF16, "kt") for _ in range(2)]
        t_qk = [Tok(), Tok()]
        NE = 6
        et = [A.tile([128, 512], BF16, "et") for _ in range(NE)]
        t_et = [Tok() for _ in range(NE)]
        rec = [A.tile([128, 512], F32, "rec") for _ in range(2)]
        t_rec = [Tok(), Tok()]
        bcs = [A.tile([64, 512], F32, "bcs") for _ in range(2)]
        t_bcs = [Tok(), Tok()]
        ot = [A.tile([64, 512], BF16, "ot") for _ in range(2)]
        t_ot = [Tok(), Tok()]
        items = []
        cnt = {"l": 0, "g": 0}

        def add_head(seq, li, pr0, pr1, vtile, t_v, hv, scale, arow, pre):
            qtile, ktile, t_in = qt[li], kt[li], t_qk[li]
            for qb in range(4):
                g = cnt["g"]
                cnt["g"] += 1
                ob = 4 + g % 2
                gi = g % 2
                for kb in range(16):
                    idx = len(items)
                    sb = idx % 4
                    ei = idx % NE

                    def qk(sb=sb, ei=ei, kb=kb, qb=qb):
                        mm(bank(sb), ktile[pr0:pr1, kb * 128:(kb + 1) * 128], qtile[pr0:pr1, qb * 512:(qb + 1) * 512], True, True,
                           r=[t_in], w=[PT[sb]])
                        act(et[ei][:], bank(sb), AF.Exp, r=[PT[sb]], w=[t_et[ei]], scale=scale)

                    def pv(ei=ei, kb=kb, ob=ob):
                        mm(bank(ob, 0, 512, 0, 65), vtile[:, seq * 16 + kb, hv, :], et[ei][:], kb == 0, kb == 15,
                           r=[t_et[ei], t_v[seq * 16 + kb]], w=[PT[ob]])

                    n1 = n2 = None
                    if kb == 15:
                        def n1(ob=ob, gi=gi):
                            vop("reciprocal", out=rec[gi][64:65, :], in_=bank(ob, 0, 512, 64, 65), r=[PT[ob]], w=[t_rec[gi]])

                        def n2(ob=ob, gi=gi, qb=qb):
                            mm(bank(6 + gi, 0, 512, 0, 64), ones_f[64:65, 0:64], rec[gi][64:65, :], True, True,
                               r=[t_rec[gi], t_const], w=[PT[6 + gi]])
                            cp(bcs[gi][:], bank(6 + gi, 0, 512, 0, 64), r=[PT[6 + gi]], w=[t_bcs[gi]], eng="vector")
                            tt(ot[gi][:], bank(ob, 0, 512, 0, 64), bcs[gi][:], ALU.mult, r=[PT[ob], t_bcs[gi]], w=[t_ot[gi]])
                            c0 = seq * L + qb * 512
                            dma("sync", AT[arow:arow + 64, c0:c0 + 512], ot[gi][:], r=[t_ot[gi]], w=[tAT((arow // 128, c0 // 512))])
                    items.append((pre if (qb == 0 and kb == 0) else None, qk, pv, n1, n2))

        for seq in range(2):
            rq = [t_QT[seq * 4 + i] for i in range(4)]
            for j in range(4):
                li = cnt["l"] % 2
                cnt["l"] += 1

                def pre(li=li, j=j, seq=seq, rq=rq):
                    dma("sync", qt[li][:], QTd[j, :, seq * L:(seq + 1) * L], r=rq, w=[t_qk[li]])
                    dma("sync", kt[li][:], QTd[4 + j // 2, :, seq * L:(seq + 1) * L], r=rq, w=[t_qk[li]])
                for hh in range(2):
                    add_head(seq, li, hh * 64, hh * 64 + 64, VA, t_VA, j // 2, 0.125, (2 * j + hh) * 64, pre if hh == 0 else None)
            rq2 = [t_QCT[seq * 4 + i] for i in range(4)] + [t_KCT[seq * 4 + i] for i in range(4)]
            for h in range(4):
                li = cnt["l"] % 2
                cnt["l"] += 1

                def pre(li=li, h=h, seq=seq, rq2=rq2):
                    dma("sync", qt[li][0:96, :], QCTd[h, :, seq * L:(seq + 1) * L], r=rq2, w=[t_qk[li]])
                    dma("sync", kt[li][0:96, :], KCTd[h, :, seq * L:(seq + 1) * L], r=rq2, w=[t_qk[li]])
                add_head(seq, li, 0, 96, VC, t_VC, h, float(96 ** -0.5), 768 + h * 64, pre)
        D1, D2 = 2, 5
        n = len(items)
        for i in range(n + D2 + 1):
            if i < n:
                if items[i][0] is not None:
                    items[i][0]()
                items[i][1]()
            if 0 <= i - D1 < n:
                items[i - D1][2]()
                if items[i - D1][3] is not None:
                    items[i - D1][3]()
            if 0 <= i - D2 < n and items[i - D2][4] is not None:
                items[i - D2][4]()

    def stage_WO(l, xsrc, have_hyena):
        A.reset()
        wo = A.tile([128, 8, D], BF16, "wo")
        t_wo = [Tok() for _ in range(8)]
        for c in range(8):
            dma("gpsimd", wo[:, c, :], Wt["w_o"][l, c * 128:(c + 1) * 128, :], w=[t_wo[c]])
        G, Bt, t_gb = load_gb("ln1_g", "ln1_b", l)
        at = [A.tile([128, 8, 512], BF16, "at") for _ in range(2)]
        t_at = [Tok(), Tok()]
        xst = [A.tile([128, 8, 512], BF16, "xst") for _ in range(2)]
        t_xst = [Tok(), Tok()]
        if not have_hyena:
            zt_ = A.tile([128, 512], BF16, "zero")
            t_z = Tok()
            P.op("vector", lambda e: e.memset(zt_[:], 0.0), w=[t_z])
            for c in (4, 5):
                for ck in range(8):
                    dma("sync", AT[c * 128:(c + 1) * 128, ck * 512:(ck + 1) * 512], zt_[:], r=[t_z], w=[tAT((c, ck))])
        NR = 12
        sms = [small_sm() for _ in range(NR)]
        junks = [(A.tile([128, D], BF16, "junk"), Tok()) for _ in range(2)]
        xbs = [A.tile([128, D], F32, "xbr") for _ in range(NR)]
        t_xbs = [Tok() for _ in range(NR)]

        def s0(tb):
            ck, tbl = tb // 4, tb % 4
            k2 = ck % 2
            kb = tb % NR
            yb_ = 2 * (tb % 2)
            if tbl == 0:
                dma("sync", at[k2][:], ATv[:, :, ck * 512:(ck + 1) * 512], r=[tAT((c, ck)) for c in range(8)], w=[t_at[k2]])
            dma("sync", xbs[kb][:], xsrc[tb * 128:(tb + 1) * 128, :], r=[t_X32[tb]], w=[t_xbs[kb]])
            for half in range(2):
                for c in range(8):
                    mm(bank(yb_ + half), at[k2][:, c, tbl * 128:(tbl + 1) * 128], wo[:, c, half * 512:(half + 1) * 512],
                       c == 0, c == 7, r=[t_at[k2], t_wo[c]], w=[PT[yb_ + half]])
            stt(xbs[kb][:], xbs[kb][:], ALPHA, ps[:, yb_ * 512:(yb_ + 2) * 512], ALU.mult, ALU.add,
                r=[t_xbs[kb], PT[yb_], PT[yb_ + 1]], w=[t_xbs[kb]])

        def s8(tb):
            ck, tbl = tb // 4, tb % 4
            k2 = ck % 2
            kb = tb % NR
            dma("sync", X32[tb * 128:(tb + 1) * 128, :], xbs[kb][:], r=[t_xbs[kb]], w=[t_X32[tb]])
            tbk = 4 + 2 * (tb % 2)
            emit_xT(xbs[kb], t_xbs[kb], xst[k2], t_xst[k2], tbl, tbk, tbk + 1)

        def s9(tb):
            ck, tbl = tb // 4, tb % 4
            k2 = ck % 2
            if tbl == 3:
                dma("sync", XT16v[:, :, ck * 512:(ck + 1) * 512], xst[k2][:], r=[t_xst[k2]], w=[t_XT[ck]])
        run_pipe(NB, [s0] + ln_hops(xbs, t_xbs, sms, junks, G, Bt, t_gb, NR) + [s8, s9])

    def stage_final():
        for tb in range(NB):
            dma("sync", out[tb * 128:(tb + 1) * 128, :], X32[tb * 128:(tb + 1) * 128, :], r=[t_X32[tb]])

    def stage_XA(l):
        A.reset()
        wq = A.tile([128, 8, D], BF16, "wq")
        wo = A.tile([128, 8, D], BF16, "wo")
        t_wkv = [Tok() for _ in range(8)]
        t_wq = [Tok() for _ in range(8)]
        t_wo = [Tok() for _ in range(8)]
        G = A.tile([128, D], F32, "G"); Bt = A.tile([128, D], F32, "B")
        t_gb = Tok()
        KmT = A.tile([128, 8, 512], BF16, "KmT"); t_Km = Tok()
        Vm = A.tile([128, 4, D], BF16, "Vm"); t_Vm = Tok()
        xt = A.tile([128, 8, 512], BF16, "xt"); t_xt = Tok()
        QxT = A.tile([128, 8, 512], BF16, "QxT"); t_Qx = Tok()
        axT = A.tile([128, 8, 512], BF16, "axT"); t_ax = Tok()
        et = [A.tile([128, 512], BF16, "et") for _ in range(4)]
        t_et = [Tok() for _ in range(4)]
        rden = A.tile([128, 512], F32, "rden"); t_rden = Tok()
        xst = [A.tile([128, 8, 512], BF16, "xst") for _ in range(2)]
        t_xst = [Tok(), Tok()]
        off_wkv = A.off
        wkv = A.tile([128, 8, 2048], BF16, "wkv")
        for c in range(8):
            dma("gpsimd", wkv[:, c, :], Wt["xa_wkv"][l, c * 128:(c + 1) * 128, :], w=[t_wkv[c]])
        for c in range(8):
            dma("gpsimd", wq[:, c, :], Wt["xa_wq"][l, c * 128:(c + 1) * 128, :], w=[t_wq[c]])
        for c in range(8):
            dma("gpsimd", wo[:, c, :], Wt["xa_wo"][l, c * 128:(c + 1) * 128, :], w=[t_wo[c]])
        dma("sync", G[:], Wt["ln2_g"][l].partition_broadcast(128), w=[t_gb])
        dma("sync", Bt[:], Wt["ln2_b"][l].partition_broadcast(128), w=[t_gb])
        for c in range(8):
            b = c % 2
            for d in range(8):
                mm(bank(b), wkv[:, d, c * 128:(c + 1) * 128], memT[:, d, :], d == 0, d == 7, r=[t_wkv[d], t_memT], w=[PT[b]])
            cp(KmT[:, c, :], bank(b), r=[PT[b]], w=[t_Km], eng="scalar" if b == 0 else "vector")
        for sb in range(4):
            for half in range(2):
                b = 2 + half
                for d in range(8):
                    mm(bank(b), memT[:, d, sb * 128:(sb + 1) * 128], wkv[:, d, 1024 + half * 512:1024 + (half + 1) * 512],
                       d == 0, d == 7, r=[t_wkv[d], t_memT], w=[PT[b]])
                cp(Vm[:, sb, half * 512:(half + 1) * 512], bank(b), r=[PT[b]], w=[t_Vm], eng="scalar" if half == 0 else "vector")
        P.barrier()
        A.off = off_wkv
        NR = 10
        sms = [small_sm() for _ in range(NR)]
        junks = [(A.tile([128, D], BF16, "junk"), Tok()) for _ in range(2)]
        xbs = [A.tile([128, D], F32, "xbr") for _ in range(NR)]
        t_xbs = [Tok() for _ in range(NR)]
        zbs = [A.tile([128, D], F32, "zbr") for _ in range(2)]
        t_zbs = [Tok(), Tok()]
        ei = [0]

        def chunk_front(ck):
            seq = ck // 4
            dma("sync", xt[:], XT16v[:, :, ck * 512:(ck + 1) * 512], r=[t_XT[ck]], w=[t_xt])
            for c in range(8):
                b = c % 2
                for d in range(8):
                    mm(bank(b), wq[:, d, c * 128:(c + 1) * 128], xt[:, d, :], d == 0, d == 7, r=[t_wq[d], t_xt], w=[PT[b]])
                cp(QxT[:, c, :], bank(b), r=[PT[b]], w=[t_Qx], eng="scalar" if b == 0 else "vector")
            skh = Skew(1)
            for h in range(4):
                es = []
                for kb in range(2):
                    es.append(ei[0] % 4)
                    ei[0] += 1

                def hfront(h=h, es=es):
                    for kb in range(2):
                        e_i = es[kb]
                        for cc in range(2):
                            mm(bank(2 + kb), KmT[:, 2 * h + cc, seq * 256 + kb * 128:seq * 256 + (kb + 1) * 128], QxT[:, 2 * h + cc, :],
                               cc == 0, cc == 1, r=[t_Km, t_Qx], w=[PT[2 + kb]])
                        act(et[e_i][:], bank(2 + kb), AF.Exp, r=[PT[2 + kb]], w=[t_et[e_i]], scale=1.0 / 16)

                def hback(h=h, es=es):
                    for kb in range(2):
                        mm(bank(4), ones_bf[:], et[es[kb]][:], kb == 0, kb == 1, r=[t_et[es[kb]], t_const], w=[PT[4]])
                    vop("reciprocal", out=rden[:], in_=bank(4), r=[PT[4]], w=[t_rden])
                    for dc in range(2):
                        for kb in range(2):
                            mm(bank(5 + dc), Vm[:, seq * 2 + kb, h * 256 + dc * 128:h * 256 + (dc + 1) * 128], et[es[kb]][:],
                               kb == 0, kb == 1, r=[t_et[es[kb]], t_Vm], w=[PT[5 + dc]])
                        tt(axT[:, 2 * h + dc, :], bank(5 + dc), rden[:], ALU.mult, r=[PT[5 + dc], t_rden], w=[t_ax])
                skh.push(hfront, hback)
            skh.flush()

        def s0(tb):
            ck, tbl = tb // 4, tb % 4
            kb = tb % NR
            if tbl == 0:
                chunk_front(ck)
            dma("sync", xbs[kb][:], X32[tb * 128:(tb + 1) * 128, :], r=[t_X32[tb]], w=[t_xbs[kb]])
            for half in range(2):
                for c in range(8):
                    mm(bank(half), axT[:, c, tbl * 128:(tbl + 1) * 128], wo[:, c, half * 512:(half + 1) * 512],
                       c == 0, c == 7, r=[t_ax, t_wo[c]], w=[PT[half]])
            stt(xbs[kb][:], xbs[kb][:], ALPHA, ps[:, 0:1024], ALU.mult, ALU.add, r=[t_xbs[kb], PT[0], PT[1]], w=[t_xbs[kb]])

        def s8(tb):
            ck, tbl = tb // 4, tb % 4
            k2 = ck % 2
            kb = tb % NR
            z2 = tb % 2
            dma("sync", X32[tb * 128:(tb + 1) * 128, :], xbs[kb][:], r=[t_xbs[kb]], w=[t_X32[tb]])
            act(zbs[z2][:], xbs[kb][:], AF.Copy, r=[t_xbs[kb]], w=[t_zbs[z2]], scale=ALPHA)
            emit_xT(xbs[kb], t_xbs[kb], xst[k2], t_xst[k2], tbl, 7, 6)

        def s9(tb):
            ck, tbl = tb // 4, tb % 4
            k2 = ck % 2
            z2 = tb % 2
            dma("sync", Z32[tb * 128:(tb + 1) * 128, :], zbs[z2][:], r=[t_zbs[z2]], w=[t_Z32[tb]])
            if tbl == 3:
                dma("sync", XT16v[:, :, ck * 512:(ck + 1) * 512], xst[k2][:], r=[t_xst[k2]], w=[t_XT[ck]])
        run_pipe(NB, [s0] + ln_hops(xbs, t_xbs, sms, junks, G, Bt, t_gb, NR) + [s8, s9])

    def stage_MOE(l):
        A.reset()
        wr = A.tile([128, 8, 16], BF16, "wr"); t_wr = Tok()
        dma("gpsimd", wr[:], Wt["moe_router"][l].rearrange("(c p) e -> p c e", p=128), w=[t_wr])
        o48 = A.tile([48, 1], F32, "o48")
        dma("sync", o48[:], Cn["o48"], w=[t_wr])
        xt = A.tile([128, 8, 512], BF16, "xt"); t_xt = Tok()
        AF48 = A.tile([128, 16, 48], F32, "AF48"); t_AF = Tok()
        P.op("vector", lambda e: e.memset(AF48[:], 0.0), w=[t_AF])
        ex = A.tile([128, 16], F32, "ex"); t_ex = Tok()
        sx = A.tile([128, 2], F32, "sx"); t_sx = Tok()
        AffT = A.tile([48, L], F32, "AffT"); t_AffT = Tok()
        MX = A.tile([48, 256], F32, "MX"); t_MX = Tok()
        IX = A.tile([48, 256], U32, "IX"); t_IX = Tok()
        IXF = A.tile([48, 256], F32, "IXF"); t_IXF = Tok()
        IDXT = A.tile([128, 2, 48], I32, "IDXT"); t_IDXT = Tok()
        GT = A.tile([128, 2, 48], F32, "GT"); t_GT = Tok()
        for ck in range(8):
            dma("sync", xt[:], XT16v[:, :, ck * 512:(ck + 1) * 512], w=[t_xt])
            for tbl in range(4):
                tb = ck * 4 + tbl
                seq, tbs = tb // 16, tb % 16
                for d in range(8):
                    mm(bank(0, 0, 16), xt[:, d, tbl * 128:(tbl + 1) * 128], wr[:, d, :], d == 0, d == 7, r=[t_xt, t_wr], w=[PT[0]])
                act(ex[:], bank(0, 0, 16), AF.Exp, r=[PT[0]], w=[t_ex, t_sx], accum_out=sx[:, 0:1])
                vop("reciprocal", out=sx[:, 1:2], in_=sx[:, 0:1], r=[t_sx], w=[t_sx])
                ts(AF48[:, tbs, seq * 32:seq * 32 + 16], ex[:], sx[:, 1:2], None, ALU.mult, r=[t_ex, t_sx], w=[t_AF])
        for tbs in range(16):
            b = 1 + tbs % 2
            tr(bank(b, 0, 128, 0, 48), AF48[:, tbs, :], r=[t_AF], w=[PT[b]])
            cp(AffT[:, tbs * 128:(tbs + 1) * 128], bank(b, 0, 128, 0, 48), r=[PT[b]], w=[t_AffT], eng="vector")
        for rd in range(32):
            sl = slice(rd * 8, rd * 8 + 8)
            vop("max", out=MX[:, sl], in_=AffT[:], r=[t_AffT], w=[t_MX])
            vop("max_index", out=IX[:, sl], in_max=MX[:, sl], in_values=AffT[:], r=[t_AffT, t_MX], w=[t_IX])
            vop("match_replace", out=AffT[:], in_to_replace=MX[:, sl], in_values=AffT[:], imm_value=-1.0, r=[t_MX, t_IX], w=[t_AffT])
        cp(IXF[:], IX[:], r=[t_IX], w=[t_IXF], eng="vector")
        ts(IXF[:], IXF[:], o48[:, 0:1], None, ALU.add, r=[t_IXF, t_wr], w=[t_IXF])
        for half in range(2):
            tr(bank(3, 0, 48), IXF[:, half * 128:(half + 1) * 128], r=[t_IXF], w=[PT[3]])
            cp(IDXT[:, half, :], bank(3, 0, 48), r=[PT[3]], w=[t_IDXT], eng="vector")
            tr(bank(4, 0, 48), MX[:, half * 128:(half + 1) * 128], r=[t_MX], w=[PT[4]])
            cp(GT[:, half, :], bank(4, 0, 48), r=[PT[4]], w=[t_GT], eng="vector")
        wg = [A.tile([128, 8, 512], BF16, "wg") for _ in range(2)]
        wu = [A.tile([128, 8, 512], BF16, "wu") for _ in range(2)]
        wd = [A.tile([128, 4, D], BF16, "wd") for _ in range(2)]
        t_w = [Tok(), Tok()]
        xg = [A.tile([128, D], F32, "xg") for _ in range(2)]
        t_xg = [Tok(), Tok()]
        xeT = A.tile([128, 8, 512], BF16, "xeT"); t_xe = Tok()
        sg = [A.tile([128, 512], F32, "sg") for _ in range(2)]
        t_sg = [Tok(), Tok()]
        hidT = A.tile([128, 4, 512], BF16, "hidT"); t_hid = Tok()
        yb = [A.tile([128, D], F32, "yb") for _ in range(2)]
        t_yb = [Tok(), Tok()]

        def load_w(e):
            k = e % 2
            dma("gpsimd", wg[k][:], Wt["moe_w_gate"][l, e].rearrange("(c p) f -> p c f", p=128), w=[t_w[k]])
            dma("gpsimd", wu[k][:], Wt["moe_w_up"][l, e].rearrange("(c p) f -> p c f", p=128), w=[t_w[k]])
            dma("gpsimd", wd[k][:], Wt["moe_w_down"][l, e].rearrange("(c p) n -> p c n", p=128), w=[t_w[k]])

        load_w(0)
        gi = 0
        for e in range(16):
            k = e % 2
            for sh in range(4):
                seq, half = sh // 2, sh % 2
                g2 = gi % 2
                gi += 1
                col = seq * 32 + e
                P.op("gpsimd", lambda eng, g2=g2, half=half, col=col: eng.indirect_dma_start(
                    out=xg[g2][:], out_offset=None, in_=X32,
                    in_offset=bass.IndirectOffsetOnAxis(ap=IDXT[:, half, col:col + 1], axis=0),
                    bounds_check=bcr(eng), oob_is_err=False), r=[t_IDXT], w=[t_xg[g2]], dma=True)
                emit_xT(xg[g2], t_xg[g2], xeT, t_xe, sh, 5 + 2 * 0, 6)
            if e + 1 < 16:
                load_w(e + 1)
            for fc in range(4):
                s2 = fc % 2
                for d in range(8):
                    mm(bank(1), wg[k][:, d, fc * 128:(fc + 1) * 128], xeT[:, d, :], d == 0, d == 7, r=[t_w[k], t_xe], w=[PT[1]])
                for d in range(8):
                    mm(bank(2), wu[k][:, d, fc * 128:(fc + 1) * 128], xeT[:, d, :], d == 0, d == 7, r=[t_w[k], t_xe], w=[PT[2]])
                act(sg[s2][:], bank(1), AF.Silu, r=[PT[1]], w=[t_sg[s2]])
                tt(hidT[:, fc, :], sg[s2][:], bank(2), ALU.mult, r=[t_sg[s2], PT[2]], w=[t_hid])
            for sh in range(4):
                seq, half = sh // 2, sh % 2
                y2 = sh % 2
                col = seq * 32 + e
                for h2 in range(2):
                    for fc in range(4):
                        mm(bank(3 + h2), hidT[:, fc, sh * 128:(sh + 1) * 128], wd[k][:, fc, h2 * 512:(h2 + 1) * 512],
                           fc == 0, fc == 3, r=[t_hid, t_w[k]], w=[PT[3 + h2]])
                ts(yb[y2][:], ps[:, 3 * 512:5 * 512], GT[:, half, col:col + 1], None, ALU.mult, r=[PT[3], PT[4], t_GT], w=[t_yb[y2]])
                P.op("gpsimd", lambda eng, y2=y2, half=half, col=col: eng.indirect_dma_start(
                    out=Z32, out_offset=bass.IndirectOffsetOnAxis(ap=IDXT[:, half, col:col + 1], axis=0),
                    in_=yb[y2][:], in_offset=None, compute_op=ALU.add,
                    bounds_check=bcr(eng), oob_is_err=False), r=[t_IDXT, t_yb[y2]], w=[t_Zs[seq]], dma=True)

    def stage_LN3(l, dst):
        A.reset()
        G, Bt, t_gb = load_gb("ln3_g", "ln3_b", l)
        NR = 12
        sms = [small_sm() for _ in range(NR)]
        junks = [(A.tile([128, D], BF16, "junk"), Tok()) for _ in range(2)]
        xbs = [A.tile([128, D], F32, "xbr") for _ in range(NR)]
        t_xbs = [Tok() for _ in range(NR)]
        xst = [A.tile([128, 8, 512], BF16, "xst") for _ in range(2)]
        t_xst = [Tok(), Tok()]

        def s0(tb):
            kb = tb % NR
            dma("sync", xbs[kb][:], Z32[tb * 128:(tb + 1) * 128, :], w=[t_xbs[kb]])

        def s8(tb):
            ck, tbl = tb // 4, tb % 4
            k2 = ck % 2
            kb = tb % NR
            dma("sync", dst[tb * 128:(tb + 1) * 128, :], xbs[kb][:], r=[t_xbs[kb]], w=[t_X32[tb]])
            bb = 2 * (tb % 4)
            emit_xT(xbs[kb], t_xbs[kb], xst[k2], t_xst[k2], tbl, bb, bb + 1)

        def s9(tb):
            ck, tbl = tb // 4, tb % 4
            k2 = ck % 2
            if tbl == 3:
                dma("sync", XT16v[:, :, ck * 512:(ck + 1) * 512], xst[k2][:], r=[t_xst[k2]], w=[t_XT[ck]])
        run_pipe(NB, [s0] + ln_hops(xbs, t_xbs, sms, junks, G, Bt, t_gb, NR) + [s8, s9])

    def stage_HY(l):
        A.reset()
        HS = A.tile([128, 16, 256], BF16, "HS"); t_HS = Tok()
        HD = A.tile([128, 16, 256], BF16, "HD"); t_HD = Tok()
        x0T = A.tile([128, 2, T], BF16, "x0T"); t_x0 = Tok()
        zT = A.tile([128, 2, T], BF16, "zT"); t_zT = Tok()
        z_tm = A.tile([128, 16, 512], BF16, "z_tm"); t_ztm = Tok()
        hbias = A.tile([128, 2], F32, "hbias"); t_hb = Tok()
        for j in range(2):
            dma("sync", hbias[:, j:j + 1], Wt["hy_bias"][l, j * 128:(j + 1) * 128].rearrange("(p o) -> p o", o=1), w=[t_hb])
        sub_mark = A.off
        featsT = A.tile([33, L], F32, "featsT"); t_f = Tok()
        dma("sync", featsT[:], Cn["featsT"], w=[t_f])
        negt = A.tile([128, 16], F32, "negt"); m0 = A.tile([128, 16], F32, "m0")
        dma("sync", negt[:], Cn["negt"], w=[t_f])
        dma("sync", m0[:], Cn["m0"], w=[t_f])
        w1 = A.tile([33, 64], F32, "w1"); w2 = A.tile([64, 64], F32, "w2"); w3 = A.tile([64, 64], F32, "w3")
        wout = A.tile([64, 512], F32, "wout")
        dma("sync", w1[:], Wt["hy_w1"][l], w=[t_f])
        dma("sync", w2[:], Wt["hy_w2"][l], w=[t_f])
        dma("sync", w3[:], Wt["hy_w3"][l], w=[t_f])
        dma("sync", wout[:], Wt["hy_wout"][l], w=[t_f])
        prm = A.tile([64, 8], F32, "prm"); t_prm = Tok()
        for i, nm in enumerate(("hy_b1", "hy_b2", "hy_b3", "hy_freq")):
            dma("sync", prm[:, i:i + 1], Wt[nm][l].rearrange("(p o) -> p o", o=1), w=[t_prm])
        ts(prm[:, 4:7], prm[:, 0:3], prm[:, 3:4], None, ALU.mult, r=[t_prm], w=[t_prm])
        AD = A.tile([128, 512], F32, "AD"); t_AD = Tok()
        dma("sync", AD[:], Wt["hy_decay"][l].rearrange("a c -> (a c)").partition_broadcast(128), w=[t_AD])
        act(AD[:], AD[:], AF.Abs, r=[t_AD], w=[t_AD])
        hA = A.tile([64, L], F32, "hA"); hB = A.tile([64, L], F32, "hB")
        t_hA, t_hB = Tok(), Tok()
        a1 = [A.tile([64, 512], F32, "a1") for _ in range(2)]
        a2 = [A.tile([64, 512], F32, "a2") for _ in range(2)]
        t_a = [Tok(), Tok()]
        chain = [(featsT, t_f, 33, w1, hA, t_hA), (hA, t_hA, 64, w2, hB, t_hB), (hB, t_hB, 64, w3, hA, t_hA)]
        for i, (src, t_src, kk, wi, dst, t_dst) in enumerate(chain):
            for nq in range(4):
                b = nq % 2
                mm(bank(b, 0, 512, 0, 64), wi[0:kk, :], src[0:kk, nq * 512:(nq + 1) * 512], True, True, r=[t_f, t_src], w=[PT[b]])
                ts(a1[b][:], bank(b, 0, 512, 0, 64), prm[:, 3:4], prm[:, 4 + i:5 + i], ALU.mult, ALU.add, r=[PT[b], t_prm], w=[t_a[b]])
                ts(a2[b][:], a1[b][:], 1.0 / TWO_PI, MAGIC, ALU.mult, ALU.add, r=[t_a[b]], w=[t_a[b]])
                ts(a2[b][:], a2[b][:], MAGIC, -TWO_PI, ALU.subtract, ALU.mult, r=[t_a[b]], w=[t_a[b]])
                tt(a1[b][:], a1[b][:], a2[b][:], ALU.add, r=[t_a[b]], w=[t_a[b]])
                ts(a1[b][:], a1[b][:], 3.1415925, -3.1415925, ALU.min, ALU.max, r=[t_a[b]], w=[t_a[b]])
                act(dst[:, nq * 512:(nq + 1) * 512], a1[b][:], AF.Sin, r=[t_a[b]], w=[t_dst])
        h3, t_h3 = hA, t_hA
        E = [A.tile([128, 512], F32, "E") for _ in range(2)]
        fl = [A.tile([128, 512], F32, "fl") for _ in range(2)]
        t_E = [Tok(), Tok()]
        t_fl = [Tok(), Tok()]
        for lb in range(16):
            b = 2 + lb % 2
            k = lb % 2
            mm(bank(b), h3[:, lb * 128:(lb + 1) * 128], wout[:], True, True, r=[t_h3, t_f], w=[PT[b]])
            act(E[k][:], AD[:], AF.Exp, r=[t_AD, t_f], w=[t_E[k]], scale=negt[:, lb:lb + 1])
            tt(fl[k][:], bank(b), E[k][:], ALU.mult, r=[PT[b], t_E[k]], w=[t_fl[k]])
            ts(fl[k][:, 256:512], fl[k][:, 256:512], m0[:, lb:lb + 1], None, ALU.mult, r=[t_fl[k], t_f], w=[t_fl[k]], eng="vector")
            tt(HS[:, lb, :], fl[k][:, 0:256], fl[k][:, 256:512], ALU.add, r=[t_fl[k]], w=[t_HS], eng="gpsimd")
            tt(HD[:, lb, :], fl[k][:, 256:512], fl[k][:, 0:256], ALU.subtract, r=[t_fl[k]], w=[t_HD], eng="vector")
        P.barrier()
        A.off = sub_mark
        cw = A.tile([128, 6, 4], F32, "cw"); t_cw = Tok()
        for cc in range(6):
            for k in range(3):
                dma("sync", cw[:, cc, k:k + 1], Wt["hy_conv_w"][l, k, cc * 128:(cc + 1) * 128].rearrange("(p o) -> p o", o=1), w=[t_cw])
            dma("sync", cw[:, cc, 3:4], Wt["hy_conv_b"][l, cc * 128:(cc + 1) * 128].rearrange("(p o) -> p o", o=1), w=[t_cw])
        ut = [A.tile([128, L], F32, "ut") for _ in range(2)]
        t_ut = [Tok(), Tok()]
        uc = [A.tile([128, L], F32, "uc") for _ in range(2)]
        t_uc = [Tok(), Tok()]
        zf = A.tile([128, L], F32, "zf"); t_zf = Tok()
        ui = [0]

        def conv(seq, cc, dstk):
            u = ui[0] % 2
            ui[0] += 1
            dma("sync", ut[u][:], UT[cc * 128:(cc + 1) * 128, seq * L:(seq + 1) * L], w=[t_ut[u]])
            o, t_o = uc[dstk], t_uc[dstk]
            ts(o[:], ut[u][:], cw[:, cc, 1:2], cw[:, cc, 3:4], ALU.mult, ALU.add, r=[t_ut[u], t_cw], w=[t_o])
            stt(o[:, 1:L], ut[u][:, 0:L - 1], cw[:, cc, 0:1], o[:, 1:L], ALU.mult, ALU.add, r=[t_ut[u], t_cw, t_o], w=[t_o])
            stt(o[:, 0:L - 1], ut[u][:, 1:L], cw[:, cc, 2:3], o[:, 0:L - 1], ALU.mult, ALU.add, r=[t_ut[u], t_cw, t_o], w=[t_o])

        for seq in range(2):
            for j in range(2):
                conv(seq, 2 + j, 0)
                conv(seq, 4 + j, 1)
                tt(zf[:], uc[0][:], uc[1][:], ALU.mult, r=[t_uc[0], t_uc[1]], w=[t_zf])
                cp(zT[:, j, seq * L:(seq + 1) * L], zf[:], r=[t_zf], w=[t_zT], eng="scalar")
                for g in range(4):
                    b = 4 + g % 2
                    for i in range(4):
                        tb = g * 4 + i
                        tr(bank(b, i * 128, (i + 1) * 128), zf[:, tb * 128:(tb + 1) * 128], r=[t_zf], w=[PT[b]])
                    c0 = seq * 256 + j * 128
                    cp(z_tm[:, g * 4:(g + 1) * 4, c0:c0 + 128], bank(b).rearrange("p (a t) -> p a t", a=4), r=[PT[b]], w=[t_ztm],
                       eng="vector" if g % 2 else "scalar")
                conv(seq, j, 0)
                cp(x0T[:, j, seq * L:(seq + 1) * L], uc[0][:], r=[t_uc[0]], w=[t_x0], eng="scalar")
        P.barrier()
        A.off = sub_mark
        Pre = A.tile([128, 16, 512], BF16, "Pre"); t_Pre = Tok()
        Pim = A.tile([128, 16, 512], BF16, "Pim"); t_Pim = Tok()
        cfb = [A.tile([128, 16, 128], BF16, "cfb") for _ in range(2)]
        sfb = [A.tile([128, 16, 128], BF16, "sfb") for _ in range(2)]
        t_cs = [Tok(), Tok()]
        hh = [A.tile([128, 512], F32, "hh") for _ in range(2)]
        t_hh = [Tok(), Tok()]
        _tq1 = A.tile([128, 4, 512], F32, "tq")
        tq_ = [_tq1, _tq1]
        _ttq = Tok()
        t_tq = [_ttq, _ttq]
        for fb in range(16):
            k = fb % 2
            b0 = 4 * k
            dma("sync", cfb[k][:], Cn["cf"][fb], w=[t_cs[k]])
            dma("sync", sfb[k][:], Cn["sf"][fb], w=[t_cs[k]])
            for lb in range(16):
                mm(bank(b0, 0, 256), cfb[k][:, lb, :], HS[:, lb, :], lb == 0, lb == 15, r=[t_cs[k]], w=[PT[b0]])
            for lb in range(16):
                mm(bank(b0 + 1, 0, 256), sfb[k][:, lb, :], HD[:, lb, :], lb == 0, lb == 15, r=[t_cs[k]], w=[PT[b0 + 1]])
            for tb in range(16):
                mm(bank(b0 + 2), cfb[k][:, tb, :], z_tm[:, tb, :], tb == 0, tb == 15, r=[t_cs[k]], w=[PT[b0 + 2]])
            for tb in range(16):
                mm(bank(b0 + 3), sfb[k][:, tb, :], z_tm[:, tb, :], tb == 0, tb == 15, r=[t_cs[k]], w=[PT[b0 + 3]])
            cp(hh[k][:, 0:256], bank(b0, 0, 256), r=[PT[b0]], w=[t_hh[k]], eng="vector")
            cp(hh[k][:, 256:512], bank(b0 + 1, 0, 256), r=[PT[b0 + 1]], w=[t_hh[k]], eng="vector")
            hre = hh[k][:, 0:256].unsqueeze(1).to_broadcast([128, 2, 256])
            him = hh[k][:, 256:512].unsqueeze(1).to_broadcast([128, 2, 256])
            Av = bank(b0 + 2).rearrange("p (s c) -> p s c", s=2)
            Bv = bank(b0 + 3).rearrange("p (s c) -> p s c", s=2)
            tv = [tq_[k][:, i, :].rearrange("p (s c) -> p s c", s=2) for i in range(4)]
            tt(tv[0], Av, hre, ALU.mult, r=[PT[b0 + 2], t_hh[k]], w=[t_tq[k]])
            tt(tv[1], Bv, him, ALU.mult, r=[PT[b0 + 3], t_hh[k]], w=[t_tq[k]])
            tt(tv[2], Bv, hre, ALU.mult, r=[PT[b0 + 3], t_hh[k]], w=[t_tq[k]])
            tt(tv[3], Av, him, ALU.mult, r=[PT[b0 + 2], t_hh[k]], w=[t_tq[k]])
            tt(Pre[:, fb, :], tq_[k][:, 0, :], tq_[k][:, 1, :], ALU.add, r=[t_tq[k]], w=[t_Pre], eng="gpsimd")
            tt(Pim[:, fb, :], tq_[k][:, 2, :], tq_[k][:, 3, :], ALU.subtract, r=[t_tq[k]], w=[t_Pim], eng="gpsimd")
        cit = A.tile([128, 16, 512], BF16, "cit")
        sit = A.tile([128, 16, 512], BF16, "sit")
        t_ci = Tok()
        tmp = [A.tile([128, 512], F32, "tmp") for _ in range(2)]
        t_tmp = [Tok(), Tok()]
        obt = [A.tile([128, 512], BF16, "obt") for _ in range(2)]
        t_obt = [Tok(), Tok()]
        n = 0
        for tq in range(4):
            dma("sync", cit[:], Cn["ci"][:, :, tq * 512:(tq + 1) * 512], w=[t_ci])
            dma("sync", sit[:], Cn["si"][:, :, tq * 512:(tq + 1) * 512], w=[t_ci])
            for sq in range(2):
                for j in range(2):
                    b = n % 2
                    n += 1
                    c0 = sq * 256 + j * 128
                    for kb in range(16):
                        mm(bank(b), Pre[:, kb, c0:c0 + 128], cit[:, kb, :], kb == 0, False, r=[t_Pre, t_ci], w=[PT[b]])
                        mm(bank(b), Pim[:, kb, c0:c0 + 128], sit[:, kb, :], False, kb == 15, r=[t_Pim, t_ci], w=[PT[b]])
                    t0 = sq * L + tq * 512
                    stt(tmp[b][:], zT[:, j, t0:t0 + 512], hbias[:, j:j + 1], bank(b), ALU.mult, ALU.add, r=[t_zT, t_hb, PT[b]], w=[t_tmp[b]])
                    tt(obt[b][:], tmp[b][:], x0T[:, j, t0:t0 + 512], ALU.mult, r=[t_tmp[b], t_x0], w=[t_obt[b]], eng="gpsimd")
                    dma("sync", AT[512 + j * 128:512 + (j + 1) * 128, t0:t0 + 512], obt[b][:], r=[t_obt[b]], w=[tAT((4 + j, t0 // 512))])

    stage_init()
    P.barrier()
    for l in range(NL if stop_after != "init" else 0):
        stage_A(l)
        P.barrier()
        if stop_after == "A":
            break
        if HY:
            stage_HY(l)
            P.barrier()
            if stop_after == "HY":
                break
        stage_ATT()
        P.barrier()
        if stop_after == "ATT":
            break
        stage_WO(l, x_in if l == 0 else X32, HY)
        P.barrier()
        if stop_after == "WO":
            break
        stage_XA(l)
        P.barrier()
        if stop_after == "XA":
            break
        stage_MOE(l)
        P.barrier()
        last = (l == NL - 1)
        stage_LN3(l, out if last else X32)
        P.barrier()
    if stop_after in ("WO", "XA"):
        stage_final()
    P.emit(st)
    st.close()
    return nc


_CONSTS = None


def _run(inputs, NL=DEPTH, dbg=(), stop_after=None, cores=NCORES):
    global _CONSTS
    if _CONSTS is None:
        _CONSTS = _host_consts()
    nc = build(NL, dbg, stop_after)
    x = np.ascontiguousarray(np.asarray(inputs["x"], dtype=np.float32)).reshape(16, L, D)
    mem = np.ascontiguousarray(np.asarray(inputs["mem"], dtype=np.float32)).reshape(16, 256, D)
    in_maps = []
    for c in range(cores):
        m = {"x": x[2 * c:2 * c + 2].reshape(T, D), "mem": mem[2 * c:2 * c + 2].reshape(512, D)}
        for k in _W_SHAPES:
            m[k] = np.ascontiguousarray(np.asarray(inputs[k], dtype=np.float32))
        for k in _CONST_SHAPES:
            m[k] = _CONSTS[k]
        in_maps.append(m)
    res = run_bass_kernel_spmd(nc, in_maps, core_ids=list(range(cores)))
    return res.results


def kernel(**inputs):
    res = _run(inputs)
    out = np.stack([r["out"].reshape(2, L, D) for r in res], axis=0).reshape(16, L, D)
    return out.astype(np.float32)
```

```python
import contextlib
import os
CUT = int(os.environ.get('A_CUT', '99'))
SUB = int(os.environ.get('A_SUB', '99'))
HY = int(os.environ.get('A_HY', '1'))
import math
import numpy as np
import ml_dtypes
import concourse.bass as bass
import concourse.mybir as mybir
from concourse.bass_utils import run_bass_kernel_spmd

F32 = mybir.dt.float32
BF16 = mybir.dt.bfloat16
I32 = mybir.dt.int32
U32 = mybir.dt.uint32
ALU = mybir.AluOpType
AF = mybir.ActivationFunctionType
AX = mybir.AxisListType

NCORES = 8
L = 2048
T = 4096
D = 1024
NB = 32
DEPTH = 4
ALPHA = float((2 * DEPTH) ** 0.25)
RMS_EPS = 1e-6
LN_EPS = 1e-5
TWO_PI = 2.0 * math.pi
MAGIC = 12582912.0


class Tok:
    __slots__ = ("w", "r")

    def __init__(self):
        self.w = None
        self.r = {}


class Prog:
    ENG = ["tensor", "vector", "scalar", "gpsimd", "sync"]
    NDMA = 8

    def __init__(self, nc):
        self.nc = nc
        self.ops = {e: [] for e in self.ENG}
        self.cnt = {e: 0 for e in self.ENG}
        self.seen = {e: {} for e in self.ENG}
        self.dma_n = {e: 0 for e in self.ENG}
        self.last = {}

    def op(self, eng, fn, r=(), w=(), dma=False):
        deps = {}

        def add(key, val):
            if deps.get(key, 0) < val:
                deps[key] = val

        for t in r:
            if t.w is not None:
                add(*t.w)
        for t in w:
            if t.w is not None:
                add(*t.w)
            for k, v in t.r.items():
                add(k, v)
        if dma:
            j = self.dma_n[eng]
            self.dma_n[eng] += 1
            slot = j % self.NDMA
            k = j // self.NDMA + 1
            me = ((eng, "dma", slot), 16 * k)
            if k > 1:
                add((eng, "dma", slot), 16 * (k - 1))
        else:
            self.cnt[eng] += 1
            me = ((eng, "c"), self.cnt[eng])
        seen = self.seen[eng]
        waits = []
        for key, val in deps.items():
            if seen.get(key, 0) >= val:
                continue
            seen[key] = val
            waits.append((key, val))
        self.ops[eng].append((fn, waits, me))
        self.last[me[0]] = me[1]
        for t in r:
            if t.r.get(me[0], 0) < me[1]:
                t.r[me[0]] = me[1]
        for t in w:
            t.w = me
            t.r = {}
        return me

    def barrier(self):
        snap = dict(self.last)
        for e in self.ENG:
            seen = self.seen[e]
            waits = []
            for key, val in snap.items():
                if seen.get(key, 0) >= val:
                    continue
                seen[key] = val
                waits.append((key, val))
            if waits:
                self.ops[e].append((None, waits, None))

    def emit(self, stack):
        nc = self.nc
        self.barrier()
        sems = {}
        for e in self.ENG:
            for fn, waits, me in self.ops[e]:
                for key, _ in waits:
                    if key not in sems:
                        sems[key] = None
                if me is not None and me[0] not in sems:
                    sems[me[0]] = None
        for key in sems:
            sems[key] = stack.enter_context(nc.semaphore("s_" + "_".join(str(x) for x in key)))
        block = stack.enter_context(nc.Block())

        def run(e):
            def body(eng):
                for fn, waits, me in self.ops[e]:
                    for key, val in waits:
                        eng.wait_ge(sems[key], val)
                    if fn is not None:
                        ins = fn(eng)
                        ins.then_inc(sems[me[0]], 16 if me[0][1] == "dma" else 1)
            return body

        block.tensor(run("tensor"))
        block.vector(run("vector"))
        block.scalar(run("scalar"))
        block.gpsimd(run("gpsimd"))
        block.sync(run("sync"))


def _esz(dt):
    return 2 if dt == BF16 else 4


class Arena:
    BASE = 17408
    LIMIT = 229376

    def __init__(self, nc):
        self.nc = nc
        self.off = self.BASE
        self.mark = self.BASE
        self.n = 0

    def reset(self):
        self.off = self.mark

    def tile(self, shape, dt, name="t"):
        sz = int(np.prod(shape[1:])) * _esz(dt)
        self.n += 1
        t = self.nc.alloc_sbuf_tensor_at(f"{name}_{self.n}", list(shape), dt, offset=self.off)
        self.off += (sz + 63) // 64 * 64
        assert self.off <= self.LIMIT, (name, self.off)
        return t


def _host_consts():
    c = {}
    c["ident"] = np.eye(128, dtype=np.float32)
    t = np.arange(L)
    row = (t // 64).astype(np.float64)
    col = (t % 64).astype(np.float64)

    def tab(n):
        inv = 10000.0 ** (-(np.arange(n, dtype=np.float64) * 2.0 / (2 * n)))
        ang = np.stack([row[:, None] * inv[None], col[:, None] * inv[None]], axis=1)
        cs = np.cos(ang).astype(np.float32).reshape(16, 128, 2 * n).transpose(1, 0, 2)
        sn = np.sin(ang).astype(np.float32).reshape(16, 128, 2 * n).transpose(1, 0, 2)
        return np.ascontiguousarray(cs), np.ascontiguousarray(sn)

    c["cq"], c["sq"] = tab(16)
    c["cm"], c["sm"] = tab(8)
    tt = np.linspace(0.0, 1.0, L, dtype=np.float32)[:, None]
    bands = 16
    fb = np.linspace(1e-4, bands - 1, bands, dtype=np.float32)[None, :]
    w = (2.0 * math.pi * np.arange(L, dtype=np.float32)[:, None] / L).astype(np.float32)
    feats = np.concatenate([tt, np.cos(fb * w), -np.sin(fb * w)], axis=-1).astype(np.float32)
    c["featsT"] = np.ascontiguousarray(feats.T)
    c["negt"] = np.ascontiguousarray((-tt[:, 0]).reshape(16, 128).T).astype(np.float32)
    m0 = np.ones((128, 16), np.float32)
    m0[0, 0] = 0.0
    c["m0"] = m0
    k = np.arange(L, dtype=np.int64)
    n = np.arange(L, dtype=np.int64)
    ph = ((2 * k[None, :] + 1) * n[:, None]) % 8192
    ang = ph.astype(np.float64) * (math.pi / 4096.0)
    C = np.cos(ang)
    S = np.sin(ang)
    def fwd(M):
        return np.ascontiguousarray(M.reshape(16, 128, 16, 128).transpose(2, 1, 0, 3)).astype(ml_dtypes.bfloat16)
    c["cf"] = fwd(C)
    c["sf"] = fwd(S)
    def inv(M):
        return np.ascontiguousarray((M.T / 2048.0).reshape(16, 128, L).transpose(1, 0, 2)).astype(ml_dtypes.bfloat16)
    c["ci"] = inv(C)
    c["si"] = inv(S)
    o48 = np.zeros((48, 1), np.float32)
    o48[32:] = 2048.0
    c["o48"] = o48
    return c


_CONST_SHAPES = {
    "ident": ([128, 128], F32), "cq": ([128, 16, 32], F32), "sq": ([128, 16, 32], F32),
    "cm": ([128, 16, 16], F32), "sm": ([128, 16, 16], F32), "featsT": ([33, L], F32),
    "negt": ([128, 16], F32), "m0": ([128, 16], F32),
    "cf": ([16, 128, 16, 128], BF16), "sf": ([16, 128, 16, 128], BF16),
    "ci": ([128, 16, L], BF16), "si": ([128, 16, L], BF16), "o48": ([48, 1], F32),
}

_W_SHAPES = {
    "w_in": [4, 1024, 1952], "gqa_q_norm": [4, 64], "gqa_k_norm": [4, 64], "hy_conv_w": [4, 3, 768],
    "hy_conv_b": [4, 768], "hy_w1": [4, 33, 64], "hy_b1": [4, 64], "hy_w2": [4, 64, 64], "hy_b2": [4, 64],
    "hy_w3": [4, 64, 64], "hy_b3": [4, 64], "hy_wout": [4, 64, 512], "hy_freq": [4, 64],
    "hy_decay": [4, 2, 256], "hy_bias": [4, 256], "mla_q_norm": [4, 256], "mla_w_uq": [4, 256, 384],
    "mla_kv_norm": [4, 128], "mla_w_ukv": [4, 128, 512], "w_o": [4, 1024, 1024], "ln1_g": [4, 1024],
    "ln1_b": [4, 1024], "xa_wq": [4, 1024, 1024], "xa_wkv": [4, 1024, 2048], "xa_wo": [4, 1024, 1024],
    "ln2_g": [4, 1024], "ln2_b": [4, 1024], "moe_router": [4, 1024, 16], "moe_w_gate": [4, 16, 1024, 512],
    "moe_w_up": [4, 16, 1024, 512], "moe_w_down": [4, 16, 512, 1024], "ln3_g": [4, 1024], "ln3_b": [4, 1024],
}


def build(NL=DEPTH, dbg=(), stop_after=None):
    nc = bass.Bass("TRN2", target_bir_lowering=False)

    def din(name, shape, dt=F32):
        return nc.dram_tensor(name, list(shape), dt, kind="ExternalInput").ap()

    x_in = din("x", [T, D])
    mem_in = din("mem", [512, D])
    Wt = {k: din(k, s) for k, s in _W_SHAPES.items()}
    Cn = {k: din(k, s, dt) for k, (s, dt) in _CONST_SHAPES.items()}
    out = nc.dram_tensor("out", [T, D], F32, kind="ExternalOutput").ap()

    def dscr(name, shape, dt):
        kind = "ExternalOutput" if name in dbg else "Internal"
        return nc.dram_tensor(name, list(shape), dt, kind=kind).ap()

    X32 = dscr("X32", [T, D], F32)
    Z32 = dscr("Z32", [T, D], F32)
    XT16 = dscr("XT16", [D, T], BF16)
    QTd = dscr("QTd", [6, 128, T], BF16)
    QCTd = dscr("QCTd", [4, 96, T], BF16)
    KCTd = dscr("KCTd", [4, 96, T], BF16)
    UT = dscr("UT", [768, T], F32)
    AT = dscr("AT", [D, T], BF16)
    XT16v = XT16.rearrange("(c p) t -> p c t", p=128)
    ATv = AT.rearrange("(c p) t -> p c t", p=128)
    t_X32 = [Tok() for _ in range(NB)]
    t_Z32 = [Tok() for _ in range(NB)]
    t_Zs = [Tok(), Tok()]
    t_XT = [Tok() for _ in range(8)]
    t_QT = [Tok() for _ in range(8)]
    t_QCT = [Tok() for _ in range(8)]
    t_KCT = [Tok() for _ in range(8)]
    t_UT = [Tok() for _ in range(8)]
    t_AT = {}

    def tAT(key):
        if key not in t_AT:
            t_AT[key] = Tok()
        return t_AT[key]

    P = Prog(nc)
    A = Arena(nc)
    _bc = {}

    def bcr(eng):
        if "r" not in _bc:
            _bc["r"] = eng.to_reg(T - 1)
        return _bc["r"]
    st = contextlib.ExitStack()
    ps = st.enter_context(nc.psum_tensor("ps", [128, 4096], F32))
    PT = [Tok() for _ in range(8)]

    def bank(b, lo=0, hi=512, p0=0, p1=128):
        return ps[p0:p1, b * 512 + lo:b * 512 + hi]

    def dma(q, out_, in_, r=(), w=()):
        P.op(q, lambda e: e.dma_start(out=out_, in_=in_), r=r, w=w, dma=True)

    def mm(out_, lhsT, rhs, start, stop, r=(), w=()):
        P.op("tensor", lambda e: e.matmul(out_, lhsT=lhsT, rhs=rhs, start=start, stop=stop), r=r, w=w)

    def tr(out_, in_, r=(), w=()):
        pp = in_.shape[0]
        P.op("tensor", lambda e: e.transpose(out=out_, in_=in_, identity=ident[0:pp, 0:pp]), r=list(r) + [t_const], w=w)

    def act(out_, in_, func, r=(), w=(), **kw):
        P.op("scalar", lambda e: e.activation(out=out_, in_=in_, func=func, **kw), r=r, w=w)

    def vop(name, r=(), w=(), eng="vector", **kw):
        P.op(eng, lambda e: getattr(e, name)(**kw), r=r, w=w)

    def tt(out_, in0, in1, op, r=(), w=(), eng="vector"):
        P.op(eng, lambda e: e.tensor_tensor(out=out_, in0=in0, in1=in1, op=op), r=r, w=w)

    def ts(out_, in0, s1, s2, op0, op1=None, r=(), w=(), eng="vector"):
        if op1 is None:
            P.op(eng, lambda e: e.tensor_scalar(out=out_, in0=in0, scalar1=s1, scalar2=None, op0=op0), r=r, w=w)
        else:
            P.op(eng, lambda e: e.tensor_scalar(out=out_, in0=in0, scalar1=s1, scalar2=s2, op0=op0, op1=op1), r=r, w=w)

    def stt(out_, in0, scalar, in1, op0, op1, r=(), w=(), eng="vector"):
        P.op(eng, lambda e: e.scalar_tensor_tensor(out=out_, in0=in0, scalar=scalar, in1=in1, op0=op0, op1=op1), r=r, w=w)

    def cp(out_, in_, r=(), w=(), eng="vector"):
        if eng == "scalar":
            P.op("scalar", lambda e: e.copy(out=out_, in_=in_), r=r, w=w)
        else:
            P.op(eng, lambda e: e.tensor_copy(out=out_, in_=in_), r=r, w=w)

    t_const = Tok()
    ident = A.tile([128, 128], F32, "ident")
    dma("sync", ident[:], Cn["ident"], w=[t_const])
    ones_bf = A.tile([128, 128], BF16, "ones")
    P.op("vector", lambda e: e.memset(ones_bf[:], 1.0), w=[t_const])
    ones_f = A.tile([128, 64], F32, "ones_f")
    P.op("vector", lambda e: e.memset(ones_f[:], 1.0), w=[t_const])
    CQ = A.tile([128, 16, 32], F32, "CQ"); SQ = A.tile([128, 16, 32], F32, "SQ")
    CM = A.tile([128, 16, 16], F32, "CM"); SM = A.tile([128, 16, 16], F32, "SM")
    for tl, nm in ((CQ, "cq"), (SQ, "sq"), (CM, "cm"), (SM, "sm")):
        dma("sync", tl[:], Cn[nm], w=[t_const])
    VA = A.tile([128, NB, 2, 65], BF16, "VA")
    VC = A.tile([128, NB, 4, 65], BF16, "VC")
    t_VA = [Tok() for _ in range(NB)]
    t_VC = [Tok() for _ in range(NB)]
    P.op("vector", lambda e: e.memset(VA[:], 1.0), w=t_VA)
    P.op("vector", lambda e: e.memset(VC[:], 1.0), w=t_VC)
    memT = A.tile([128, 8, 512], BF16, "memT")
    t_memT = Tok()
    A.mark = A.off

    def emit_xT(src, t_src, xst, t_xst, tbl, b0, b1):
        for c in range(8):
            b = b0 if c < 4 else b1
            tr(bank(b, (c % 4) * 128, (c % 4) * 128 + 128), src[:, c * 128:(c + 1) * 128], r=[t_src], w=[PT[b]])
        cp(xst[:, 0:4, tbl * 128:(tbl + 1) * 128], bank(b0).rearrange("p (c t) -> p c t", c=4),
           r=[PT[b0]], w=[t_xst], eng="scalar")
        cp(xst[:, 4:8, tbl * 128:(tbl + 1) * 128], bank(b1).rearrange("p (c t) -> p c t", c=4),
           r=[PT[b1]], w=[t_xst], eng="vector")

    def emit_ln(zt, t_z, G, Bt, t_gb, sm, t_sm):
        junk = sm["junk"]
        act(junk[:], zt[:], AF.Copy, r=[t_z], w=[sm["tj"], t_sm], scale=1.0 / D, accum_out=sm["s"][:, 2:3])
        act(junk[:], zt[:], AF.Square, r=[t_z], w=[sm["tj"], t_sm], scale=float(D ** -0.5), accum_out=sm["s"][:, 1:2])
        stt(sm["s"][:, 4:5], sm["s"][:, 2:3], sm["s"][:, 2:3], sm["s"][:, 1:2], ALU.mult, ALU.subtract, r=[t_sm], w=[t_sm])
        act(sm["s"][:, 5:6], sm["s"][:, 4:5], AF.Sqrt, r=[t_sm], w=[t_sm], bias=LN_EPS, scale=-1.0)
        vop("reciprocal", out=sm["s"][:, 5:6], in_=sm["s"][:, 5:6], r=[t_sm], w=[t_sm])
        stt(sm["s"][:, 6:7], sm["s"][:, 2:3], -1.0, sm["s"][:, 5:6], ALU.mult, ALU.mult, r=[t_sm], w=[t_sm])
        act(zt[:], zt[:], AF.Identity, r=[t_z, t_sm], w=[t_z], scale=sm["s"][:, 5:6], bias=sm["s"][:, 6:7])
        tt(zt[:], zt[:], G[:], ALU.mult, r=[t_z, t_gb], w=[t_z], eng="vector")
        tt(zt[:], zt[:], Bt[:], ALU.add, r=[t_z, t_gb], w=[t_z], eng="gpsimd")

    def load_gb(gname, bname, l):
        G = A.tile([128, D], F32, "G"); Bt = A.tile([128, D], F32, "B")
        t_gb = Tok()
        dma("sync", G[:], Wt[gname][l].partition_broadcast(128), w=[t_gb])
        dma("sync", Bt[:], Wt[bname][l].partition_broadcast(128), w=[t_gb])
        return G, Bt, t_gb

    def ln_scratch():
        return {"s": A.tile([128, 8], F32, "lns"), "junk": A.tile([128, D], BF16, "junk"), "tj": Tok()}, Tok()


    def run_pipe(n, stages):
        K = len(stages)
        for i in range(n + K - 1):
            for k, f in enumerate(stages):
                j = i - k
                if 0 <= j < n:
                    f(j)

    def ln_hops(xbs, t_xbs, sms, junks, G, Bt, t_gb, NR):
        def h1(tb):
            kb = tb % NR
            sm, t_sm = sms[kb]
            jk, t_jk = junks[tb % 2]
            act(jk[:], xbs[kb][:], AF.Copy, r=[t_xbs[kb]], w=[t_jk, t_sm], scale=1.0 / D, accum_out=sm["s"][:, 2:3])
            act(jk[:], xbs[kb][:], AF.Square, r=[t_xbs[kb]], w=[t_jk, t_sm], scale=float(D ** -0.5), accum_out=sm["s"][:, 1:2])

        def h2(tb):
            sm, t_sm = sms[tb % NR]
            stt(sm["s"][:, 4:5], sm["s"][:, 2:3], sm["s"][:, 2:3], sm["s"][:, 1:2], ALU.mult, ALU.subtract, r=[t_sm], w=[t_sm])

        def h3(tb):
            sm, t_sm = sms[tb % NR]
            act(sm["s"][:, 5:6], sm["s"][:, 4:5], AF.Sqrt, r=[t_sm], w=[t_sm], bias=LN_EPS, scale=-1.0)

        def h4(tb):
            sm, t_sm = sms[tb % NR]
            vop("reciprocal", out=sm["s"][:, 5:6], in_=sm["s"][:, 5:6], r=[t_sm], w=[t_sm])
            stt(sm["s"][:, 6:7], sm["s"][:, 2:3], -1.0, sm["s"][:, 5:6], ALU.mult, ALU.mult, r=[t_sm], w=[t_sm])

        def h5(tb):
            kb = tb % NR
            sm, t_sm = sms[kb]
            act(xbs[kb][:], xbs[kb][:], AF.Identity, r=[t_xbs[kb], t_sm], w=[t_xbs[kb]], scale=sm["s"][:, 5:6], bias=sm["s"][:, 6:7])

        def h6(tb):
            kb = tb % NR
            tt(xbs[kb][:], xbs[kb][:], G[:], ALU.mult, r=[t_xbs[kb], t_gb], w=[t_xbs[kb]], eng="vector")

        def h7(tb):
            kb = tb % NR
            tt(xbs[kb][:], xbs[kb][:], Bt[:], ALU.add, r=[t_xbs[kb], t_gb], w=[t_xbs[kb]], eng="gpsimd")
        return [h1, h2, h3, h4, h5, h6, h7]

    def small_sm():
        return {"s": A.tile([128, 8], F32, "lns")}, Tok()

    class Skew:
        def __init__(self, sk):
            self.sk = sk
            self.q = []

        def push(self, front, back):
            front()
            self.q.append(back)
            while len(self.q) > self.sk:
                self.q.pop(0)()

        def flush(self):
            while self.q:
                self.q.pop(0)()

    def emit_rope(xg, t_x, H, n, ct, st_, t1, t2, ro, t_t1, t_t2, t_ro):
        def v5(tl):
            return tl[:].rearrange("p (h f j i) -> p h f j i", h=H, f=2, j=2)
        for f in range(2):
            cb = ct[:, f, :].unsqueeze(1).unsqueeze(1).to_broadcast([128, H, 2, n])
            sb = st_[:, f, :].unsqueeze(1).to_broadcast([128, H, n])
            tt(v5(t1)[:, :, f], v5(xg)[:, :, f], cb, ALU.mult, r=[t_x, t_const], w=[t_t1], eng="vector")
            tt(v5(t2)[:, :, f, 0, :], v5(xg)[:, :, f, 1, :], sb, ALU.mult, r=[t_x, t_const], w=[t_t2], eng="gpsimd")
            tt(v5(t2)[:, :, f, 1, :], v5(xg)[:, :, f, 0, :], sb, ALU.mult, r=[t_x, t_const], w=[t_t2], eng="gpsimd")

        def v4(tl):
            return tl[:].rearrange("p (hf j i) -> p hf j i", j=2, i=n)
        tt(v4(ro)[:, :, 0, :], v4(t1)[:, :, 0, :], v4(t2)[:, :, 0, :], ALU.subtract, r=[t_t1, t_t2], w=[t_ro], eng="vector")
        tt(v4(ro)[:, :, 1, :], v4(t1)[:, :, 1, :], v4(t2)[:, :, 1, :], ALU.add, r=[t_t1, t_t2], w=[t_ro], eng="gpsimd")

    def stage_init():
        A.reset()
        xb = [A.tile([128, D], F32, "xb") for _ in range(2)]
        t_xb = [Tok(), Tok()]
        xst = [A.tile([128, 8, 512], BF16, "xst") for _ in range(2)]
        t_xst = [Tok(), Tok()]
        for i in range(4):
            k = i % 2
            dma("sync", xb[k][:], mem_in[i * 128:(i + 1) * 128, :], w=[t_xb[k]])
            for c in range(8):
                b = 0 if c < 4 else 1
                tr(bank(b, (c % 4) * 128, (c % 4) * 128 + 128), xb[k][:, c * 128:(c + 1) * 128], r=[t_xb[k]], w=[PT[b]])
            cp(memT[:, 0:4, i * 128:(i + 1) * 128], bank(0).rearrange("p (c t) -> p c t", c=4), r=[PT[0]], w=[t_memT], eng="scalar")
            cp(memT[:, 4:8, i * 128:(i + 1) * 128], bank(1).rearrange("p (c t) -> p c t", c=4), r=[PT[1]], w=[t_memT], eng="vector")
        for tb in range(NB):
            k = tb % 2
            ck, tbl = tb // 4, tb % 4
            dma("sync", xb[k][:], x_in[tb * 128:(tb + 1) * 128, :], w=[t_xb[k]])
            emit_xT(xb[k], t_xb[k], xst[ck % 2], t_xst[ck % 2], tbl, 2 + 2 * k, 3 + 2 * k)
            if tbl == 3:
                dma("sync", XT16v[:, :, ck * 512:(ck + 1) * 512], xst[ck % 2][:], r=[t_xst[ck % 2]], w=[t_XT[ck]])

    def stage_A(l):
        A.reset()
        win = A.tile([128, 8, 1952], BF16, "win")
        t_win = [Tok() for _ in range(8)]
        for c in range(8):
            dma("gpsimd", win[:, c, :], Wt["w_in"][l, c * 128:(c + 1) * 128, :], w=[t_win[c]])
        t_par = Tok()
        G10 = A.tile([128, 640], F32, "G10")
        for h in range(10):
            src = Wt["gqa_q_norm"][l] if h < 8 else Wt["gqa_k_norm"][l]
            dma("sync", G10[:, h * 64:(h + 1) * 64], src.partition_broadcast(128), w=[t_par])
        Gm = A.tile([128, 384], F32, "Gm")
        dma("sync", Gm[:, 0:256], Wt["mla_q_norm"][l].partition_broadcast(128), w=[t_par])
        dma("sync", Gm[:, 256:384], Wt["mla_kv_norm"][l].partition_broadcast(128), w=[t_par])
        wuq = A.tile([128, 2, 384], BF16, "wuq")
        wukv = A.tile([128, 512], BF16, "wukv")
        dma("gpsimd", wuq[:], Wt["mla_w_uq"][l].rearrange("(c p) n -> p c n", p=128), w=[t_par])
        dma("gpsimd", wukv[:], Wt["mla_w_ukv"][l], w=[t_par])

        xt = [A.tile([128, 8, 512], BF16, "xt") for _ in range(2)]
        t_xt = [Tok(), Tok()]
        uts = [A.tile([128, 512], F32, "uts") for _ in range(2)]
        t_uts = [Tok(), Tok()]
        QS = [A.tile([128, 6, 512], BF16, "QS") for _ in range(2)]
        t_QS = [Tok(), Tok()]
        QCS = [A.tile([96, 4, 512], BF16, "QCS") for _ in range(2)]
        t_QCS = [Tok(), Tok()]
        KCS = [A.tile([96, 4, 512], BF16, "KCS") for _ in range(2)]
        t_KCS = [Tok(), Tok()]
        sqt = A.tile([128, 640], F32, "sqt"); t_sqt = Tok()
        ss = A.tile([128, 16], F32, "ss"); t_ss = Tok()
        xg = A.tile([128, 640], F32, "xg"); t_xg = Tok()
        r1 = A.tile([128, 640], F32, "r1"); t_r1 = Tok()
        r2 = A.tile([128, 640], F32, "r2"); t_r2 = Tok()
        ro = A.tile([128, 640], F32, "ro"); t_ro = Tok()
        KK = A.tile([128, 256], F32, "KK"); t_KK = Tok()
        CN = A.tile([128, 384], F32, "CN"); t_CN = Tok()
        cnT = A.tile([128, 3, 128], BF16, "cnT"); t_cnT = Tok()
        RM = A.tile([128, 160], F32, "RM"); t_RM = Tok()
        m1 = A.tile([128, 160], F32, "m1"); t_m1 = Tok()
        m2 = A.tile([128, 160], F32, "m2"); t_m2 = Tok()
        mo = A.tile([128, 160], F32, "mo"); t_mo = Tok()
        QC = A.tile([128, 4, 96], F32, "QC"); t_QC = Tok()
        KC = A.tile([128, 4, 96], F32, "KC"); t_KC = Tok()

        ui = 0
        for ck in range(8):
            k2 = ck % 2
            dma("sync", xt[k2][:], XT16v[:, :, ck * 512:(ck + 1) * 512], r=[t_XT[ck]], w=[t_xt[k2]])
            for cc in range(6):
                for d in range(8):
                    mm(bank(3), win[:, d, 768 + cc * 128:768 + (cc + 1) * 128], xt[k2][:, d, :], d == 0, d == 7,
                       r=[t_win[d], t_xt[k2]], w=[PT[3]])
                u = ui % 2
                ui += 1
                cp(uts[u][:], bank(3), r=[PT[3]], w=[t_uts[u]], eng="scalar")
                dma("sync", UT[cc * 128:(cc + 1) * 128, ck * 512:(ck + 1) * 512], uts[u][:], r=[t_uts[u]], w=[t_UT[ck]])
            for tbl in range(4):
                tb = ck * 4 + tbl
                tb16 = tb % 16
                tc0, tc1 = tbl * 128, (tbl + 1) * 128
                for d in range(8):
                    mm(bank(0), xt[k2][:, d, tc0:tc1], win[:, d, 0:512], d == 0, d == 7, r=[t_win[d], t_xt[k2]], w=[PT[0]])
                for d in range(8):
                    mm(bank(1, 0, 256), xt[k2][:, d, tc0:tc1], win[:, d, 512:768], d == 0, d == 7, r=[t_win[d], t_xt[k2]], w=[PT[1]])
                for d in range(8):
                    mm(bank(2, 0, 416), xt[k2][:, d, tc0:tc1], win[:, d, 1536:1952], d == 0, d == 7, r=[t_win[d], t_xt[k2]], w=[PT[2]])
                if CUT <= 1:
                    continue
                qk = ps[:, 0:640]
                act(sqt[:], qk, AF.Square, r=[PT[0], PT[1]], w=[t_sqt])
                vop("tensor_reduce", out=ss[:, 0:10], in_=sqt[:].rearrange("p (h d) -> p h d", d=64), axis=AX.X, op=ALU.add,
                    r=[t_sqt], w=[t_ss])
                act(ss[:, 0:10], ss[:, 0:10], AF.Sqrt, r=[t_ss], w=[t_ss], scale=1.0 / 64, bias=RMS_EPS)
                vop("reciprocal", out=ss[:, 0:10], in_=ss[:, 0:10], r=[t_ss], w=[t_ss])
                tt(xg[:].rearrange("p (h d) -> p h d", d=64), qk.rearrange("p (h d) -> p h d", d=64),
                   ss[:, 0:10].unsqueeze(2).to_broadcast([128, 10, 64]), ALU.mult, r=[PT[0], PT[1], t_ss], w=[t_xg])
                tt(xg[:], xg[:], G10[:], ALU.mult, r=[t_xg, t_par], w=[t_xg], eng="gpsimd")
                if CUT <= 2:
                    continue
                emit_rope(xg, t_xg, 10, 16, CQ[:, tb16, :].rearrange("p (f i) -> p f i", f=2),
                          SQ[:, tb16, :].rearrange("p (f i) -> p f i", f=2), r1, r2, ro, t_r1, t_r2, t_ro)
                if CUT <= 3:
                    continue
                for rr in range(2):
                    cp(KK[:].rearrange("p (h r d) -> p h r d", h=2, r=2)[:, :, rr, :],
                       ro[:, 512:640].rearrange("p (h d) -> p h d", h=2), r=[t_ro], w=[t_KK], eng="gpsimd")
                for j in range(6):
                    src = ro[:, j * 128:(j + 1) * 128] if j < 4 else KK[:, (j - 4) * 128:(j - 3) * 128]
                    b = 5 if j < 3 else 6
                    tr(bank(b, (j % 3) * 128, (j % 3) * 128 + 128), src, r=[t_ro, t_KK], w=[PT[b]])
                cp(QS[k2][:, 0:3, tc0:tc1], bank(5, 0, 384).rearrange("p (c t) -> p c t", c=3), r=[PT[5]], w=[t_QS[k2]], eng="scalar")
                cp(QS[k2][:, 3:6, tc0:tc1], bank(6, 0, 384).rearrange("p (c t) -> p c t", c=3), r=[PT[6]], w=[t_QS[k2]], eng="vector")
                if CUT <= 4:
                    continue
                cp(VA[:, tb, :, 0:64], bank(1, 128, 256).rearrange("p (h d) -> p h d", h=2), r=[PT[1]], w=[t_VA[tb]], eng="scalar")
                act(sqt[:, 0:384], bank(2, 0, 384), AF.Square, r=[PT[2]], w=[t_sqt])
                vop("tensor_reduce", out=ss[:, 10:11], in_=sqt[:, 0:256], axis=AX.X, op=ALU.add, r=[t_sqt], w=[t_ss])
                vop("tensor_reduce", out=ss[:, 11:12], in_=sqt[:, 256:384], axis=AX.X, op=ALU.add, r=[t_sqt], w=[t_ss])
                act(ss[:, 10:11], ss[:, 10:11], AF.Sqrt, r=[t_ss], w=[t_ss], scale=1.0 / 256, bias=RMS_EPS)
                act(ss[:, 11:12], ss[:, 11:12], AF.Sqrt, r=[t_ss], w=[t_ss], scale=1.0 / 128, bias=RMS_EPS)
                vop("reciprocal", out=ss[:, 10:12], in_=ss[:, 10:12], r=[t_ss], w=[t_ss])
                ts(CN[:, 0:256], bank(2, 0, 256), ss[:, 10:11], None, ALU.mult, r=[PT[2], t_ss], w=[t_CN])
                ts(CN[:, 256:384], bank(2, 256, 384), ss[:, 11:12], None, ALU.mult, r=[PT[2], t_ss], w=[t_CN])
                tt(CN[:], CN[:], Gm[:], ALU.mult, r=[t_CN, t_par], w=[t_CN], eng="gpsimd")
                for j in range(3):
                    tr(bank(7, j * 128, (j + 1) * 128), CN[:, j * 128:(j + 1) * 128], r=[t_CN], w=[PT[7]])
                cp(cnT[:], bank(7, 0, 384).rearrange("p (c t) -> p c t", c=3), r=[PT[7]], w=[t_cnT], eng="scalar")
                cp(RM[:, 128:160], bank(2, 384, 416), r=[PT[2]], w=[t_RM], eng="vector")
                if CUT <= 5:
                    continue
                for kc in range(2):
                    mm(bank(0, 0, 384), cnT[:, kc, :], wuq[:, kc, :], kc == 0, kc == 1, r=[t_cnT, t_par], w=[PT[0]])
                mm(bank(4), cnT[:, 2, :], wukv[:], True, True, r=[t_cnT, t_par], w=[PT[4]])
                if SUB <= 1:
                    continue
                qcv = bank(0, 0, 384).rearrange("p (h e) -> p h e", h=4)
                kvv = bank(4).rearrange("p (h e) -> p h e", h=4)
                cp(RM[:, 0:128].rearrange("p (h e) -> p h e", h=4), qcv[:, :, 64:96], r=[PT[0]], w=[t_RM], eng="vector")
                cp(QC[:, :, 0:64], qcv[:, :, 0:64], r=[PT[0]], w=[t_QC], eng="vector")
                if SUB <= 2:
                    continue
                cp(KC[:, :, 0:64], kvv[:, :, 0:64], r=[PT[4]], w=[t_KC], eng="vector")
                cp(VC[:, tb, :, 0:64], kvv[:, :, 64:128], r=[PT[4]], w=[t_VC[tb]], eng="scalar")
                if CUT <= 6:
                    continue
                emit_rope(RM, t_RM, 5, 8, CM[:, tb16, :].rearrange("p (f i) -> p f i", f=2),
                          SM[:, tb16, :].rearrange("p (f i) -> p f i", f=2), m1, m2, mo, t_m1, t_m2, t_mo)
                cp(QC[:, :, 64:96], mo[:, 0:128].rearrange("p (h e) -> p h e", h=4), r=[t_mo], w=[t_QC], eng="gpsimd")
                cp(KC[:, :, 64:96], mo[:, 128:160].unsqueeze(1).to_broadcast([128, 4, 32]), r=[t_mo], w=[t_KC], eng="gpsimd")
                if CUT <= 7:
                    continue
                for h in range(4):
                    tr(bank(1, h * 128, (h + 1) * 128, 0, 96), QC[:, h, :], r=[t_QC], w=[PT[1]])
                for h in range(4):
                    tr(bank(2, h * 128, (h + 1) * 128, 0, 96), KC[:, h, :], r=[t_KC], w=[PT[2]])
                cp(QCS[k2][:, :, tc0:tc1], bank(1, 0, 512, 0, 96).rearrange("p (c t) -> p c t", c=4), r=[PT[1]], w=[t_QCS[k2]], eng="scalar")
                cp(KCS[k2][:, :, tc0:tc1], bank(2, 0, 512, 0, 96).rearrange("p (c t) -> p c t", c=4), r=[PT[2]], w=[t_KCS[k2]], eng="vector")
            c0, c1 = ck * 512, (ck + 1) * 512
            dma("sync", QTd[:, :, c0:c1].rearrange("j p t -> p j t"), QS[k2][:], r=[t_QS[k2]], w=[t_QT[ck]])
            dma("sync", QCTd[:, :, c0:c1].rearrange("j p t -> p j t"), QCS[k2][:], r=[t_QCS[k2]], w=[t_QCT[ck]])
            dma("sync", KCTd[:, :, c0:c1].rearrange("j p t -> p j t"), KCS[k2][:], r=[t_KCS[k2]], w=[t_KCT[ck]])

    def stage_ATT():
        A.reset()
        qt = [A.tile([128, L], BF16, "qt") for _ in range(2)]
        kt = [A.tile([128, L], BF16, "kt") for _ in range(2)]
        t_qk = [Tok(), Tok()]
        NE = 6
        et = [A.tile([128, 512], BF16, "et") for _ in range(NE)]
        t_et = [Tok() for _ in range(NE)]
        rec = [A.tile([128, 512], F32, "rec") for _ in range(2)]
        t_rec = [Tok(), Tok()]
        bcs = [A.tile([64, 512], F32, "bcs") for _ in range(2)]
        t_bcs = [Tok(), Tok()]
        ot = [A.tile([64, 512], BF16, "ot") for _ in range(2)]
        t_ot = [Tok(), Tok()]
        items = []
        cnt = {"l": 0, "g": 0}

        def add_head(seq, li, pr0, pr1, vtile, t_v, hv, scale, arow, pre):
            qtile, ktile, t_in = qt[li], kt[li], t_qk[li]
            for qb in range(4):
                g = cnt["g"]
                cnt["g"] += 1
                ob = 4 + g % 2
                gi = g % 2
                for kb in range(16):
                    idx = len(items)
                    sb = idx % 4
                    ei = idx % NE

                    def qk(sb=sb, ei=ei, kb=kb, qb=qb):
                        mm(bank(sb), ktile[pr0:pr1, kb * 128:(kb + 1) * 128], qtile[pr0:pr1, qb * 512:(qb + 1) * 512], True, True,
                           r=[t_in], w=[PT[sb]])
                        act(et[ei][:], bank(sb), AF.Exp, r=[PT[sb]], w=[t_et[ei]], scale=scale)

                    def pv(ei=ei, kb=kb, ob=ob):
                        mm(bank(ob, 0, 512, 0, 65), vtile[:, seq * 16 + kb, hv, :], et[ei][:], kb == 0, kb == 15,
                           r=[t_et[ei], t_v[seq * 16 + kb]], w=[PT[ob]])

                    n1 = n2 = None
                    if kb == 15:
                        def n1(ob=ob, gi=gi):
                            vop("reciprocal", out=rec[gi][64:65, :], in_=bank(ob, 0, 512, 64, 65), r=[PT[ob]], w=[t_rec[gi]])

                        def n2(ob=ob, gi=gi, qb=qb):
                            mm(bank(6 + gi, 0, 512, 0, 64), ones_f[64:65, 0:64], rec[gi][64:65, :], True, True,
                               r=[t_rec[gi], t_const], w=[PT[6 + gi]])
                            cp(bcs[gi][:], bank(6 + gi, 0, 512, 0, 64), r=[PT[6 + gi]], w=[t_bcs[gi]], eng="vector")
                            tt(ot[gi][:], bank(ob, 0, 512, 0, 64), bcs[gi][:], ALU.mult, r=[PT[ob], t_bcs[gi]], w=[t_ot[gi]])
                            c0 = seq * L + qb * 512
                            dma("sync", AT[arow:arow + 64, c0:c0 + 512], ot[gi][:], r=[t_ot[gi]], w=[tAT((arow // 128, c0 // 512))])
                    items.append((pre if (qb == 0 and kb == 0) else None, qk, pv, n1, n2))

        for seq in range(2):
            rq = [t_QT[seq * 4 + i] for i in range(4)]
            for j in range(4):
                li = cnt["l"] % 2
                cnt["l"] += 1

                def pre(li=li, j=j, seq=seq, rq=rq):
                    dma("sync", qt[li][:], QTd[j, :, seq * L:(seq + 1) * L], r=rq, w=[t_qk[li]])
                    dma("sync", kt[li][:], QTd[4 + j // 2, :, seq * L:(seq + 1) * L], r=rq, w=[t_qk[li]])
                for hh in range(2):
                    add_head(seq, li, hh * 64, hh * 64 + 64, VA, t_VA, j // 2, 0.125, (2 * j + hh) * 64, pre if hh == 0 else None)
            rq2 = [t_QCT[seq * 4 + i] for i in range(4)] + [t_KCT[seq * 4 + i] for i in range(4)]
            for h in range(4):
                li = cnt["l"] % 2
                cnt["l"] += 1

                def pre(li=li, h=h, seq=seq, rq2=rq2):
                    dma("sync", qt[li][0:96, :], QCTd[h, :, seq * L:(seq + 1) * L], r=rq2, w=[t_qk[li]])
                    dma("sync", kt[li][0:96, :], KCTd[h, :, seq * L:(seq + 1) * L], r=rq2, w=[t_qk[li]])
                add_head(seq, li, 0, 96, VC, t_VC, h, float(96 ** -0.5), 768 + h * 64, pre)
        D1, D2 = 2, 5
        n = len(items)
        for i in range(n + D2 + 1):
            if i < n:
                if items[i][0] is not None:
                    items[i][0]()
                items[i][1]()
            if 0 <= i - D1 < n:
                items[i - D1][2]()
                if items[i - D1][3] is not None:
                    items[i - D1][3]()
            if 0 <= i - D2 < n and items[i - D2][4] is not None:
                items[i - D2][4]()

    def stage_WO(l, xsrc, have_hyena):
        A.reset()
        wo = A.tile([128, 8, D], BF16, "wo")
        t_wo = [Tok() for _ in range(8)]
        for c in range(8):
            dma("gpsimd", wo[:, c, :], Wt["w_o"][l, c * 128:(c + 1) * 128, :], w=[t_wo[c]])
        G, Bt, t_gb = load_gb("ln1_g", "ln1_b", l)
        at = [A.tile([128, 8, 512], BF16, "at") for _ in range(2)]
        t_at = [Tok(), Tok()]
        xst = [A.tile([128, 8, 512], BF16, "xst") for _ in range(2)]
        t_xst = [Tok(), Tok()]
        if not have_hyena:
            zt_ = A.tile([128, 512], BF16, "zero")
            t_z = Tok()
            P.op("vector", lambda e: e.memset(zt_[:], 0.0), w=[t_z])
            for c in (4, 5):
                for ck in range(8):
                    dma("sync", AT[c * 128:(c + 1) * 128, ck * 512:(ck + 1) * 512], zt_[:], r=[t_z], w=[tAT((c, ck))])
        NR = 12
        sms = [small_sm() for _ in range(NR)]
        junks = [(A.tile([128, D], BF16, "junk"), Tok()) for _ in range(2)]
        xbs = [A.tile([128, D], F32, "xbr") for _ in range(NR)]
        t_xbs = [Tok() for _ in range(NR)]

        def s0(tb):
            ck, tbl = tb // 4, tb % 4
            k2 = ck % 2
            kb = tb % NR
            yb_ = 2 * (tb % 2)
            if tbl == 0:
                dma("sync", at[k2][:], ATv[:, :, ck * 512:(ck + 1) * 512], r=[tAT((c, ck)) for c in range(8)], w=[t_at[k2]])
            dma("sync", xbs[kb][:], xsrc[tb * 128:(tb + 1) * 128, :], r=[t_X32[tb]], w=[t_xbs[kb]])
            for half in range(2):
                for c in range(8):
                    mm(bank(yb_ + half), at[k2][:, c, tbl * 128:(tbl + 1) * 128], wo[:, c, half * 512:(half + 1) * 512],
                       c == 0, c == 7, r=[t_at[k2], t_wo[c]], w=[PT[yb_ + half]])
            stt(xbs[kb][:], xbs[kb][:], ALPHA, ps[:, yb_ * 512:(yb_ + 2) * 512], ALU.mult, ALU.add,
                r=[t_xbs[kb], PT[yb_], PT[yb_ + 1]], w=[t_xbs[kb]])

        def s8(tb):
            ck, tbl = tb // 4, tb % 4
            k2 = ck % 2
            kb = tb % NR
            dma("sync", X32[tb * 128:(tb + 1) * 128, :], xbs[kb][:], r=[t_xbs[kb]], w=[t_X32[tb]])
            tbk = 4 + 2 * (tb % 2)
            emit_xT(xbs[kb], t_xbs[kb], xst[k2], t_xst[k2], tbl, tbk, tbk + 1)

        def s9(tb):
            ck, tbl = tb // 4, tb % 4
            k2 = ck % 2
            if tbl == 3:
                dma("sync", XT16v[:, :, ck * 512:(ck + 1) * 512], xst[k2][:], r=[t_xst[k2]], w=[t_XT[ck]])
        run_pipe(NB, [s0] + ln_hops(xbs, t_xbs, sms, junks, G, Bt, t_gb, NR) + [s8, s9])

    def stage_final():
        for tb in range(NB):
            dma("sync", out[tb * 128:(tb + 1) * 128, :], X32[tb * 128:(tb + 1) * 128, :], r=[t_X32[tb]])

    def stage_XA(l):
        A.reset()
        wq = A.tile([128, 8, D], BF16, "wq")
        wo = A.tile([128, 8, D], BF16, "wo")
        t_wkv = [Tok() for _ in range(8)]
        t_wq = [Tok() for _ in range(8)]
        t_wo = [Tok() for _ in range(8)]
        G = A.tile([128, D], F32, "G"); Bt = A.tile([128, D], F32, "B")
        t_gb = Tok()
        KmT = A.tile([128, 8, 512], BF16, "KmT"); t_Km = Tok()
        Vm = A.tile([128, 4, D], BF16, "Vm"); t_Vm = Tok()
        xt = A.tile([128, 8, 512], BF16, "xt"); t_xt = Tok()
        QxT = A.tile([128, 8, 512], BF16, "QxT"); t_Qx = Tok()
        axT = A.tile([128, 8, 512], BF16, "axT"); t_ax = Tok()
        et = [A.tile([128, 512], BF16, "et") for _ in range(4)]
        t_et = [Tok() for _ in range(4)]
        rden = A.tile([128, 512], F32, "rden"); t_rden = Tok()
        xst = [A.tile([128, 8, 512], BF16, "xst") for _ in range(2)]
        t_xst = [Tok(), Tok()]
        off_wkv = A.off
        wkv = A.tile([128, 8, 2048], BF16, "wkv")
        for c in range(8):
            dma("gpsimd", wkv[:, c, :], Wt["xa_wkv"][l, c * 128:(c + 1) * 128, :], w=[t_wkv[c]])
        for c in range(8):
            dma("gpsimd", wq[:, c, :], Wt["xa_wq"][l, c * 128:(c + 1) * 128, :], w=[t_wq[c]])
        for c in range(8):
            dma("gpsimd", wo[:, c, :], Wt["xa_wo"][l, c * 128:(c + 1) * 128, :], w=[t_wo[c]])
        dma("sync", G[:], Wt["ln2_g"][l].partition_broadcast(128), w=[t_gb])
        dma("sync", Bt[:], Wt["ln2_b"][l].partition_broadcast(128), w=[t_gb])
        for c in range(8):
            b = c % 2
            for d in range(8):
                mm(bank(b), wkv[:, d, c * 128:(c + 1) * 128], memT[:, d, :], d == 0, d == 7, r=[t_wkv[d], t_memT], w=[PT[b]])
            cp(KmT[:, c, :], bank(b), r=[PT[b]], w=[t_Km], eng="scalar" if b == 0 else "vector")
        for sb in range(4):
            for half in range(2):
                b = 2 + half
                for d in range(8):
                    mm(bank(b), memT[:, d, sb * 128:(sb + 1) * 128], wkv[:, d, 1024 + half * 512:1024 + (half + 1) * 512],
                       d == 0, d == 7, r=[t_wkv[d], t_memT], w=[PT[b]])
                cp(Vm[:, sb, half * 512:(half + 1) * 512], bank(b), r=[PT[b]], w=[t_Vm], eng="scalar" if half == 0 else "vector")
        P.barrier()
        A.off = off_wkv
        NR = 10
        sms = [small_sm() for _ in range(NR)]
        junks = [(A.tile([128, D], BF16, "junk"), Tok()) for _ in range(2)]
        xbs = [A.tile([128, D], F32, "xbr") for _ in range(NR)]
        t_xbs = [Tok() for _ in range(NR)]
        zbs = [A.tile([128, D], F32, "zbr") for _ in range(2)]
        t_zbs = [Tok(), Tok()]
        ei = [0]

        def chunk_front(ck):
            seq = ck // 4
            dma("sync", xt[:], XT16v[:, :, ck * 512:(ck + 1) * 512], r=[t_XT[ck]], w=[t_xt])
            for c in range(8):
                b = c % 2
                for d in range(8):
                    mm(bank(b), wq[:, d, c * 128:(c + 1) * 128], xt[:, d, :], d == 0, d == 7, r=[t_wq[d], t_xt], w=[PT[b]])
                cp(QxT[:, c, :], bank(b), r=[PT[b]], w=[t_Qx], eng="scalar" if b == 0 else "vector")
            skh = Skew(1)
            for h in range(4):
                es = []
                for kb in range(2):
                    es.append(ei[0] % 4)
                    ei[0] += 1

                def hfront(h=h, es=es):
                    for kb in range(2):
                        e_i = es[kb]
                        for cc in range(2):
                            mm(bank(2 + kb), KmT[:, 2 * h + cc, seq * 256 + kb * 128:seq * 256 + (kb + 1) * 128], QxT[:, 2 * h + cc, :],
                               cc == 0, cc == 1, r=[t_Km, t_Qx], w=[PT[2 + kb]])
                        act(et[e_i][:], bank(2 + kb), AF.Exp, r=[PT[2 + kb]], w=[t_et[e_i]], scale=1.0 / 16)

                def hback(h=h, es=es):
                    for kb in range(2):
                        mm(bank(4), ones_bf[:], et[es[kb]][:], kb == 0, kb == 1, r=[t_et[es[kb]], t_const], w=[PT[4]])
                    vop("reciprocal", out=rden[:], in_=bank(4), r=[PT[4]], w=[t_rden])
                    for dc in range(2):
                        for kb in range(2):
                            mm(bank(5 + dc), Vm[:, seq * 2 + kb, h * 256 + dc * 128:h * 256 + (dc + 1) * 128], et[es[kb]][:],
                               kb == 0, kb == 1, r=[t_et[es[kb]], t_Vm], w=[PT[5 + dc]])
                        tt(axT[:, 2 * h + dc, :], bank(5 + dc), rden[:], ALU.mult, r=[PT[5 + dc], t_rden], w=[t_ax])
                skh.push(hfront, hback)
            skh.flush()

        def s0(tb):
            ck, tbl = tb // 4, tb % 4
            kb = tb % NR
            if tbl == 0:
                chunk_front(ck)
            dma("sync", xbs[kb][:], X32[tb * 128:(tb + 1) * 128, :], r=[t_X32[tb]], w=[t_xbs[kb]])
            for half in range(2):
                for c in range(8):
                    mm(bank(half), axT[:, c, tbl * 128:(tbl + 1) * 128], wo[:, c, half * 512:(half + 1) * 512],
                       c == 0, c == 7, r=[t_ax, t_wo[c]], w=[PT[half]])
            stt(xbs[kb][:], xbs[kb][:], ALPHA, ps[:, 0:1024], ALU.mult, ALU.add, r=[t_xbs[kb], PT[0], PT[1]], w=[t_xbs[kb]])

        def s8(tb):
            ck, tbl = tb // 4, tb % 4
            k2 = ck % 2
            kb = tb % NR
            z2 = tb % 2
            dma("sync", X32[tb * 128:(tb + 1) * 128, :], xbs[kb][:], r=[t_xbs[kb]], w=[t_X32[tb]])
            act(zbs[z2][:], xbs[kb][:], AF.Copy, r=[t_xbs[kb]], w=[t_zbs[z2]], scale=ALPHA)
            emit_xT(xbs[kb], t_xbs[kb], xst[k2], t_xst[k2], tbl, 7, 6)

        def s9(tb):
            ck, tbl = tb // 4, tb % 4
            k2 = ck % 2
            z2 = tb % 2
            dma("sync", Z32[tb * 128:(tb + 1) * 128, :], zbs[z2][:], r=[t_zbs[z2]], w=[t_Z32[tb]])
            if tbl == 3:
                dma("sync", XT16v[:, :, ck * 512:(ck + 1) * 512], xst[k2][:], r=[t_xst[k2]], w=[t_XT[ck]])
        run_pipe(NB, [s0] + ln_hops(xbs, t_xbs, sms, junks, G, Bt, t_gb, NR) + [s8, s9])

    def stage_MOE(l):
        A.reset()
        wr = A.tile([128, 8, 16], BF16, "wr"); t_wr = Tok()
        dma("gpsimd", wr[:], Wt["moe_router"][l].rearrange("(c p) e -> p c e", p=128), w=[t_wr])
        o48 = A.tile([48, 1], F32, "o48")
        dma("sync", o48[:], Cn["o48"], w=[t_wr])
        xt = A.tile([128, 8, 512], BF16, "xt"); t_xt = Tok()
        AF48 = A.tile([128, 16, 48], F32, "AF48"); t_AF = Tok()
        P.op("vector", lambda e: e.memset(AF48[:], 0.0), w=[t_AF])
        NRr = 4
        exs = [A.tile([128, 16], F32, "ex") for _ in range(NRr)]
        t_exs = [Tok() for _ in range(NRr)]
        sxs = [A.tile([128, 2], F32, "sx") for _ in range(NRr)]
        t_sxs = [Tok() for _ in range(NRr)]
        xts = [A.tile([128, 8, 512], BF16, "xtr") for _ in range(2)]
        t_xts = [Tok(), Tok()]
        AffT = A.tile([48, L], F32, "AffT"); t_AffT = Tok()
        MX = A.tile([48, 256], F32, "MX"); t_MX = Tok()
        IX = A.tile([48, 256], U32, "IX"); t_IX = Tok()
        IXF = A.tile([48, 256], F32, "IXF"); t_IXF = Tok()
        IDXT = A.tile([128, 2, 48], I32, "IDXT"); t_IDXT = Tok()
        GT = A.tile([128, 2, 48], F32, "GT"); t_GT = Tok()
        def r0(tb):
            ck, tbl = tb // 4, tb % 4
            k2 = ck % 2
            b = tb % NRr
            if tbl == 0:
                dma("sync", xts[k2][:], XT16v[:, :, ck * 512:(ck + 1) * 512], w=[t_xts[k2]])
            for d in range(8):
                mm(bank(b, 0, 16), xts[k2][:, d, tbl * 128:(tbl + 1) * 128], wr[:, d, :], d == 0, d == 7, r=[t_xts[k2], t_wr], w=[PT[b]])

        def r1(tb):
            b = tb % NRr
            act(exs[b][:], bank(b, 0, 16), AF.Exp, r=[PT[b]], w=[t_exs[b], t_sxs[b]], accum_out=sxs[b][:, 0:1])

        def r2(tb):
            b = tb % NRr
            vop("reciprocal", out=sxs[b][:, 1:2], in_=sxs[b][:, 0:1], r=[t_sxs[b]], w=[t_sxs[b]])

        def r3(tb):
            b = tb % NRr
            seq, tbs = tb // 16, tb % 16
            ts(AF48[:, tbs, seq * 32:seq * 32 + 16], exs[b][:], sxs[b][:, 1:2], None, ALU.mult, r=[t_exs[b], t_sxs[b]], w=[t_AF])
        run_pipe(NB, [r0, r1, r2, r3])
        for tbs in range(16):
            b = 1 + tbs % 2
            tr(bank(b, 0, 128, 0, 48), AF48[:, tbs, :], r=[t_AF], w=[PT[b]])
            cp(AffT[:, tbs * 128:(tbs + 1) * 128], bank(b, 0, 128, 0, 48), r=[PT[b]], w=[t_AffT], eng="vector")
        for rd in range(32):
            sl = slice(rd * 8, rd * 8 + 8)
            vop("max", out=MX[:, sl], in_=AffT[:], r=[t_AffT], w=[t_MX])
            vop("max_index", out=IX[:, sl], in_max=MX[:, sl], in_values=AffT[:], r=[t_AffT, t_MX], w=[t_IX])
            vop("match_replace", out=AffT[:], in_to_replace=MX[:, sl], in_values=AffT[:], imm_value=-1.0, r=[t_MX, t_IX], w=[t_AffT])
        cp(IXF[:], IX[:], r=[t_IX], w=[t_IXF], eng="vector")
        ts(IXF[:], IXF[:], o48[:, 0:1], None, ALU.add, r=[t_IXF, t_wr], w=[t_IXF])
        for half in range(2):
            tr(bank(3, 0, 48), IXF[:, half * 128:(half + 1) * 128], r=[t_IXF], w=[PT[3]])
            cp(IDXT[:, half, :], bank(3, 0, 48), r=[PT[3]], w=[t_IDXT], eng="vector")
            tr(bank(4, 0, 48), MX[:, half * 128:(half + 1) * 128], r=[t_MX], w=[PT[4]])
            cp(GT[:, half, :], bank(4, 0, 48), r=[PT[4]], w=[t_GT], eng="vector")
        wg = [A.tile([128, 8, 512], BF16, "wg") for _ in range(2)]
        wu = [A.tile([128, 8, 512], BF16, "wu") for _ in range(2)]
        wd = [A.tile([128, 4, D], BF16, "wd") for _ in range(2)]
        t_w = [Tok(), Tok()]
        xg = [A.tile([128, D], F32, "xg") for _ in range(2)]
        t_xg = [Tok(), Tok()]
        xeT = A.tile([128, 8, 512], BF16, "xeT"); t_xe = Tok()
        sg = [A.tile([128, 512], F32, "sg") for _ in range(2)]
        t_sg = [Tok(), Tok()]
        hidT = A.tile([128, 4, 512], BF16, "hidT"); t_hid = Tok()
        yb = [A.tile([128, D], F32, "yb") for _ in range(2)]
        t_yb = [Tok(), Tok()]

        def load_w(e):
            k = e % 2
            dma("gpsimd", wg[k][:], Wt["moe_w_gate"][l, e].rearrange("(c p) f -> p c f", p=128), w=[t_w[k]])
            dma("gpsimd", wu[k][:], Wt["moe_w_up"][l, e].rearrange("(c p) f -> p c f", p=128), w=[t_w[k]])
            dma("gpsimd", wd[k][:], Wt["moe_w_down"][l, e].rearrange("(c p) n -> p c n", p=128), w=[t_w[k]])

        load_w(0)
        gi = 0
        for e in range(16):
            k = e % 2
            for sh in range(4):
                seq, half = sh // 2, sh % 2
                g2 = gi % 2
                gi += 1
                col = seq * 32 + e
                P.op("gpsimd", lambda eng, g2=g2, half=half, col=col: eng.indirect_dma_start(
                    out=xg[g2][:], out_offset=None, in_=X32,
                    in_offset=bass.IndirectOffsetOnAxis(ap=IDXT[:, half, col:col + 1], axis=0),
                    bounds_check=bcr(eng), oob_is_err=False), r=[t_IDXT], w=[t_xg[g2]], dma=True)
                emit_xT(xg[g2], t_xg[g2], xeT, t_xe, sh, 5 + 2 * 0, 6)
            if e + 1 < 16:
                load_w(e + 1)
            for fc in range(4):
                s2 = fc % 2
                for d in range(8):
                    mm(bank(1), wg[k][:, d, fc * 128:(fc + 1) * 128], xeT[:, d, :], d == 0, d == 7, r=[t_w[k], t_xe], w=[PT[1]])
                for d in range(8):
                    mm(bank(2), wu[k][:, d, fc * 128:(fc + 1) * 128], xeT[:, d, :], d == 0, d == 7, r=[t_w[k], t_xe], w=[PT[2]])
                act(sg[s2][:], bank(1), AF.Silu, r=[PT[1]], w=[t_sg[s2]])
                tt(hidT[:, fc, :], sg[s2][:], bank(2), ALU.mult, r=[t_sg[s2], PT[2]], w=[t_hid])
            for sh in range(4):
                seq, half = sh // 2, sh % 2
                y2 = sh % 2
                col = seq * 32 + e
                for h2 in range(2):
                    for fc in range(4):
                        mm(bank(3 + h2), hidT[:, fc, sh * 128:(sh + 1) * 128], wd[k][:, fc, h2 * 512:(h2 + 1) * 512],
                           fc == 0, fc == 3, r=[t_hid, t_w[k]], w=[PT[3 + h2]])
                ts(yb[y2][:], ps[:, 3 * 512:5 * 512], GT[:, half, col:col + 1], None, ALU.mult, r=[PT[3], PT[4], t_GT], w=[t_yb[y2]])
                P.op("gpsimd", lambda eng, y2=y2, half=half, col=col: eng.indirect_dma_start(
                    out=Z32, out_offset=bass.IndirectOffsetOnAxis(ap=IDXT[:, half, col:col + 1], axis=0),
                    in_=yb[y2][:], in_offset=None, compute_op=ALU.add,
                    bounds_check=bcr(eng), oob_is_err=False), r=[t_IDXT, t_yb[y2]], w=[t_Zs[seq]], dma=True)

    def stage_LN3(l, dst):
        A.reset()
        G, Bt, t_gb = load_gb("ln3_g", "ln3_b", l)
        NR = 12
        sms = [small_sm() for _ in range(NR)]
        junks = [(A.tile([128, D], BF16, "junk"), Tok()) for _ in range(2)]
        xbs = [A.tile([128, D], F32, "xbr") for _ in range(NR)]
        t_xbs = [Tok() for _ in range(NR)]
        xst = [A.tile([128, 8, 512], BF16, "xst") for _ in range(2)]
        t_xst = [Tok(), Tok()]

        def s0(tb):
            kb = tb % NR
            dma("sync", xbs[kb][:], Z32[tb * 128:(tb + 1) * 128, :], w=[t_xbs[kb]])

        def s8(tb):
            ck, tbl = tb // 4, tb % 4
            k2 = ck % 2
            kb = tb % NR
            dma("sync", dst[tb * 128:(tb + 1) * 128, :], xbs[kb][:], r=[t_xbs[kb]], w=[t_X32[tb]])
            bb = 2 * (tb % 4)
            emit_xT(xbs[kb], t_xbs[kb], xst[k2], t_xst[k2], tbl, bb, bb + 1)

        def s9(tb):
            ck, tbl = tb // 4, tb % 4
            k2 = ck % 2
            if tbl == 3:
                dma("sync", XT16v[:, :, ck * 512:(ck + 1) * 512], xst[k2][:], r=[t_xst[k2]], w=[t_XT[ck]])
        run_pipe(NB, [s0] + ln_hops(xbs, t_xbs, sms, junks, G, Bt, t_gb, NR) + [s8, s9])

    def stage_HY(l):
        A.reset()
        HS = A.tile([128, 16, 256], BF16, "HS"); t_HS = Tok()
        HD = A.tile([128, 16, 256], BF16, "HD"); t_HD = Tok()
        x0T = A.tile([128, 2, T], BF16, "x0T"); t_x0 = Tok()
        zT = A.tile([128, 2, T], BF16, "zT"); t_zT = Tok()
        z_tm = A.tile([128, 16, 512], BF16, "z_tm"); t_ztm = Tok()
        hbias = A.tile([128, 2], F32, "hbias"); t_hb = Tok()
        for j in range(2):
            dma("sync", hbias[:, j:j + 1], Wt["hy_bias"][l, j * 128:(j + 1) * 128].rearrange("(p o) -> p o", o=1), w=[t_hb])
        sub_mark = A.off
        featsT = A.tile([33, L], F32, "featsT"); t_f = Tok()
        dma("sync", featsT[:], Cn["featsT"], w=[t_f])
        negt = A.tile([128, 16], F32, "negt"); m0 = A.tile([128, 16], F32, "m0")
        dma("sync", negt[:], Cn["negt"], w=[t_f])
        dma("sync", m0[:], Cn["m0"], w=[t_f])
        w1 = A.tile([33, 64], F32, "w1"); w2 = A.tile([64, 64], F32, "w2"); w3 = A.tile([64, 64], F32, "w3")
        wout = A.tile([64, 512], F32, "wout")
        dma("sync", w1[:], Wt["hy_w1"][l], w=[t_f])
        dma("sync", w2[:], Wt["hy_w2"][l], w=[t_f])
        dma("sync", w3[:], Wt["hy_w3"][l], w=[t_f])
        dma("sync", wout[:], Wt["hy_wout"][l], w=[t_f])
        prm = A.tile([64, 8], F32, "prm"); t_prm = Tok()
        for i, nm in enumerate(("hy_b1", "hy_b2", "hy_b3", "hy_freq")):
            dma("sync", prm[:, i:i + 1], Wt[nm][l].rearrange("(p o) -> p o", o=1), w=[t_prm])
        ts(prm[:, 4:7], prm[:, 0:3], prm[:, 3:4], None, ALU.mult, r=[t_prm], w=[t_prm])
        AD = A.tile([128, 512], F32, "AD"); t_AD = Tok()
        dma("sync", AD[:], Wt["hy_decay"][l].rearrange("a c -> (a c)").partition_broadcast(128), w=[t_AD])
        act(AD[:], AD[:], AF.Abs, r=[t_AD], w=[t_AD])
        hA = A.tile([64, L], F32, "hA"); hB = A.tile([64, L], F32, "hB")
        t_hA, t_hB = Tok(), Tok()
        a1 = [A.tile([64, 512], F32, "a1") for _ in range(2)]
        a2 = [A.tile([64, 512], F32, "a2") for _ in range(2)]
        t_a = [Tok(), Tok()]
        chain = [(featsT, t_f, 33, w1, hA, t_hA), (hA, t_hA, 64, w2, hB, t_hB), (hB, t_hB, 64, w3, hA, t_hA)]
        for i, (src, t_src, kk, wi, dst, t_dst) in enumerate(chain):
            for nq in range(4):
                b = nq % 2
                mm(bank(b, 0, 512, 0, 64), wi[0:kk, :], src[0:kk, nq * 512:(nq + 1) * 512], True, True, r=[t_f, t_src], w=[PT[b]])
                ts(a1[b][:], bank(b, 0, 512, 0, 64), prm[:, 3:4], prm[:, 4 + i:5 + i], ALU.mult, ALU.add, r=[PT[b], t_prm], w=[t_a[b]])
                ts(a2[b][:], a1[b][:], 1.0 / TWO_PI, MAGIC, ALU.mult, ALU.add, r=[t_a[b]], w=[t_a[b]])
                ts(a2[b][:], a2[b][:], MAGIC, -TWO_PI, ALU.subtract, ALU.mult, r=[t_a[b]], w=[t_a[b]])
                tt(a1[b][:], a1[b][:], a2[b][:], ALU.add, r=[t_a[b]], w=[t_a[b]])
                ts(a1[b][:], a1[b][:], 3.1415925, -3.1415925, ALU.min, ALU.max, r=[t_a[b]], w=[t_a[b]])
                act(dst[:, nq * 512:(nq + 1) * 512], a1[b][:], AF.Sin, r=[t_a[b]], w=[t_dst])
        h3, t_h3 = hA, t_hA
        E = [A.tile([128, 512], F32, "E") for _ in range(2)]
        fl = [A.tile([128, 512], F32, "fl") for _ in range(2)]
        t_E = [Tok(), Tok()]
        t_fl = [Tok(), Tok()]
        for lb in range(16):
            b = 2 + lb % 2
            k = lb % 2
            mm(bank(b), h3[:, lb * 128:(lb + 1) * 128], wout[:], True, True, r=[t_h3, t_f], w=[PT[b]])
            act(E[k][:], AD[:], AF.Exp, r=[t_AD, t_f], w=[t_E[k]], scale=negt[:, lb:lb + 1])
            tt(fl[k][:], bank(b), E[k][:], ALU.mult, r=[PT[b], t_E[k]], w=[t_fl[k]])
            ts(fl[k][:, 256:512], fl[k][:, 256:512], m0[:, lb:lb + 1], None, ALU.mult, r=[t_fl[k], t_f], w=[t_fl[k]], eng="vector")
            tt(HS[:, lb, :], fl[k][:, 0:256], fl[k][:, 256:512], ALU.add, r=[t_fl[k]], w=[t_HS], eng="gpsimd")
            tt(HD[:, lb, :], fl[k][:, 256:512], fl[k][:, 0:256], ALU.subtract, r=[t_fl[k]], w=[t_HD], eng="vector")
        P.barrier()
        A.off = sub_mark
        cw = A.tile([128, 6, 4], F32, "cw"); t_cw = Tok()
        for cc in range(6):
            for k in range(3):
                dma("sync", cw[:, cc, k:k + 1], Wt["hy_conv_w"][l, k, cc * 128:(cc + 1) * 128].rearrange("(p o) -> p o", o=1), w=[t_cw])
            dma("sync", cw[:, cc, 3:4], Wt["hy_conv_b"][l, cc * 128:(cc + 1) * 128].rearrange("(p o) -> p o", o=1), w=[t_cw])
        ut = [A.tile([128, L], F32, "ut") for _ in range(2)]
        t_ut = [Tok(), Tok()]
        uc = [A.tile([128, L], F32, "uc") for _ in range(2)]
        t_uc = [Tok(), Tok()]
        zf = A.tile([128, L], F32, "zf"); t_zf = Tok()
        ui = [0]

        def conv(seq, cc, dstk):
            u = ui[0] % 2
            ui[0] += 1
            dma("sync", ut[u][:], UT[cc * 128:(cc + 1) * 128, seq * L:(seq + 1) * L], w=[t_ut[u]])
            o, t_o = uc[dstk], t_uc[dstk]
            ts(o[:], ut[u][:], cw[:, cc, 1:2], cw[:, cc, 3:4], ALU.mult, ALU.add, r=[t_ut[u], t_cw], w=[t_o])
            stt(o[:, 1:L], ut[u][:, 0:L - 1], cw[:, cc, 0:1], o[:, 1:L], ALU.mult, ALU.add, r=[t_ut[u], t_cw, t_o], w=[t_o])
            stt(o[:, 0:L - 1], ut[u][:, 1:L], cw[:, cc, 2:3], o[:, 0:L - 1], ALU.mult, ALU.add, r=[t_ut[u], t_cw, t_o], w=[t_o])

        for seq in range(2):
            for j in range(2):
                conv(seq, 2 + j, 0)
                conv(seq, 4 + j, 1)
                tt(zf[:], uc[0][:], uc[1][:], ALU.mult, r=[t_uc[0], t_uc[1]], w=[t_zf])
                cp(zT[:, j, seq * L:(seq + 1) * L], zf[:], r=[t_zf], w=[t_zT], eng="scalar")
                for g in range(4):
                    b = 4 + g % 2
                    for i in range(4):
                        tb = g * 4 + i
                        tr(bank(b, i * 128, (i + 1) * 128), zf[:, tb * 128:(tb + 1) * 128], r=[t_zf], w=[PT[b]])
                    c0 = seq * 256 + j * 128
                    cp(z_tm[:, g * 4:(g + 1) * 4, c0:c0 + 128], bank(b).rearrange("p (a t) -> p a t", a=4), r=[PT[b]], w=[t_ztm],
                       eng="vector" if g % 2 else "scalar")
                conv(seq, j, 0)
                cp(x0T[:, j, seq * L:(seq + 1) * L], uc[0][:], r=[t_uc[0]], w=[t_x0], eng="scalar")
        P.barrier()
        A.off = sub_mark
        Pre = A.tile([128, 16, 512], BF16, "Pre"); t_Pre = Tok()
        Pim = A.tile([128, 16, 512], BF16, "Pim"); t_Pim = Tok()
        cfb = [A.tile([128, 16, 128], BF16, "cfb") for _ in range(2)]
        sfb = [A.tile([128, 16, 128], BF16, "sfb") for _ in range(2)]
        t_cs = [Tok(), Tok()]
        hh = [A.tile([128, 512], F32, "hh") for _ in range(2)]
        t_hh = [Tok(), Tok()]
        _tq1 = A.tile([128, 4, 512], F32, "tq")
        tq_ = [_tq1, _tq1]
        _ttq = Tok()
        t_tq = [_ttq, _ttq]
        for fb in range(16):
            k = fb % 2
            b0 = 4 * k
            dma("sync", cfb[k][:], Cn["cf"][fb], w=[t_cs[k]])
            dma("sync", sfb[k][:], Cn["sf"][fb], w=[t_cs[k]])
            for lb in range(16):
                mm(bank(b0, 0, 256), cfb[k][:, lb, :], HS[:, lb, :], lb == 0, lb == 15, r=[t_cs[k]], w=[PT[b0]])
            for lb in range(16):
                mm(bank(b0 + 1, 0, 256), sfb[k][:, lb, :], HD[:, lb, :], lb == 0, lb == 15, r=[t_cs[k]], w=[PT[b0 + 1]])
            for tb in range(16):
                mm(bank(b0 + 2), cfb[k][:, tb, :], z_tm[:, tb, :], tb == 0, tb == 15, r=[t_cs[k]], w=[PT[b0 + 2]])
            for tb in range(16):
                mm(bank(b0 + 3), sfb[k][:, tb, :], z_tm[:, tb, :], tb == 0, tb == 15, r=[t_cs[k]], w=[PT[b0 + 3]])
            cp(hh[k][:, 0:256], bank(b0, 0, 256), r=[PT[b0]], w=[t_hh[k]], eng="vector")
            cp(hh[k][:, 256:512], bank(b0 + 1, 0, 256), r=[PT[b0 + 1]], w=[t_hh[k]], eng="vector")
            hre = hh[k][:, 0:256].unsqueeze(1).to_broadcast([128, 2, 256])
            him = hh[k][:, 256:512].unsqueeze(1).to_broadcast([128, 2, 256])
            Av = bank(b0 + 2).rearrange("p (s c) -> p s c", s=2)
            Bv = bank(b0 + 3).rearrange("p (s c) -> p s c", s=2)
            tv = [tq_[k][:, i, :].rearrange("p (s c) -> p s c", s=2) for i in range(4)]
            tt(tv[0], Av, hre, ALU.mult, r=[PT[b0 + 2], t_hh[k]], w=[t_tq[k]])
            tt(tv[1], Bv, him, ALU.mult, r=[PT[b0 + 3], t_hh[k]], w=[t_tq[k]])
            tt(tv[2], Bv, hre, ALU.mult, r=[PT[b0 + 3], t_hh[k]], w=[t_tq[k]])
            tt(tv[3], Av, him, ALU.mult, r=[PT[b0 + 2], t_hh[k]], w=[t_tq[k]])
            tt(Pre[:, fb, :], tq_[k][:, 0, :], tq_[k][:, 1, :], ALU.add, r=[t_tq[k]], w=[t_Pre], eng="gpsimd")
            tt(Pim[:, fb, :], tq_[k][:, 2, :], tq_[k][:, 3, :], ALU.subtract, r=[t_tq[k]], w=[t_Pim], eng="gpsimd")
        cit = A.tile([128, 16, 512], BF16, "cit")
        sit = A.tile([128, 16, 512], BF16, "sit")
        t_ci = Tok()
        tmp = [A.tile([128, 512], F32, "tmp") for _ in range(2)]
        t_tmp = [Tok(), Tok()]
        obt = [A.tile([128, 512], BF16, "obt") for _ in range(2)]
        t_obt = [Tok(), Tok()]
        n = 0
        for tq in range(4):
            dma("sync", cit[:], Cn["ci"][:, :, tq * 512:(tq + 1) * 512], w=[t_ci])
            dma("sync", sit[:], Cn["si"][:, :, tq * 512:(tq + 1) * 512], w=[t_ci])
            for sq in range(2):
                for j in range(2):
                    b = n % 2
                    n += 1
                    c0 = sq * 256 + j * 128
                    for kb in range(16):
                        mm(bank(b), Pre[:, kb, c0:c0 + 128], cit[:, kb, :], kb == 0, False, r=[t_Pre, t_ci], w=[PT[b]])
                        mm(bank(b), Pim[:, kb, c0:c0 + 128], sit[:, kb, :], False, kb == 15, r=[t_Pim, t_ci], w=[PT[b]])
                    t0 = sq * L + tq * 512
                    stt(tmp[b][:], zT[:, j, t0:t0 + 512], hbias[:, j:j + 1], bank(b), ALU.mult, ALU.add, r=[t_zT, t_hb, PT[b]], w=[t_tmp[b]])
                    tt(obt[b][:], tmp[b][:], x0T[:, j, t0:t0 + 512], ALU.mult, r=[t_tmp[b], t_x0], w=[t_obt[b]], eng="gpsimd")
                    dma("sync", AT[512 + j * 128:512 + (j + 1) * 128, t0:t0 + 512], obt[b][:], r=[t_obt[b]], w=[tAT((4 + j, t0 // 512))])

    stage_init()
    P.barrier()
    for l in range(NL if stop_after != "init" else 0):
        stage_A(l)
        P.barrier()
        if stop_after == "A":
            break
        if HY:
            stage_HY(l)
            P.barrier()
            if stop_after == "HY":
                break
        stage_ATT()
        P.barrier()
        if stop_after == "ATT":
            break
        stage_WO(l, x_in if l == 0 else X32, HY)
        P.barrier()
        if stop_after == "WO":
            break
        stage_XA(l)
        P.barrier()
        if stop_after == "XA":
            break
        stage_MOE(l)
        P.barrier()
        last = (l == NL - 1)
        stage_LN3(l, out if last else X32)
        P.barrier()
    if stop_after in ("WO", "XA"):
        stage_final()
    P.emit(st)
    st.close()
    return nc


_CONSTS = None


def _run(inputs, NL=DEPTH, dbg=(), stop_after=None, cores=NCORES):
    global _CONSTS
    if _CONSTS is None:
        _CONSTS = _host_consts()
    nc = build(NL, dbg, stop_after)
    x = np.ascontiguousarray(np.asarray(inputs["x"], dtype=np.float32)).reshape(16, L, D)
    mem = np.ascontiguousarray(np.asarray(inputs["mem"], dtype=np.float32)).reshape(16, 256, D)
    in_maps = []
    for c in range(cores):
        m = {"x": x[2 * c:2 * c + 2].reshape(T, D), "mem": mem[2 * c:2 * c + 2].reshape(512, D)}
        for k in _W_SHAPES:
            m[k] = np.ascontiguousarray(np.asarray(inputs[k], dtype=np.float32))
        for k in _CONST_SHAPES:
            m[k] = _CONSTS[k]
        in_maps.append(m)
    res = run_bass_kernel_spmd(nc, in_maps, core_ids=list(range(cores)))
    return res.results


def kernel(**inputs):
    res = _run(inputs)
    out = np.stack([r["out"].reshape(2, L, D) for r in res], axis=0).reshape(16, L, D)
    return out.astype(np.float32)
```

```python
import contextlib
import os
CUT = int(os.environ.get('A_CUT', '99'))
SUB = int(os.environ.get('A_SUB', '99'))
HY = int(os.environ.get('A_HY', '1'))
import math
import numpy as np
import ml_dtypes
import concourse.bass as bass
import concourse.mybir as mybir
from concourse.bass_utils import run_bass_kernel_spmd

F32 = mybir.dt.float32
BF16 = mybir.dt.bfloat16
I32 = mybir.dt.int32
U32 = mybir.dt.uint32
ALU = mybir.AluOpType
AF = mybir.ActivationFunctionType
AX = mybir.AxisListType

NCORES = 8
L = 2048
T = 4096
D = 1024
NB = 32
DEPTH = 4
ALPHA = float((2 * DEPTH) ** 0.25)
RMS_EPS = 1e-6
LN_EPS = 1e-5
TWO_PI = 2.0 * math.pi
MAGIC = 12582912.0


class Tok:
    __slots__ = ("w", "r")

    def __init__(self):
        self.w = None
        self.r = {}


class Prog:
    ENG = ["tensor", "vector", "scalar", "gpsimd", "sync"]
    NDMA = 8

    def __init__(self, nc):
        self.nc = nc
        self.ops = {e: [] for e in self.ENG}
        self.cnt = {e: 0 for e in self.ENG}
        self.seen = {e: {} for e in self.ENG}
        self.dma_n = {e: 0 for e in self.ENG}
        self.last = {}

    def op(self, eng, fn, r=(), w=(), dma=False):
        deps = {}

        def add(key, val):
            if deps.get(key, 0) < val:
                deps[key] = val

        for t in r:
            if t.w is not None:
                add(*t.w)
        for t in w:
            if t.w is not None:
                add(*t.w)
            for k, v in t.r.items():
                add(k, v)
        if dma:
            j = self.dma_n[eng]
            self.dma_n[eng] += 1
            slot = j % self.NDMA
            k = j // self.NDMA + 1
            me = ((eng, "dma", slot), 16 * k)
            if k > 1:
                add((eng, "dma", slot), 16 * (k - 1))
        else:
            self.cnt[eng] += 1
            me = ((eng, "c"), self.cnt[eng])
        seen = self.seen[eng]
        waits = []
        for key, val in deps.items():
            if seen.get(key, 0) >= val:
                continue
            seen[key] = val
            waits.append((key, val))
        self.ops[eng].append((fn, waits, me))
        self.last[me[0]] = me[1]
        for t in r:
            if t.r.get(me[0], 0) < me[1]:
                t.r[me[0]] = me[1]
        for t in w:
            t.w = me
            t.r = {}
        return me

    def barrier(self):
        snap = dict(self.last)
        for e in self.ENG:
            seen = self.seen[e]
            waits = []
            for key, val in snap.items():
                if seen.get(key, 0) >= val:
                    continue
                seen[key] = val
                waits.append((key, val))
            if waits:
                self.ops[e].append((None, waits, None))

    def emit(self, stack):
        nc = self.nc
        self.barrier()
        sems = {}
        for e in self.ENG:
            for fn, waits, me in self.ops[e]:
                for key, _ in waits:
                    if key not in sems:
                        sems[key] = None
                if me is not None and me[0] not in sems:
                    sems[me[0]] = None
        for key in sems:
            sems[key] = stack.enter_context(nc.semaphore("s_" + "_".join(str(x) for x in key)))
        block = stack.enter_context(nc.Block())

        def run(e):
            def body(eng):
                for fn, waits, me in self.ops[e]:
                    for key, val in waits:
                        eng.wait_ge(sems[key], val)
                    if fn is not None:
                        ins = fn(eng)
                        ins.then_inc(sems[me[0]], 16 if me[0][1] == "dma" else 1)
            return body

        block.tensor(run("tensor"))
        block.vector(run("vector"))
        block.scalar(run("scalar"))
        block.gpsimd(run("gpsimd"))
        block.sync(run("sync"))


def _esz(dt):
    return 2 if dt == BF16 else 4


class Arena:
    BASE = 17408
    LIMIT = 229376

    def __init__(self, nc):
        self.nc = nc
        self.off = self.BASE
        self.mark = self.BASE
        self.n = 0

    def reset(self):
        self.off = self.mark

    def tile(self, shape, dt, name="t"):
        sz = int(np.prod(shape[1:])) * _esz(dt)
        self.n += 1
        t = self.nc.alloc_sbuf_tensor_at(f"{name}_{self.n}", list(shape), dt, offset=self.off)
        self.off += (sz + 63) // 64 * 64
        assert self.off <= self.LIMIT, (name, self.off)
        return t


def _host_consts():
    c = {}
    c["ident"] = np.eye(128, dtype=np.float32)
    t = np.arange(L)
    row = (t // 64).astype(np.float64)
    col = (t % 64).astype(np.float64)

    def tab(n):
        inv = 10000.0 ** (-(np.arange(n, dtype=np.float64) * 2.0 / (2 * n)))
        ang = np.stack([row[:, None] * inv[None], col[:, None] * inv[None]], axis=1)
        cs = np.cos(ang).astype(np.float32).reshape(16, 128, 2 * n).transpose(1, 0, 2)
        sn = np.sin(ang).astype(np.float32).reshape(16, 128, 2 * n).transpose(1, 0, 2)
        return np.ascontiguousarray(cs), np.ascontiguousarray(sn)

    c["cq"], c["sq"] = tab(16)
    c["cm"], c["sm"] = tab(8)
    tt = np.linspace(0.0, 1.0, L, dtype=np.float32)[:, None]
    bands = 16
    fb = np.linspace(1e-4, bands - 1, bands, dtype=np.float32)[None, :]
    w = (2.0 * math.pi * np.arange(L, dtype=np.float32)[:, None] / L).astype(np.float32)
    feats = np.concatenate([tt, np.cos(fb * w), -np.sin(fb * w)], axis=-1).astype(np.float32)
    c["featsT"] = np.ascontiguousarray(feats.T)
    c["negt"] = np.ascontiguousarray((-tt[:, 0]).reshape(16, 128).T).astype(np.float32)
    m0 = np.ones((128, 16), np.float32)
    m0[0, 0] = 0.0
    c["m0"] = m0
    k = np.arange(L, dtype=np.int64)
    n = np.arange(L, dtype=np.int64)
    ph = ((2 * k[None, :] + 1) * n[:, None]) % 8192
    ang = ph.astype(np.float64) * (math.pi / 4096.0)
    C = np.cos(ang)
    S = np.sin(ang)
    def fwd(M):
        return np.ascontiguousarray(M.reshape(16, 128, 16, 128).transpose(2, 1, 0, 3)).astype(ml_dtypes.bfloat16)
    c["cf"] = fwd(C)
    c["sf"] = fwd(S)
    def inv(M):
        return np.ascontiguousarray((M.T / 2048.0).reshape(16, 128, L).transpose(1, 0, 2)).astype(ml_dtypes.bfloat16)
    c["ci"] = inv(C)
    c["si"] = inv(S)
    o48 = np.zeros((48, 1), np.float32)
    o48[32:] = 2048.0
    c["o48"] = o48
    return c


_CONST_SHAPES = {
    "ident": ([128, 128], F32), "cq": ([128, 16, 32], F32), "sq": ([128, 16, 32], F32),
    "cm": ([128, 16, 16], F32), "sm": ([128, 16, 16], F32), "featsT": ([33, L], F32),
    "negt": ([128, 16], F32), "m0": ([128, 16], F32),
    "cf": ([16, 128, 16, 128], BF16), "sf": ([16, 128, 16, 128], BF16),
    "ci": ([128, 16, L], BF16), "si": ([128, 16, L], BF16), "o48": ([48, 1], F32),
}

_W_SHAPES = {
    "w_in": [4, 1024, 1952], "gqa_q_norm": [4, 64], "gqa_k_norm": [4, 64], "hy_conv_w": [4, 3, 768],
    "hy_conv_b": [4, 768], "hy_w1": [4, 33, 64], "hy_b1": [4, 64], "hy_w2": [4, 64, 64], "hy_b2": [4, 64],
    "hy_w3": [4, 64, 64], "hy_b3": [4, 64], "hy_wout": [4, 64, 512], "hy_freq": [4, 64],
    "hy_decay": [4, 2, 256], "hy_bias": [4, 256], "mla_q_norm": [4, 256], "mla_w_uq": [4, 256, 384],
    "mla_kv_norm": [4, 128], "mla_w_ukv": [4, 128, 512], "w_o": [4, 1024, 1024], "ln1_g": [4, 1024],
    "ln1_b": [4, 1024], "xa_wq": [4, 1024, 1024], "xa_wkv": [4, 1024, 2048], "xa_wo": [4, 1024, 1024],
    "ln2_g": [4, 1024], "ln2_b": [4, 1024], "moe_router": [4, 1024, 16], "moe_w_gate": [4, 16, 1024, 512],
    "moe_w_up": [4, 16, 1024, 512], "moe_w_down": [4, 16, 512, 1024], "ln3_g": [4, 1024], "ln3_b": [4, 1024],
}


def build(NL=DEPTH, dbg=(), stop_after=None):
    nc = bass.Bass("TRN2", target_bir_lowering=False)

    def din(name, shape, dt=F32):
        return nc.dram_tensor(name, list(shape), dt, kind="ExternalInput").ap()

    x_in = din("x", [T, D])
    mem_in = din("mem", [512, D])
    Wt = {k: din(k, s) for k, s in _W_SHAPES.items()}
    Cn = {k: din(k, s, dt) for k, (s, dt) in _CONST_SHAPES.items()}
    out = nc.dram_tensor("out", [T, D], F32, kind="ExternalOutput").ap()

    def dscr(name, shape, dt):
        kind = "ExternalOutput" if name in dbg else "Internal"
        return nc.dram_tensor(name, list(shape), dt, kind=kind).ap()

    X32 = dscr("X32", [T, D], F32)
    Z32 = dscr("Z32", [T, D], F32)
    XT16 = dscr("XT16", [D, T], BF16)
    QTd = dscr("QTd", [6, 128, T], BF16)
    QCTd = dscr("QCTd", [4, 96, T], BF16)
    KCTd = dscr("KCTd", [4, 96, T], BF16)
    UT = dscr("UT", [768, T], F32)
    AT = dscr("AT", [D, T], BF16)
    XT16v = XT16.rearrange("(c p) t -> p c t", p=128)
    ATv = AT.rearrange("(c p) t -> p c t", p=128)
    t_X32 = [Tok() for _ in range(NB)]
    t_Z32 = [Tok() for _ in range(NB)]
    t_Zs = [Tok(), Tok()]
    t_XT = [Tok() for _ in range(8)]
    t_QT = [Tok() for _ in range(8)]
    t_QCT = [Tok() for _ in range(8)]
    t_KCT = [Tok() for _ in range(8)]
    t_UT = [Tok() for _ in range(8)]
    t_AT = {}

    def tAT(key):
        if key not in t_AT:
            t_AT[key] = Tok()
        return t_AT[key]

    P = Prog(nc)
    A = Arena(nc)
    _bc = {}

    def bcr(eng):
        if "r" not in _bc:
            _bc["r"] = eng.to_reg(T - 1)
        return _bc["r"]
    st = contextlib.ExitStack()
    ps = st.enter_context(nc.psum_tensor("ps", [128, 4096], F32))
    PT = [Tok() for _ in range(8)]

    def bank(b, lo=0, hi=512, p0=0, p1=128):
        return ps[p0:p1, b * 512 + lo:b * 512 + hi]

    def dma(q, out_, in_, r=(), w=()):
        P.op(q, lambda e: e.dma_start(out=out_, in_=in_), r=r, w=w, dma=True)

    def mm(out_, lhsT, rhs, start, stop, r=(), w=()):
        P.op("tensor", lambda e: e.matmul(out_, lhsT=lhsT, rhs=rhs, start=start, stop=stop), r=r, w=w)

    def tr(out_, in_, r=(), w=()):
        pp = in_.shape[0]
        P.op("tensor", lambda e: e.transpose(out=out_, in_=in_, identity=ident[0:pp, 0:pp]), r=list(r) + [t_const], w=w)

    def act(out_, in_, func, r=(), w=(), **kw):
        P.op("scalar", lambda e: e.activation(out=out_, in_=in_, func=func, **kw), r=r, w=w)

    def vop(name, r=(), w=(), eng="vector", **kw):
        P.op(eng, lambda e: getattr(e, name)(**kw), r=r, w=w)

    def tt(out_, in0, in1, op, r=(), w=(), eng="vector"):
        P.op(eng, lambda e: e.tensor_tensor(out=out_, in0=in0, in1=in1, op=op), r=r, w=w)

    def ts(out_, in0, s1, s2, op0, op1=None, r=(), w=(), eng="vector"):
        if op1 is None:
            P.op(eng, lambda e: e.tensor_scalar(out=out_, in0=in0, scalar1=s1, scalar2=None, op0=op0), r=r, w=w)
        else:
            P.op(eng, lambda e: e.tensor_scalar(out=out_, in0=in0, scalar1=s1, scalar2=s2, op0=op0, op1=op1), r=r, w=w)

    def stt(out_, in0, scalar, in1, op0, op1, r=(), w=(), eng="vector"):
        P.op(eng, lambda e: e.scalar_tensor_tensor(out=out_, in0=in0, scalar=scalar, in1=in1, op0=op0, op1=op1), r=r, w=w)

    def cp(out_, in_, r=(), w=(), eng="vector"):
        if eng == "scalar":
            P.op("scalar", lambda e: e.copy(out=out_, in_=in_), r=r, w=w)
        else:
            P.op(eng, lambda e: e.tensor_copy(out=out_, in_=in_), r=r, w=w)

    t_const = Tok()
    ident = A.tile([128, 128], F32, "ident")
    dma("sync", ident[:], Cn["ident"], w=[t_const])
    ones_bf = A.tile([128, 128], BF16, "ones")
    P.op("vector", lambda e: e.memset(ones_bf[:], 1.0), w=[t_const])
    ones_f = A.tile([128, 64], F32, "ones_f")
    P.op("vector", lambda e: e.memset(ones_f[:], 1.0), w=[t_const])
    CQ = A.tile([128, 16, 32], F32, "CQ"); SQ = A.tile([128, 16, 32], F32, "SQ")
    CM = A.tile([128, 16, 16], F32, "CM"); SM = A.tile([128, 16, 16], F32, "SM")
    for tl, nm in ((CQ, "cq"), (SQ, "sq"), (CM, "cm"), (SM, "sm")):
        dma("sync", tl[:], Cn[nm], w=[t_const])
    VA = A.tile([128, NB, 2, 65], BF16, "VA")
    VC = A.tile([128, NB, 4, 65], BF16, "VC")
    t_VA = [Tok() for _ in range(NB)]
    t_VC = [Tok() for _ in range(NB)]
    P.op("vector", lambda e: e.memset(VA[:], 1.0), w=t_VA)
    P.op("vector", lambda e: e.memset(VC[:], 1.0), w=t_VC)
    memT = A.tile([128, 8, 512], BF16, "memT")
    t_memT = Tok()
    A.mark = A.off

    def emit_xT(src, t_src, xst, t_xst, tbl, b0, b1):
        for c in range(8):
            b = b0 if c < 4 else b1
            tr(bank(b, (c % 4) * 128, (c % 4) * 128 + 128), src[:, c * 128:(c + 1) * 128], r=[t_src], w=[PT[b]])
        cp(xst[:, 0:4, tbl * 128:(tbl + 1) * 128], bank(b0).rearrange("p (c t) -> p c t", c=4),
           r=[PT[b0]], w=[t_xst], eng="scalar")
        cp(xst[:, 4:8, tbl * 128:(tbl + 1) * 128], bank(b1).rearrange("p (c t) -> p c t", c=4),
           r=[PT[b1]], w=[t_xst], eng="vector")

    def emit_ln(zt, t_z, G, Bt, t_gb, sm, t_sm):
        junk = sm["junk"]
        act(junk[:], zt[:], AF.Copy, r=[t_z], w=[sm["tj"], t_sm], scale=1.0 / D, accum_out=sm["s"][:, 2:3])
        act(junk[:], zt[:], AF.Square, r=[t_z], w=[sm["tj"], t_sm], scale=float(D ** -0.5), accum_out=sm["s"][:, 1:2])
        stt(sm["s"][:, 4:5], sm["s"][:, 2:3], sm["s"][:, 2:3], sm["s"][:, 1:2], ALU.mult, ALU.subtract, r=[t_sm], w=[t_sm])
        act(sm["s"][:, 5:6], sm["s"][:, 4:5], AF.Sqrt, r=[t_sm], w=[t_sm], bias=LN_EPS, scale=-1.0)
        vop("reciprocal", out=sm["s"][:, 5:6], in_=sm["s"][:, 5:6], r=[t_sm], w=[t_sm])
        stt(sm["s"][:, 6:7], sm["s"][:, 2:3], -1.0, sm["s"][:, 5:6], ALU.mult, ALU.mult, r=[t_sm], w=[t_sm])
        act(zt[:], zt[:], AF.Identity, r=[t_z, t_sm], w=[t_z], scale=sm["s"][:, 5:6], bias=sm["s"][:, 6:7])
        tt(zt[:], zt[:], G[:], ALU.mult, r=[t_z, t_gb], w=[t_z], eng="vector")
        tt(zt[:], zt[:], Bt[:], ALU.add, r=[t_z, t_gb], w=[t_z], eng="gpsimd")

    def load_gb(gname, bname, l):
        G = A.tile([128, D], F32, "G"); Bt = A.tile([128, D], F32, "B")
        t_gb = Tok()
        dma("sync", G[:], Wt[gname][l].partition_broadcast(128), w=[t_gb])
        dma("sync", Bt[:], Wt[bname][l].partition_broadcast(128), w=[t_gb])
        return G, Bt, t_gb

    def ln_scratch():
        return {"s": A.tile([128, 8], F32, "lns"), "junk": A.tile([128, D], BF16, "junk"), "tj": Tok()}, Tok()


    def run_pipe(n, stages):
        K = len(stages)
        for i in range(n + K - 1):
            for k, f in enumerate(stages):
                j = i - k
                if 0 <= j < n:
                    f(j)

    def ln_hops(xbs, t_xbs, sms, junks, G, Bt, t_gb, NR):
        def h1(tb):
            kb = tb % NR
            sm, t_sm = sms[kb]
            jk, t_jk = junks[tb % 2]
            act(jk[:], xbs[kb][:], AF.Copy, r=[t_xbs[kb]], w=[t_jk, t_sm], scale=1.0 / D, accum_out=sm["s"][:, 2:3])
            act(jk[:], xbs[kb][:], AF.Square, r=[t_xbs[kb]], w=[t_jk, t_sm], scale=float(D ** -0.5), accum_out=sm["s"][:, 1:2])

        def h2(tb):
            sm, t_sm = sms[tb % NR]
            stt(sm["s"][:, 4:5], sm["s"][:, 2:3], sm["s"][:, 2:3], sm["s"][:, 1:2], ALU.mult, ALU.subtract, r=[t_sm], w=[t_sm])

        def h3(tb):
            sm, t_sm = sms[tb % NR]
            act(sm["s"][:, 5:6], sm["s"][:, 4:5], AF.Sqrt, r=[t_sm], w=[t_sm], bias=LN_EPS, scale=-1.0)

        def h4(tb):
            sm, t_sm = sms[tb % NR]
            vop("reciprocal", out=sm["s"][:, 5:6], in_=sm["s"][:, 5:6], r=[t_sm], w=[t_sm])
            stt(sm["s"][:, 6:7], sm["s"][:, 2:3], -1.0, sm["s"][:, 5:6], ALU.mult, ALU.mult, r=[t_sm], w=[t_sm])

        def h5(tb):
            kb = tb % NR
            sm, t_sm = sms[kb]
            act(xbs[kb][:], xbs[kb][:], AF.Identity, r=[t_xbs[kb], t_sm], w=[t_xbs[kb]], scale=sm["s"][:, 5:6], bias=sm["s"][:, 6:7])

        def h6(tb):
            kb = tb % NR
            tt(xbs[kb][:], xbs[kb][:], G[:], ALU.mult, r=[t_xbs[kb], t_gb], w=[t_xbs[kb]], eng="vector")

        def h7(tb):
            kb = tb % NR
            tt(xbs[kb][:], xbs[kb][:], Bt[:], ALU.add, r=[t_xbs[kb], t_gb], w=[t_xbs[kb]], eng="gpsimd")
        return [h1, h2, h3, h4, h5, h6, h7]

    def small_sm():
        return {"s": A.tile([128, 8], F32, "lns")}, Tok()

    class Skew:
        def __init__(self, sk):
            self.sk = sk
            self.q = []

        def push(self, front, back):
            front()
            self.q.append(back)
            while len(self.q) > self.sk:
                self.q.pop(0)()

        def flush(self):
            while self.q:
                self.q.pop(0)()

    def emit_rope(xg, t_x, H, n, ct, st_, t1, t2, ro, t_t1, t_t2, t_ro):
        def v5(tl):
            return tl[:].rearrange("p (h f j i) -> p h f j i", h=H, f=2, j=2)
        for f in range(2):
            cb = ct[:, f, :].unsqueeze(1).unsqueeze(1).to_broadcast([128, H, 2, n])
            sb = st_[:, f, :].unsqueeze(1).to_broadcast([128, H, n])
            tt(v5(t1)[:, :, f], v5(xg)[:, :, f], cb, ALU.mult, r=[t_x, t_const], w=[t_t1], eng="vector")
            tt(v5(t2)[:, :, f, 0, :], v5(xg)[:, :, f, 1, :], sb, ALU.mult, r=[t_x, t_const], w=[t_t2], eng="gpsimd")
            tt(v5(t2)[:, :, f, 1, :], v5(xg)[:, :, f, 0, :], sb, ALU.mult, r=[t_x, t_const], w=[t_t2], eng="gpsimd")

        def v4(tl):
            return tl[:].rearrange("p (hf j i) -> p hf j i", j=2, i=n)
        tt(v4(ro)[:, :, 0, :], v4(t1)[:, :, 0, :], v4(t2)[:, :, 0, :], ALU.subtract, r=[t_t1, t_t2], w=[t_ro], eng="vector")
        tt(v4(ro)[:, :, 1, :], v4(t1)[:, :, 1, :], v4(t2)[:, :, 1, :], ALU.add, r=[t_t1, t_t2], w=[t_ro], eng="gpsimd")

    def stage_init():
        A.reset()
        xb = [A.tile([128, D], F32, "xb") for _ in range(2)]
        t_xb = [Tok(), Tok()]
        xst = [A.tile([128, 8, 512], BF16, "xst") for _ in range(2)]
        t_xst = [Tok(), Tok()]
        for i in range(4):
            k = i % 2
            dma("sync", xb[k][:], mem_in[i * 128:(i + 1) * 128, :], w=[t_xb[k]])
            for c in range(8):
                b = 0 if c < 4 else 1
                tr(bank(b, (c % 4) * 128, (c % 4) * 128 + 128), xb[k][:, c * 128:(c + 1) * 128], r=[t_xb[k]], w=[PT[b]])
            cp(memT[:, 0:4, i * 128:(i + 1) * 128], bank(0).rearrange("p (c t) -> p c t", c=4), r=[PT[0]], w=[t_memT], eng="scalar")
            cp(memT[:, 4:8, i * 128:(i + 1) * 128], bank(1).rearrange("p (c t) -> p c t", c=4), r=[PT[1]], w=[t_memT], eng="vector")
        for tb in range(NB):
            k = tb % 2
            ck, tbl = tb // 4, tb % 4
            dma("sync", xb[k][:], x_in[tb * 128:(tb + 1) * 128, :], w=[t_xb[k]])
            emit_xT(xb[k], t_xb[k], xst[ck % 2], t_xst[ck % 2], tbl, 2 + 2 * k, 3 + 2 * k)
            if tbl == 3:
                dma("sync", XT16v[:, :, ck * 512:(ck + 1) * 512], xst[ck % 2][:], r=[t_xst[ck % 2]], w=[t_XT[ck]])

    def stage_A(l):
        A.reset()
        win = A.tile([128, 8, 1952], BF16, "win")
        t_win = [Tok() for _ in range(8)]
        for c in range(8):
            dma("gpsimd", win[:, c, :], Wt["w_in"][l, c * 128:(c + 1) * 128, :], w=[t_win[c]])
        t_par = Tok()
        G10 = A.tile([128, 640], F32, "G10")
        for h in range(10):
            src = Wt["gqa_q_norm"][l] if h < 8 else Wt["gqa_k_norm"][l]
            dma("sync", G10[:, h * 64:(h + 1) * 64], src.partition_broadcast(128), w=[t_par])
        Gm = A.tile([128, 384], F32, "Gm")
        dma("sync", Gm[:, 0:256], Wt["mla_q_norm"][l].partition_broadcast(128), w=[t_par])
        dma("sync", Gm[:, 256:384], Wt["mla_kv_norm"][l].partition_broadcast(128), w=[t_par])
        wuq = A.tile([128, 2, 384], BF16, "wuq")
        wukv = A.tile([128, 512], BF16, "wukv")
        dma("gpsimd", wuq[:], Wt["mla_w_uq"][l].rearrange("(c p) n -> p c n", p=128), w=[t_par])
        dma("gpsimd", wukv[:], Wt["mla_w_ukv"][l], w=[t_par])

        xt = [A.tile([128, 8, 512], BF16, "xt") for _ in range(2)]
        t_xt = [Tok(), Tok()]
        uts = [A.tile([128, 512], F32, "uts") for _ in range(2)]
        t_uts = [Tok(), Tok()]
        QS = [A.tile([128, 6, 512], BF16, "QS") for _ in range(2)]
        t_QS = [Tok(), Tok()]
        QCS = [A.tile([96, 4, 512], BF16, "QCS") for _ in range(2)]
        t_QCS = [Tok(), Tok()]
        KCS = [A.tile([96, 4, 512], BF16, "KCS") for _ in range(2)]
        t_KCS = [Tok(), Tok()]
        sqt = A.tile([128, 640], F32, "sqt"); t_sqt = Tok()
        ss = A.tile([128, 16], F32, "ss"); t_ss = Tok()
        xg = A.tile([128, 640], F32, "xg"); t_xg = Tok()
        r1 = A.tile([128, 640], F32, "r1"); t_r1 = Tok()
        r2 = A.tile([128, 640], F32, "r2"); t_r2 = Tok()
        ro = A.tile([128, 640], F32, "ro"); t_ro = Tok()
        KK = A.tile([128, 256], F32, "KK"); t_KK = Tok()
        CN = A.tile([128, 384], F32, "CN"); t_CN = Tok()
        cnT = A.tile([128, 3, 128], BF16, "cnT"); t_cnT = Tok()
        RM = A.tile([128, 160], F32, "RM"); t_RM = Tok()
        m1 = A.tile([128, 160], F32, "m1"); t_m1 = Tok()
        m2 = A.tile([128, 160], F32, "m2"); t_m2 = Tok()
        mo = A.tile([128, 160], F32, "mo"); t_mo = Tok()
        QC = A.tile([128, 4, 96], F32, "QC"); t_QC = Tok()
        KC = A.tile([128, 4, 96], F32, "KC"); t_KC = Tok()

        ui = 0
        for ck in range(8):
            k2 = ck % 2
            dma("sync", xt[k2][:], XT16v[:, :, ck * 512:(ck + 1) * 512], r=[t_XT[ck]], w=[t_xt[k2]])
            for cc in range(6):
                for d in range(8):
                    mm(bank(3), win[:, d, 768 + cc * 128:768 + (cc + 1) * 128], xt[k2][:, d, :], d == 0, d == 7,
                       r=[t_win[d], t_xt[k2]], w=[PT[3]])
                u = ui % 2
                ui += 1
                cp(uts[u][:], bank(3), r=[PT[3]], w=[t_uts[u]], eng="scalar")
                dma("sync", UT[cc * 128:(cc + 1) * 128, ck * 512:(ck + 1) * 512], uts[u][:], r=[t_uts[u]], w=[t_UT[ck]])
            for tbl in range(4):
                tb = ck * 4 + tbl
                tb16 = tb % 16
                tc0, tc1 = tbl * 128, (tbl + 1) * 128
                for d in range(8):
                    mm(bank(0), xt[k2][:, d, tc0:tc1], win[:, d, 0:512], d == 0, d == 7, r=[t_win[d], t_xt[k2]], w=[PT[0]])
                for d in range(8):
                    mm(bank(1, 0, 256), xt[k2][:, d, tc0:tc1], win[:, d, 512:768], d == 0, d == 7, r=[t_win[d], t_xt[k2]], w=[PT[1]])
                for d in range(8):
                    mm(bank(2, 0, 416), xt[k2][:, d, tc0:tc1], win[:, d, 1536:1952], d == 0, d == 7, r=[t_win[d], t_xt[k2]], w=[PT[2]])
                if CUT <= 1:
                    continue
                qk = ps[:, 0:640]
                act(sqt[:], qk, AF.Square, r=[PT[0], PT[1]], w=[t_sqt])
                vop("tensor_reduce", out=ss[:, 0:10], in_=sqt[:].rearrange("p (h d) -> p h d", d=64), axis=AX.X, op=ALU.add,
                    r=[t_sqt], w=[t_ss])
                act(ss[:, 0:10], ss[:, 0:10], AF.Sqrt, r=[t_ss], w=[t_ss], scale=1.0 / 64, bias=RMS_EPS)
                vop("reciprocal", out=ss[:, 0:10], in_=ss[:, 0:10], r=[t_ss], w=[t_ss])
                tt(xg[:].rearrange("p (h d) -> p h d", d=64), qk.rearrange("p (h d) -> p h d", d=64),
                   ss[:, 0:10].unsqueeze(2).to_broadcast([128, 10, 64]), ALU.mult, r=[PT[0], PT[1], t_ss], w=[t_xg])
                tt(xg[:], xg[:], G10[:], ALU.mult, r=[t_xg, t_par], w=[t_xg], eng="gpsimd")
                if CUT <= 2:
                    continue
                emit_rope(xg, t_xg, 10, 16, CQ[:, tb16, :].rearrange("p (f i) -> p f i", f=2),
                          SQ[:, tb16, :].rearrange("p (f i) -> p f i", f=2), r1, r2, ro, t_r1, t_r2, t_ro)
                if CUT <= 3:
                    continue
                for rr in range(2):
                    cp(KK[:].rearrange("p (h r d) -> p h r d", h=2, r=2)[:, :, rr, :],
                       ro[:, 512:640].rearrange("p (h d) -> p h d", h=2), r=[t_ro], w=[t_KK], eng="gpsimd")
                for j in range(6):
                    src = ro[:, j * 128:(j + 1) * 128] if j < 4 else KK[:, (j - 4) * 128:(j - 3) * 128]
                    b = 5 if j < 3 else 6
                    tr(bank(b, (j % 3) * 128, (j % 3) * 128 + 128), src, r=[t_ro, t_KK], w=[PT[b]])
                cp(QS[k2][:, 0:3, tc0:tc1], bank(5, 0, 384).rearrange("p (c t) -> p c t", c=3), r=[PT[5]], w=[t_QS[k2]], eng="scalar")
                cp(QS[k2][:, 3:6, tc0:tc1], bank(6, 0, 384).rearrange("p (c t) -> p c t", c=3), r=[PT[6]], w=[t_QS[k2]], eng="vector")
                if CUT <= 4:
                    continue
                cp(VA[:, tb, :, 0:64], bank(1, 128, 256).rearrange("p (h d) -> p h d", h=2), r=[PT[1]], w=[t_VA[tb]], eng="scalar")
                act(sqt[:, 0:384], bank(2, 0, 384), AF.Square, r=[PT[2]], w=[t_sqt])
                vop("tensor_reduce", out=ss[:, 10:11], in_=sqt[:, 0:256], axis=AX.X, op=ALU.add, r=[t_sqt], w=[t_ss])
                vop("tensor_reduce", out=ss[:, 11:12], in_=sqt[:, 256:384], axis=AX.X, op=ALU.add, r=[t_sqt], w=[t_ss])
                act(ss[:, 10:11], ss[:, 10:11], AF.Sqrt, r=[t_ss], w=[t_ss], scale=1.0 / 256, bias=RMS_EPS)
                act(ss[:, 11:12], ss[:, 11:12], AF.Sqrt, r=[t_ss], w=[t_ss], scale=1.0 / 128, bias=RMS_EPS)
                vop("reciprocal", out=ss[:, 10:12], in_=ss[:, 10:12], r=[t_ss], w=[t_ss])
                ts(CN[:, 0:256], bank(2, 0, 256), ss[:, 10:11], None, ALU.mult, r=[PT[2], t_ss], w=[t_CN])
                ts(CN[:, 256:384], bank(2, 256, 384), ss[:, 11:12], None, ALU.mult, r=[PT[2], t_ss], w=[t_CN])
                tt(CN[:], CN[:], Gm[:], ALU.mult, r=[t_CN, t_par], w=[t_CN], eng="gpsimd")
                for j in range(3):
                    tr(bank(7, j * 128, (j + 1) * 128), CN[:, j * 128:(j + 1) * 128], r=[t_CN], w=[PT[7]])
                cp(cnT[:], bank(7, 0, 384).rearrange("p (c t) -> p c t", c=3), r=[PT[7]], w=[t_cnT], eng="scalar")
                cp(RM[:, 128:160], bank(2, 384, 416), r=[PT[2]], w=[t_RM], eng="vector")
                if CUT <= 5:
                    continue
                for kc in range(2):
                    mm(bank(0, 0, 384), cnT[:, kc, :], wuq[:, kc, :], kc == 0, kc == 1, r=[t_cnT, t_par], w=[PT[0]])
                mm(bank(4), cnT[:, 2, :], wukv[:], True, True, r=[t_cnT, t_par], w=[PT[4]])
                if SUB <= 1:
                    continue
                qcv = bank(0, 0, 384).rearrange("p (h e) -> p h e", h=4)
                kvv = bank(4).rearrange("p (h e) -> p h e", h=4)
                cp(RM[:, 0:128].rearrange("p (h e) -> p h e", h=4), qcv[:, :, 64:96], r=[PT[0]], w=[t_RM], eng="vector")
                cp(QC[:, :, 0:64], qcv[:, :, 0:64], r=[PT[0]], w=[t_QC], eng="vector")
                if SUB <= 2:
                    continue
                cp(KC[:, :, 0:64], kvv[:, :, 0:64], r=[PT[4]], w=[t_KC], eng="vector")
                cp(VC[:, tb, :, 0:64], kvv[:, :, 64:128], r=[PT[4]], w=[t_VC[tb]], eng="scalar")
                if CUT <= 6:
                    continue
                emit_rope(RM, t_RM, 5, 8, CM[:, tb16, :].rearrange("p (f i) -> p f i", f=2),
                          SM[:, tb16, :].rearrange("p (f i) -> p f i", f=2), m1, m2, mo, t_m1, t_m2, t_mo)
                cp(QC[:, :, 64:96], mo[:, 0:128].rearrange("p (h e) -> p h e", h=4), r=[t_mo], w=[t_QC], eng="gpsimd")
                cp(KC[:, :, 64:96], mo[:, 128:160].unsqueeze(1).to_broadcast([128, 4, 32]), r=[t_mo], w=[t_KC], eng="gpsimd")
                if CUT <= 7:
                    continue
                for h in range(4):
                    tr(bank(1, h * 128, (h + 1) * 128, 0, 96), QC[:, h, :], r=[t_QC], w=[PT[1]])
                for h in range(4):
                    tr(bank(2, h * 128, (h + 1) * 128, 0, 96), KC[:, h, :], r=[t_KC], w=[PT[2]])
                cp(QCS[k2][:, :, tc0:tc1], bank(1, 0, 512, 0, 96).rearrange("p (c t) -> p c t", c=4), r=[PT[1]], w=[t_QCS[k2]], eng="scalar")
                cp(KCS[k2][:, :, tc0:tc1], bank(2, 0, 512, 0, 96).rearrange("p (c t) -> p c t", c=4), r=[PT[2]], w=[t_KCS[k2]], eng="vector")
            c0, c1 = ck * 512, (ck + 1) * 512
            dma("sync", QTd[:, :, c0:c1].rearrange("j p t -> p j t"), QS[k2][:], r=[t_QS[k2]], w=[t_QT[ck]])
            dma("sync", QCTd[:, :, c0:c1].rearrange("j p t -> p j t"), QCS[k2][:], r=[t_QCS[k2]], w=[t_QCT[ck]])
            dma("sync", KCTd[:, :, c0:c1].rearrange("j p t -> p j t"), KCS[k2][:], r=[t_KCS[k2]], w=[t_KCT[ck]])

    def stage_ATT():
        A.reset()
        qt = [A.tile([128, L], BF16, "qt") for _ in range(2)]
        kt = [A.tile([128, L], BF16, "kt") for _ in range(2)]
        t_qk = [Tok(), Tok()]
        NE = 6
        et = [A.tile([128, 512], BF16, "et") for _ in range(NE)]
        t_et = [Tok() for _ in range(NE)]
        rec = [A.tile([128, 512], F32, "rec") for _ in range(2)]
        t_rec = [Tok(), Tok()]
        bcs = [A.tile([64, 512], F32, "bcs") for _ in range(2)]
        t_bcs = [Tok(), Tok()]
        ot = [A.tile([64, 512], BF16, "ot") for _ in range(2)]
        t_ot = [Tok(), Tok()]
        items = []
        cnt = {"l": 0, "g": 0}

        def add_head(seq, li, pr0, pr1, vtile, t_v, hv, scale, arow, pre):
            qtile, ktile, t_in = qt[li], kt[li], t_qk[li]
            for qb in range(4):
                g = cnt["g"]
                cnt["g"] += 1
                ob = 4 + g % 2
                gi = g % 2
                for kb in range(16):
                    idx = len(items)
                    sb = idx % 4
                    ei = idx % NE

                    def qk(sb=sb, ei=ei, kb=kb, qb=qb):
                        mm(bank(sb), ktile[pr0:pr1, kb * 128:(kb + 1) * 128], qtile[pr0:pr1, qb * 512:(qb + 1) * 512], True, True,
                           r=[t_in], w=[PT[sb]])
                        act(et[ei][:], bank(sb), AF.Exp, r=[PT[sb]], w=[t_et[ei]], scale=scale)

                    def pv(ei=ei, kb=kb, ob=ob):
                        mm(bank(ob, 0, 512, 0, 65), vtile[:, seq * 16 + kb, hv, :], et[ei][:], kb == 0, kb == 15,
                           r=[t_et[ei], t_v[seq * 16 + kb]], w=[PT[ob]])

                    n1 = n2 = None
                    if kb == 15:
                        def n1(ob=ob, gi=gi):
                            vop("reciprocal", out=rec[gi][64:65, :], in_=bank(ob, 0, 512, 64, 65), r=[PT[ob]], w=[t_rec[gi]])

                        def n2(ob=ob, gi=gi, qb=qb):
                            mm(bank(6 + gi, 0, 512, 0, 64), ones_f[64:65, 0:64], rec[gi][64:65, :], True, True,
                               r=[t_rec[gi], t_const], w=[PT[6 + gi]])
                            cp(bcs[gi][:], bank(6 + gi, 0, 512, 0, 64), r=[PT[6 + gi]], w=[t_bcs[gi]], eng="vector")
                            tt(ot[gi][:], bank(ob, 0, 512, 0, 64), bcs[gi][:], ALU.mult, r=[PT[ob], t_bcs[gi]], w=[t_ot[gi]])
                            c0 = seq * L + qb * 512
                            dma("sync", AT[arow:arow + 64, c0:c0 + 512], ot[gi][:], r=[t_ot[gi]], w=[tAT((arow // 128, c0 // 512))])
                    items.append((pre if (qb == 0 and kb == 0) else None, qk, pv, n1, n2))

        for seq in range(2):
            rq = [t_QT[seq * 4 + i] for i in range(4)]
            for j in range(4):
                li = cnt["l"] % 2
                cnt["l"] += 1

                def pre(li=li, j=j, seq=seq, rq=rq):
                    dma("sync", qt[li][:], QTd[j, :, seq * L:(seq + 1) * L], r=rq, w=[t_qk[li]])
                    dma("sync", kt[li][:], QTd[4 + j // 2, :, seq * L:(seq + 1) * L], r=rq, w=[t_qk[li]])
                for hh in range(2):
                    add_head(seq, li, hh * 64, hh * 64 + 64, VA, t_VA, j // 2, 0.125, (2 * j + hh) * 64, pre if hh == 0 else None)
            rq2 = [t_QCT[seq * 4 + i] for i in range(4)] + [t_KCT[seq * 4 + i] for i in range(4)]
            for h in range(4):
                li = cnt["l"] % 2
                cnt["l"] += 1

                def pre(li=li, h=h, seq=seq, rq2=rq2):
                    dma("sync", qt[li][0:96, :], QCTd[h, :, seq * L:(seq + 1) * L], r=rq2, w=[t_qk[li]])
                    dma("sync", kt[li][0:96, :], KCTd[h, :, seq * L:(seq + 1) * L], r=rq2, w=[t_qk[li]])
                add_head(seq, li, 0, 96, VC, t_VC, h, float(96 ** -0.5), 768 + h * 64, pre)
        D1, D2 = 3, 7
        n = len(items)
        for i in range(n + D2 + 1):
            if i < n:
                if items[i][0] is not None:
                    items[i][0]()
                items[i][1]()
            if 0 <= i - D1 < n:
                items[i - D1][2]()
                if items[i - D1][3] is not None:
                    items[i - D1][3]()
            if 0 <= i - D2 < n and items[i - D2][4] is not None:
                items[i - D2][4]()

    def stage_WO(l, xsrc, have_hyena):
        A.reset()
        wo = A.tile([128, 8, D], BF16, "wo")
        t_wo = [Tok() for _ in range(8)]
        for c in range(8):
            dma("gpsimd", wo[:, c, :], Wt["w_o"][l, c * 128:(c + 1) * 128, :], w=[t_wo[c]])
        G, Bt, t_gb = load_gb("ln1_g", "ln1_b", l)
        at = [A.tile([128, 8, 512], BF16, "at") for _ in range(2)]
        t_at = [Tok(), Tok()]
        xst = [A.tile([128, 8, 512], BF16, "xst") for _ in range(2)]
        t_xst = [Tok(), Tok()]
        if not have_hyena:
            zt_ = A.tile([128, 512], BF16, "zero")
            t_z = Tok()
            P.op("vector", lambda e: e.memset(zt_[:], 0.0), w=[t_z])
            for c in (4, 5):
                for ck in range(8):
                    dma("sync", AT[c * 128:(c + 1) * 128, ck * 512:(ck + 1) * 512], zt_[:], r=[t_z], w=[tAT((c, ck))])
        NR = 12
        sms = [small_sm() for _ in range(NR)]
        junks = [(A.tile([128, D], BF16, "junk"), Tok()) for _ in range(2)]
        xbs = [A.tile([128, D], F32, "xbr") for _ in range(NR)]
        t_xbs = [Tok() for _ in range(NR)]

        def s0(tb):
            ck, tbl = tb // 4, tb % 4
            k2 = ck % 2
            kb = tb % NR
            yb_ = 2 * (tb % 2)
            if tbl == 0:
                dma("sync", at[k2][:], ATv[:, :, ck * 512:(ck + 1) * 512], r=[tAT((c, ck)) for c in range(8)], w=[t_at[k2]])
            dma("sync", xbs[kb][:], xsrc[tb * 128:(tb + 1) * 128, :], r=[t_X32[tb]], w=[t_xbs[kb]])
            for half in range(2):
                for c in range(8):
                    mm(bank(yb_ + half), at[k2][:, c, tbl * 128:(tbl + 1) * 128], wo[:, c, half * 512:(half + 1) * 512],
                       c == 0, c == 7, r=[t_at[k2], t_wo[c]], w=[PT[yb_ + half]])
            stt(xbs[kb][:], xbs[kb][:], ALPHA, ps[:, yb_ * 512:(yb_ + 2) * 512], ALU.mult, ALU.add,
                r=[t_xbs[kb], PT[yb_], PT[yb_ + 1]], w=[t_xbs[kb]])

        def s8(tb):
            ck, tbl = tb // 4, tb % 4
            k2 = ck % 2
            kb = tb % NR
            dma("sync", X32[tb * 128:(tb + 1) * 128, :], xbs[kb][:], r=[t_xbs[kb]], w=[t_X32[tb]])
            tbk = 4 + 2 * (tb % 2)
            emit_xT(xbs[kb], t_xbs[kb], xst[k2], t_xst[k2], tbl, tbk, tbk + 1)

        def s9(tb):
            ck, tbl = tb // 4, tb % 4
            k2 = ck % 2
            if tbl == 3:
                dma("sync", XT16v[:, :, ck * 512:(ck + 1) * 512], xst[k2][:], r=[t_xst[k2]], w=[t_XT[ck]])
        run_pipe(NB, [s0] + ln_hops(xbs, t_xbs, sms, junks, G, Bt, t_gb, NR) + [s8, s9])

    def stage_final():
        for tb in range(NB):
            dma("sync", out[tb * 128:(tb + 1) * 128, :], X32[tb * 128:(tb + 1) * 128, :], r=[t_X32[tb]])

    def stage_XA(l):
        A.reset()
        wq = A.tile([128, 8, D], BF16, "wq")
        wo = A.tile([128, 8, D], BF16, "wo")
        t_wkv = [Tok() for _ in range(8)]
        t_wq = [Tok() for _ in range(8)]
        t_wo = [Tok() for _ in range(8)]
        G = A.tile([128, D], F32, "G"); Bt = A.tile([128, D], F32, "B")
        t_gb = Tok()
        KmT = A.tile([128, 8, 512], BF16, "KmT"); t_Km = Tok()
        Vm = A.tile([128, 4, D], BF16, "Vm"); t_Vm = Tok()
        xt = A.tile([128, 8, 512], BF16, "xt"); t_xt = Tok()
        QxT = A.tile([128, 8, 512], BF16, "QxT"); t_Qx = Tok()
        axT = A.tile([128, 8, 512], BF16, "axT"); t_ax = Tok()
        et = [A.tile([128, 512], BF16, "et") for _ in range(4)]
        t_et = [Tok() for _ in range(4)]
        rden = A.tile([128, 512], F32, "rden"); t_rden = Tok()
        xst = [A.tile([128, 8, 512], BF16, "xst") for _ in range(2)]
        t_xst = [Tok(), Tok()]
        off_wkv = A.off
        wkv = A.tile([128, 8, 2048], BF16, "wkv")
        for c in range(8):
            dma("gpsimd", wkv[:, c, :], Wt["xa_wkv"][l, c * 128:(c + 1) * 128, :], w=[t_wkv[c]])
        for c in range(8):
            dma("gpsimd", wq[:, c, :], Wt["xa_wq"][l, c * 128:(c + 1) * 128, :], w=[t_wq[c]])
        for c in range(8):
            dma("gpsimd", wo[:, c, :], Wt["xa_wo"][l, c * 128:(c + 1) * 128, :], w=[t_wo[c]])
        dma("sync", G[:], Wt["ln2_g"][l].partition_broadcast(128), w=[t_gb])
        dma("sync", Bt[:], Wt["ln2_b"][l].partition_broadcast(128), w=[t_gb])
        for c in range(8):
            b = c % 2
            for d in range(8):
                mm(bank(b), wkv[:, d, c * 128:(c + 1) * 128], memT[:, d, :], d == 0, d == 7, r=[t_wkv[d], t_memT], w=[PT[b]])
            cp(KmT[:, c, :], bank(b), r=[PT[b]], w=[t_Km], eng="scalar" if b == 0 else "vector")
        for sb in range(4):
            for half in range(2):
                b = 2 + half
                for d in range(8):
                    mm(bank(b), memT[:, d, sb * 128:(sb + 1) * 128], wkv[:, d, 1024 + half * 512:1024 + (half + 1) * 512],
                       d == 0, d == 7, r=[t_wkv[d], t_memT], w=[PT[b]])
                cp(Vm[:, sb, half * 512:(half + 1) * 512], bank(b), r=[PT[b]], w=[t_Vm], eng="scalar" if half == 0 else "vector")
        P.barrier()
        A.off = off_wkv
        NR = 10
        sms = [small_sm() for _ in range(NR)]
        junks = [(A.tile([128, D], BF16, "junk"), Tok()) for _ in range(2)]
        xbs = [A.tile([128, D], F32, "xbr") for _ in range(NR)]
        t_xbs = [Tok() for _ in range(NR)]
        zbs = [A.tile([128, D], F32, "zbr") for _ in range(2)]
        t_zbs = [Tok(), Tok()]
        ei = [0]

        def chunk_front(ck):
            seq = ck // 4
            dma("sync", xt[:], XT16v[:, :, ck * 512:(ck + 1) * 512], r=[t_XT[ck]], w=[t_xt])
            for c in range(8):
                b = c % 2
                for d in range(8):
                    mm(bank(b), wq[:, d, c * 128:(c + 1) * 128], xt[:, d, :], d == 0, d == 7, r=[t_wq[d], t_xt], w=[PT[b]])
                cp(QxT[:, c, :], bank(b), r=[PT[b]], w=[t_Qx], eng="scalar" if b == 0 else "vector")
            skh = Skew(1)
            for h in range(4):
                es = []
                for kb in range(2):
                    es.append(ei[0] % 4)
                    ei[0] += 1

                def hfront(h=h, es=es):
                    for kb in range(2):
                        e_i = es[kb]
                        for cc in range(2):
                            mm(bank(2 + kb), KmT[:, 2 * h + cc, seq * 256 + kb * 128:seq * 256 + (kb + 1) * 128], QxT[:, 2 * h + cc, :],
                               cc == 0, cc == 1, r=[t_Km, t_Qx], w=[PT[2 + kb]])
                        act(et[e_i][:], bank(2 + kb), AF.Exp, r=[PT[2 + kb]], w=[t_et[e_i]], scale=1.0 / 16)

                def hback(h=h, es=es):
                    for kb in range(2):
                        mm(bank(4), ones_bf[:], et[es[kb]][:], kb == 0, kb == 1, r=[t_et[es[kb]], t_const], w=[PT[4]])
                    vop("reciprocal", out=rden[:], in_=bank(4), r=[PT[4]], w=[t_rden])
                    for dc in range(2):
                        for kb in range(2):
                            mm(bank(5 + dc), Vm[:, seq * 2 + kb, h * 256 + dc * 128:h * 256 + (dc + 1) * 128], et[es[kb]][:],
                               kb == 0, kb == 1, r=[t_et[es[kb]], t_Vm], w=[PT[5 + dc]])
                        tt(axT[:, 2 * h + dc, :], bank(5 + dc), rden[:], ALU.mult, r=[PT[5 + dc], t_rden], w=[t_ax])
                skh.push(hfront, hback)
            skh.flush()

        def s0(tb):
            ck, tbl = tb // 4, tb % 4
            kb = tb % NR
            if tbl == 0:
                chunk_front(ck)
            dma("sync", xbs[kb][:], X32[tb * 128:(tb + 1) * 128, :], r=[t_X32[tb]], w=[t_xbs[kb]])
            for half in range(2):
                for c in range(8):
                    mm(bank(half), axT[:, c, tbl * 128:(tbl + 1) * 128], wo[:, c, half * 512:(half + 1) * 512],
                       c == 0, c == 7, r=[t_ax, t_wo[c]], w=[PT[half]])
            stt(xbs[kb][:], xbs[kb][:], ALPHA, ps[:, 0:1024], ALU.mult, ALU.add, r=[t_xbs[kb], PT[0], PT[1]], w=[t_xbs[kb]])

        def s8(tb):
            ck, tbl = tb // 4, tb % 4
            k2 = ck % 2
            kb = tb % NR
            z2 = tb % 2
            dma("sync", X32[tb * 128:(tb + 1) * 128, :], xbs[kb][:], r=[t_xbs[kb]], w=[t_X32[tb]])
            act(zbs[z2][:], xbs[kb][:], AF.Copy, r=[t_xbs[kb]], w=[t_zbs[z2]], scale=ALPHA)
            emit_xT(xbs[kb], t_xbs[kb], xst[k2], t_xst[k2], tbl, 7, 6)

        def s9(tb):
            ck, tbl = tb // 4, tb % 4
            k2 = ck % 2
            z2 = tb % 2
            dma("sync", Z32[tb * 128:(tb + 1) * 128, :], zbs[z2][:], r=[t_zbs[z2]], w=[t_Z32[tb]])
            if tbl == 3:
                dma("sync", XT16v[:, :, ck * 512:(ck + 1) * 512], xst[k2][:], r=[t_xst[k2]], w=[t_XT[ck]])
        run_pipe(NB, [s0] + ln_hops(xbs, t_xbs, sms, junks, G, Bt, t_gb, NR) + [s8, s9])

    def stage_MOE(l):
        A.reset()
        wr = A.tile([128, 8, 16], BF16, "wr"); t_wr = Tok()
        dma("gpsimd", wr[:], Wt["moe_router"][l].rearrange("(c p) e -> p c e", p=128), w=[t_wr])
        o48 = A.tile([48, 1], F32, "o48")
        dma("sync", o48[:], Cn["o48"], w=[t_wr])
        xt = A.tile([128, 8, 512], BF16, "xt"); t_xt = Tok()
        AF48 = A.tile([128, 16, 48], F32, "AF48"); t_AF = Tok()
        P.op("vector", lambda e: e.memset(AF48[:], 0.0), w=[t_AF])
        NRr = 4
        exs = [A.tile([128, 16], F32, "ex") for _ in range(NRr)]
        t_exs = [Tok() for _ in range(NRr)]
        sxs = [A.tile([128, 2], F32, "sx") for _ in range(NRr)]
        t_sxs = [Tok() for _ in range(NRr)]
        xts = [A.tile([128, 8, 512], BF16, "xtr") for _ in range(2)]
        t_xts = [Tok(), Tok()]
        AffT = A.tile([48, L], F32, "AffT"); t_AffT = Tok()
        MX = A.tile([48, 256], F32, "MX"); t_MX = Tok()
        IX = A.tile([48, 256], U32, "IX"); t_IX = Tok()
        IXF = A.tile([48, 256], F32, "IXF"); t_IXF = Tok()
        IDXT = A.tile([128, 2, 48], I32, "IDXT"); t_IDXT = Tok()
        GT = A.tile([128, 2, 48], F32, "GT"); t_GT = Tok()
        def r0(tb):
            ck, tbl = tb // 4, tb % 4
            k2 = ck % 2
            b = tb % NRr
            if tbl == 0:
                dma("sync", xts[k2][:], XT16v[:, :, ck * 512:(ck + 1) * 512], w=[t_xts[k2]])
            for d in range(8):
                mm(bank(b, 0, 16), xts[k2][:, d, tbl * 128:(tbl + 1) * 128], wr[:, d, :], d == 0, d == 7, r=[t_xts[k2], t_wr], w=[PT[b]])

        def r1(tb):
            b = tb % NRr
            act(exs[b][:], bank(b, 0, 16), AF.Exp, r=[PT[b]], w=[t_exs[b], t_sxs[b]], accum_out=sxs[b][:, 0:1])

        def r2(tb):
            b = tb % NRr
            vop("reciprocal", out=sxs[b][:, 1:2], in_=sxs[b][:, 0:1], r=[t_sxs[b]], w=[t_sxs[b]])

        def r3(tb):
            b = tb % NRr
            seq, tbs = tb // 16, tb % 16
            ts(AF48[:, tbs, seq * 32:seq * 32 + 16], exs[b][:], sxs[b][:, 1:2], None, ALU.mult, r=[t_exs[b], t_sxs[b]], w=[t_AF])
        run_pipe(NB, [r0, r1, r2, r3])
        for tbs in range(16):
            b = 1 + tbs % 2
            tr(bank(b, 0, 128, 0, 48), AF48[:, tbs, :], r=[t_AF], w=[PT[b]])
            cp(AffT[:, tbs * 128:(tbs + 1) * 128], bank(b, 0, 128, 0, 48), r=[PT[b]], w=[t_AffT], eng="vector")
        for rd in range(32):
            sl = slice(rd * 8, rd * 8 + 8)
            vop("max", out=MX[:, sl], in_=AffT[:], r=[t_AffT], w=[t_MX])
            vop("max_index", out=IX[:, sl], in_max=MX[:, sl], in_values=AffT[:], r=[t_AffT, t_MX], w=[t_IX])
            vop("match_replace", out=AffT[:], in_to_replace=MX[:, sl], in_values=AffT[:], imm_value=-1.0, r=[t_MX, t_IX], w=[t_AffT])
        cp(IXF[:], IX[:], r=[t_IX], w=[t_IXF], eng="vector")
        ts(IXF[:], IXF[:], o48[:, 0:1], None, ALU.add, r=[t_IXF, t_wr], w=[t_IXF])
        for half in range(2):
            tr(bank(3, 0, 48), IXF[:, half * 128:(half + 1) * 128], r=[t_IXF], w=[PT[3]])
            cp(IDXT[:, half, :], bank(3, 0, 48), r=[PT[3]], w=[t_IDXT], eng="vector")
            tr(bank(4, 0, 48), MX[:, half * 128:(half + 1) * 128], r=[t_MX], w=[PT[4]])
            cp(GT[:, half, :], bank(4, 0, 48), r=[PT[4]], w=[t_GT], eng="vector")
        wg = [A.tile([128, 8, 512], BF16, "wg") for _ in range(2)]
        wu = [A.tile([128, 8, 512], BF16, "wu") for _ in range(2)]
        wd = [A.tile([128, 4, D], BF16, "wd") for _ in range(2)]
        t_w = [Tok(), Tok()]
        xg = [A.tile([128, D], F32, "xg") for _ in range(2)]
        t_xg = [Tok(), Tok()]
        xeT = A.tile([128, 8, 512], BF16, "xeT"); t_xe = Tok()
        sg = [A.tile([128, 512], F32, "sg") for _ in range(2)]
        t_sg = [Tok(), Tok()]
        hidT = A.tile([128, 4, 512], BF16, "hidT"); t_hid = Tok()
        yb = [A.tile([128, D], F32, "yb") for _ in range(2)]
        t_yb = [Tok(), Tok()]

        def load_w(e):
            k = e % 2
            dma("gpsimd", wg[k][:], Wt["moe_w_gate"][l, e].rearrange("(c p) f -> p c f", p=128), w=[t_w[k]])
            dma("gpsimd", wu[k][:], Wt["moe_w_up"][l, e].rearrange("(c p) f -> p c f", p=128), w=[t_w[k]])
            dma("gpsimd", wd[k][:], Wt["moe_w_down"][l, e].rearrange("(c p) n -> p c n", p=128), w=[t_w[k]])

        load_w(0)
        gi = 0
        for e in range(16):
            k = e % 2
            for sh in range(4):
                seq, half = sh // 2, sh % 2
                g2 = gi % 2
                gi += 1
                col = seq * 32 + e
                P.op("gpsimd", lambda eng, g2=g2, half=half, col=col: eng.indirect_dma_start(
                    out=xg[g2][:], out_offset=None, in_=X32,
                    in_offset=bass.IndirectOffsetOnAxis(ap=IDXT[:, half, col:col + 1], axis=0),
                    bounds_check=bcr(eng), oob_is_err=False), r=[t_IDXT], w=[t_xg[g2]], dma=True)
                emit_xT(xg[g2], t_xg[g2], xeT, t_xe, sh, 5 + 2 * 0, 6)
            if e + 1 < 16:
                load_w(e + 1)
            for fc in range(4):
                s2 = fc % 2
                for d in range(8):
                    mm(bank(1), wg[k][:, d, fc * 128:(fc + 1) * 128], xeT[:, d, :], d == 0, d == 7, r=[t_w[k], t_xe], w=[PT[1]])
                for d in range(8):
                    mm(bank(2), wu[k][:, d, fc * 128:(fc + 1) * 128], xeT[:, d, :], d == 0, d == 7, r=[t_w[k], t_xe], w=[PT[2]])
                act(sg[s2][:], bank(1), AF.Silu, r=[PT[1]], w=[t_sg[s2]])
                tt(hidT[:, fc, :], sg[s2][:], bank(2), ALU.mult, r=[t_sg[s2], PT[2]], w=[t_hid])
            for sh in range(4):
                seq, half = sh // 2, sh % 2
                y2 = sh % 2
                col = seq * 32 + e
                for h2 in range(2):
                    for fc in range(4):
                        mm(bank(3 + h2), hidT[:, fc, sh * 128:(sh + 1) * 128], wd[k][:, fc, h2 * 512:(h2 + 1) * 512],
                           fc == 0, fc == 3, r=[t_hid, t_w[k]], w=[PT[3 + h2]])
                ts(yb[y2][:], ps[:, 3 * 512:5 * 512], GT[:, half, col:col + 1], None, ALU.mult, r=[PT[3], PT[4], t_GT], w=[t_yb[y2]])
                P.op("gpsimd", lambda eng, y2=y2, half=half, col=col: eng.indirect_dma_start(
                    out=Z32, out_offset=bass.IndirectOffsetOnAxis(ap=IDXT[:, half, col:col + 1], axis=0),
                    in_=yb[y2][:], in_offset=None, compute_op=ALU.add,
                    bounds_check=bcr(eng), oob_is_err=False), r=[t_IDXT, t_yb[y2]], w=[t_Zs[seq]], dma=True)

    def stage_LN3(l, dst):
        A.reset()
        G, Bt, t_gb = load_gb("ln3_g", "ln3_b", l)
        NR = 12
        sms = [small_sm() for _ in range(NR)]
        junks = [(A.tile([128, D], BF16, "junk"), Tok()) for _ in range(2)]
        xbs = [A.tile([128, D], F32, "xbr") for _ in range(NR)]
        t_xbs = [Tok() for _ in range(NR)]
        xst = [A.tile([128, 8, 512], BF16, "xst") for _ in range(2)]
        t_xst = [Tok(), Tok()]

        def s0(tb):
            kb = tb % NR
            dma("sync", xbs[kb][:], Z32[tb * 128:(tb + 1) * 128, :], w=[t_xbs[kb]])

        def s8(tb):
            ck, tbl = tb // 4, tb % 4
            k2 = ck % 2
            kb = tb % NR
            dma("sync", dst[tb * 128:(tb + 1) * 128, :], xbs[kb][:], r=[t_xbs[kb]], w=[t_X32[tb]])
            bb = 2 * (tb % 4)
            emit_xT(xbs[kb], t_xbs[kb], xst[k2], t_xst[k2], tbl, bb, bb + 1)

        def s9(tb):
            ck, tbl = tb // 4, tb % 4
            k2 = ck % 2
            if tbl == 3:
                dma("sync", XT16v[:, :, ck * 512:(ck + 1) * 512], xst[k2][:], r=[t_xst[k2]], w=[t_XT[ck]])
        run_pipe(NB, [s0] + ln_hops(xbs, t_xbs, sms, junks, G, Bt, t_gb, NR) + [s8, s9])

    def stage_HY(l):
        A.reset()
        HS = A.tile([128, 16, 256], BF16, "HS"); t_HS = Tok()
        HD = A.tile([128, 16, 256], BF16, "HD"); t_HD = Tok()
        x0T = A.tile([128, 2, T], BF16, "x0T"); t_x0 = Tok()
        zT = A.tile([128, 2, T], BF16, "zT"); t_zT = Tok()
        z_tm = A.tile([128, 16, 512], BF16, "z_tm"); t_ztm = Tok()
        hbias = A.tile([128, 2], F32, "hbias"); t_hb = Tok()
        for j in range(2):
            dma("sync", hbias[:, j:j + 1], Wt["hy_bias"][l, j * 128:(j + 1) * 128].rearrange("(p o) -> p o", o=1), w=[t_hb])
        sub_mark = A.off
        featsT = A.tile([33, L], F32, "featsT"); t_f = Tok()
        dma("sync", featsT[:], Cn["featsT"], w=[t_f])
        negt = A.tile([128, 16], F32, "negt"); m0 = A.tile([128, 16], F32, "m0")
        dma("sync", negt[:], Cn["negt"], w=[t_f])
        dma("sync", m0[:], Cn["m0"], w=[t_f])
        w1 = A.tile([33, 64], F32, "w1"); w2 = A.tile([64, 64], F32, "w2"); w3 = A.tile([64, 64], F32, "w3")
        wout = A.tile([64, 512], F32, "wout")
        dma("sync", w1[:], Wt["hy_w1"][l], w=[t_f])
        dma("sync", w2[:], Wt["hy_w2"][l], w=[t_f])
        dma("sync", w3[:], Wt["hy_w3"][l], w=[t_f])
        dma("sync", wout[:], Wt["hy_wout"][l], w=[t_f])
        prm = A.tile([64, 8], F32, "prm"); t_prm = Tok()
        for i, nm in enumerate(("hy_b1", "hy_b2", "hy_b3", "hy_freq")):
            dma("sync", prm[:, i:i + 1], Wt[nm][l].rearrange("(p o) -> p o", o=1), w=[t_prm])
        ts(prm[:, 4:7], prm[:, 0:3], prm[:, 3:4], None, ALU.mult, r=[t_prm], w=[t_prm])
        AD = A.tile([128, 512], F32, "AD"); t_AD = Tok()
        dma("sync", AD[:], Wt["hy_decay"][l].rearrange("a c -> (a c)").partition_broadcast(128), w=[t_AD])
        act(AD[:], AD[:], AF.Abs, r=[t_AD], w=[t_AD])
        hA = A.tile([64, L], F32, "hA"); hB = A.tile([64, L], F32, "hB")
        t_hA, t_hB = Tok(), Tok()
        a1 = [A.tile([64, 512], F32, "a1") for _ in range(2)]
        a2 = [A.tile([64, 512], F32, "a2") for _ in range(2)]
        t_a = [Tok(), Tok()]
        chain = [(featsT, t_f, 33, w1, hA, t_hA), (hA, t_hA, 64, w2, hB, t_hB), (hB, t_hB, 64, w3, hA, t_hA)]
        for i, (src, t_src, kk, wi, dst, t_dst) in enumerate(chain):
            for nq in range(4):
                b = nq % 2
                mm(bank(b, 0, 512, 0, 64), wi[0:kk, :], src[0:kk, nq * 512:(nq + 1) * 512], True, True, r=[t_f, t_src], w=[PT[b]])
                ts(a1[b][:], bank(b, 0, 512, 0, 64), prm[:, 3:4], prm[:, 4 + i:5 + i], ALU.mult, ALU.add, r=[PT[b], t_prm], w=[t_a[b]])
                ts(a2[b][:], a1[b][:], 1.0 / TWO_PI, MAGIC, ALU.mult, ALU.add, r=[t_a[b]], w=[t_a[b]])
                ts(a2[b][:], a2[b][:], MAGIC, -TWO_PI, ALU.subtract, ALU.mult, r=[t_a[b]], w=[t_a[b]])
                tt(a1[b][:], a1[b][:], a2[b][:], ALU.add, r=[t_a[b]], w=[t_a[b]])
                ts(a1[b][:], a1[b][:], 3.1415925, -3.1415925, ALU.min, ALU.max, r=[t_a[b]], w=[t_a[b]])
                act(dst[:, nq * 512:(nq + 1) * 512], a1[b][:], AF.Sin, r=[t_a[b]], w=[t_dst])
        h3, t_h3 = hA, t_hA
        E = [A.tile([128, 512], F32, "E") for _ in range(2)]
        fl = [A.tile([128, 512], F32, "fl") for _ in range(2)]
        t_E = [Tok(), Tok()]
        t_fl = [Tok(), Tok()]
        for lb in range(16):
            b = 2 + lb % 2
            k = lb % 2
            mm(bank(b), h3[:, lb * 128:(lb + 1) * 128], wout[:], True, True, r=[t_h3, t_f], w=[PT[b]])
            act(E[k][:], AD[:], AF.Exp, r=[t_AD, t_f], w=[t_E[k]], scale=negt[:, lb:lb + 1])
            tt(fl[k][:], bank(b), E[k][:], ALU.mult, r=[PT[b], t_E[k]], w=[t_fl[k]])
            ts(fl[k][:, 256:512], fl[k][:, 256:512], m0[:, lb:lb + 1], None, ALU.mult, r=[t_fl[k], t_f], w=[t_fl[k]], eng="vector")
            tt(HS[:, lb, :], fl[k][:, 0:256], fl[k][:, 256:512], ALU.add, r=[t_fl[k]], w=[t_HS], eng="gpsimd")
            tt(HD[:, lb, :], fl[k][:, 256:512], fl[k][:, 0:256], ALU.subtract, r=[t_fl[k]], w=[t_HD], eng="vector")
        P.barrier()
        A.off = sub_mark
        cw = A.tile([128, 6, 4], F32, "cw"); t_cw = Tok()
        for cc in range(6):
            for k in range(3):
                dma("sync", cw[:, cc, k:k + 1], Wt["hy_conv_w"][l, k, cc * 128:(cc + 1) * 128].rearrange("(p o) -> p o", o=1), w=[t_cw])
            dma("sync", cw[:, cc, 3:4], Wt["hy_conv_b"][l, cc * 128:(cc + 1) * 128].rearrange("(p o) -> p o", o=1), w=[t_cw])
        ut = [A.tile([128, L], F32, "ut") for _ in range(2)]
        t_ut = [Tok(), Tok()]
        uc = [A.tile([128, L], F32, "uc") for _ in range(2)]
        t_uc = [Tok(), Tok()]
        zf = A.tile([128, L], F32, "zf"); t_zf = Tok()
        ui = [0]

        def conv(seq, cc, dstk):
            u = ui[0] % 2
            ui[0] += 1
            dma("sync", ut[u][:], UT[cc * 128:(cc + 1) * 128, seq * L:(seq + 1) * L], w=[t_ut[u]])
            o, t_o = uc[dstk], t_uc[dstk]
            ts(o[:], ut[u][:], cw[:, cc, 1:2], cw[:, cc, 3:4], ALU.mult, ALU.add, r=[t_ut[u], t_cw], w=[t_o])
            stt(o[:, 1:L], ut[u][:, 0:L - 1], cw[:, cc, 0:1], o[:, 1:L], ALU.mult, ALU.add, r=[t_ut[u], t_cw, t_o], w=[t_o])
            stt(o[:, 0:L - 1], ut[u][:, 1:L], cw[:, cc, 2:3], o[:, 0:L - 1], ALU.mult, ALU.add, r=[t_ut[u], t_cw, t_o], w=[t_o])

        for seq in range(2):
            for j in range(2):
                conv(seq, 2 + j, 0)
                conv(seq, 4 + j, 1)
                tt(zf[:], uc[0][:], uc[1][:], ALU.mult, r=[t_uc[0], t_uc[1]], w=[t_zf])
                cp(zT[:, j, seq * L:(seq + 1) * L], zf[:], r=[t_zf], w=[t_zT], eng="scalar")
                for g in range(4):
                    b = 4 + g % 2
                    for i in range(4):
                        tb = g * 4 + i
                        tr(bank(b, i * 128, (i + 1) * 128), zf[:, tb * 128:(tb + 1) * 128], r=[t_zf], w=[PT[b]])
                    c0 = seq * 256 + j * 128
                    cp(z_tm[:, g * 4:(g + 1) * 4, c0:c0 + 128], bank(b).rearrange("p (a t) -> p a t", a=4), r=[PT[b]], w=[t_ztm],
                       eng="vector" if g % 2 else "scalar")
                conv(seq, j, 0)
                cp(x0T[:, j, seq * L:(seq + 1) * L], uc[0][:], r=[t_uc[0]], w=[t_x0], eng="scalar")
        P.barrier()
        A.off = sub_mark
        Pre = A.tile([128, 16, 512], BF16, "Pre"); t_Pre = Tok()
        Pim = A.tile([128, 16, 512], BF16, "Pim"); t_Pim = Tok()
        cfb = [A.tile([128, 16, 128], BF16, "cfb") for _ in range(2)]
        sfb = [A.tile([128, 16, 128], BF16, "sfb") for _ in range(2)]
        t_cs = [Tok(), Tok()]
        hh = [A.tile([128, 512], F32, "hh") for _ in range(2)]
        t_hh = [Tok(), Tok()]
        _tq1 = A.tile([128, 4, 512], F32, "tq")
        tq_ = [_tq1, _tq1]
        _ttq = Tok()
        t_tq = [_ttq, _ttq]
        for fb in range(16):
            k = fb % 2
            b0 = 4 * k
            dma("sync", cfb[k][:], Cn["cf"][fb], w=[t_cs[k]])
            dma("sync", sfb[k][:], Cn["sf"][fb], w=[t_cs[k]])
            for lb in range(16):
                mm(bank(b0, 0, 256), cfb[k][:, lb, :], HS[:, lb, :], lb == 0, lb == 15, r=[t_cs[k]], w=[PT[b0]])
            for lb in range(16):
                mm(bank(b0 + 1, 0, 256), sfb[k][:, lb, :], HD[:, lb, :], lb == 0, lb == 15, r=[t_cs[k]], w=[PT[b0 + 1]])
            for tb in range(16):
                mm(bank(b0 + 2), cfb[k][:, tb, :], z_tm[:, tb, :], tb == 0, tb == 15, r=[t_cs[k]], w=[PT[b0 + 2]])
            for tb in range(16):
                mm(bank(b0 + 3), sfb[k][:, tb, :], z_tm[:, tb, :], tb == 0, tb == 15, r=[t_cs[k]], w=[PT[b0 + 3]])
            cp(hh[k][:, 0:256], bank(b0, 0, 256), r=[PT[b0]], w=[t_hh[k]], eng="vector")
            cp(hh[k][:, 256:512], bank(b0 + 1, 0, 256), r=[PT[b0 + 1]], w=[t_hh[k]], eng="vector")
            hre = hh[k][:, 0:256].unsqueeze(1).to_broadcast([128, 2, 256])
            him = hh[k][:, 256:512].unsqueeze(1).to_broadcast([128, 2, 256])
            Av = bank(b0 + 2).rearrange("p (s c) -> p s c", s=2)
            Bv = bank(b0 + 3).rearrange("p (s c) -> p s c", s=2)
            tv = [tq_[k][:, i, :].rearrange("p (s c) -> p s c", s=2) for i in range(4)]
            tt(tv[0], Av, hre, ALU.mult, r=[PT[b0 + 2], t_hh[k]], w=[t_tq[k]])
            tt(tv[1], Bv, him, ALU.mult, r=[PT[b0 + 3], t_hh[k]], w=[t_tq[k]])
            tt(tv[2], Bv, hre, ALU.mult, r=[PT[b0 + 3], t_hh[k]], w=[t_tq[k]])
            tt(tv[3], Av, him, ALU.mult, r=[PT[b0 + 2], t_hh[k]], w=[t_tq[k]])
            tt(Pre[:, fb, :], tq_[k][:, 0, :], tq_[k][:, 1, :], ALU.add, r=[t_tq[k]], w=[t_Pre], eng="gpsimd")
            tt(Pim[:, fb, :], tq_[k][:, 2, :], tq_[k][:, 3, :], ALU.subtract, r=[t_tq[k]], w=[t_Pim], eng="gpsimd")
        cit = A.tile([128, 16, 512], BF16, "cit")
        sit = A.tile([128, 16, 512], BF16, "sit")
        t_ci = Tok()
        tmp = [A.tile([128, 512], F32, "tmp") for _ in range(2)]
        t_tmp = [Tok(), Tok()]
        obt = [A.tile([128, 512], BF16, "obt") for _ in range(2)]
        t_obt = [Tok(), Tok()]
        n = 0
        for tq in range(4):
            dma("sync", cit[:], Cn["ci"][:, :, tq * 512:(tq + 1) * 512], w=[t_ci])
            dma("sync", sit[:], Cn["si"][:, :, tq * 512:(tq + 1) * 512], w=[t_ci])
            for sq in range(2):
                for j in range(2):
                    b = n % 2
                    n += 1
                    c0 = sq * 256 + j * 128
                    for kb in range(16):
                        mm(bank(b), Pre[:, kb, c0:c0 + 128], cit[:, kb, :], kb == 0, False, r=[t_Pre, t_ci], w=[PT[b]])
                        mm(bank(b), Pim[:, kb, c0:c0 + 128], sit[:, kb, :], False, kb == 15, r=[t_Pim, t_ci], w=[PT[b]])
                    t0 = sq * L + tq * 512
                    stt(tmp[b][:], zT[:, j, t0:t0 + 512], hbias[:, j:j + 1], bank(b), ALU.mult, ALU.add, r=[t_zT, t_hb, PT[b]], w=[t_tmp[b]])
                    tt(obt[b][:], tmp[b][:], x0T[:, j, t0:t0 + 512], ALU.mult, r=[t_tmp[b], t_x0], w=[t_obt[b]], eng="gpsimd")
                    dma("sync", AT[512 + j * 128:512 + (j + 1) * 128, t0:t0 + 512], obt[b][:], r=[t_obt[b]], w=[tAT((4 + j, t0 // 512))])

    stage_init()
    P.barrier()
    for l in range(NL if stop_after != "init" else 0):
        stage_A(l)
        P.barrier()
        if stop_after == "A":
            break
        if HY:
            stage_HY(l)
            P.barrier()
            if stop_after == "HY":
                break
        stage_ATT()
        P.barrier()
        if stop_after == "ATT":
            break
        stage_WO(l, x_in if l == 0 else X32, HY)
        P.barrier()
        if stop_after == "WO":
            break
        stage_XA(l)
        P.barrier()
        if stop_after == "XA":
            break
        stage_MOE(l)
        P.barrier()
        last = (l == NL - 1)
        stage_LN3(l, out if last else X32)
        P.barrier()
    if stop_after in ("WO", "XA"):
        stage_final()
    P.emit(st)
    st.close()
    return nc


_CONSTS = None


def _run(inputs, NL=DEPTH, dbg=(), stop_after=None, cores=NCORES):
    global _CONSTS
    if _CONSTS is None:
        _CONSTS = _host_consts()
    nc = build(NL, dbg, stop_after)
    x = np.ascontiguousarray(np.asarray(inputs["x"], dtype=np.float32)).reshape(16, L, D)
    mem = np.ascontiguousarray(np.asarray(inputs["mem"], dtype=np.float32)).reshape(16, 256, D)
    in_maps = []
    for c in range(cores):
        m = {"x": x[2 * c:2 * c + 2].reshape(T, D), "mem": mem[2 * c:2 * c + 2].reshape(512, D)}
        for k in _W_SHAPES:
            m[k] = np.ascontiguousarray(np.asarray(inputs[k], dtype=np.float32))
        for k in _CONST_SHAPES:
            m[k] = _CONSTS[k]
        in_maps.append(m)
    res = run_bass_kernel_spmd(nc, in_maps, core_ids=list(range(cores)))
    return res.results


def kernel(**inputs):
    res = _run(inputs)
    out = np.stack([r["out"].reshape(2, L, D) for r in res], axis=0).reshape(16, L, D)
    return out.astype(np.float32)
```
